# Optimizing a Trainium2 kernel written in Bass

```python
import math
import jax, jax.numpy as jnp
from jax import lax
import numpy as np


D_MODEL = 1024
BATCH = 32
SEQ = 2048
DEPTH = 4

CHUNK = 64
MIX_WIDTH = D_MODEL
POOL_WIDTH = MIX_WIDTH // 2
POOL_WINDOWS = (2, 4, 8, 16)
POOL_GROUPS = len(POOL_WINDOWS)
POOL_GROUP = POOL_WIDTH // POOL_GROUPS
DN_HEAD_DIM = 128
DN_HEADS = (MIX_WIDTH - POOL_WIDTH) // DN_HEAD_DIM
DN_WIDTH = DN_HEADS * DN_HEAD_DIM
DN_CONV = 4
SGU_WIDTH = MIX_WIDTH // 2
SGU_BLOCK = 128
SGU_HEADS = 4
SGU_HEAD_CH = SGU_WIDTH // SGU_HEADS
SC_WIDTH = MIX_WIDTH - SGU_WIDTH
SC_CONV = 3
FFN_DIM = ((8 * D_MODEL // 3 + 127) // 128) * 128
AB_IN = POOL_WIDTH + 4 * DN_WIDTH + 2 * DN_HEADS
CD_IN = 2 * SGU_WIDTH + 3 * SC_WIDTH
N_EVEN = (DEPTH + 1) // 2
N_ODD = DEPTH // 2
EPS = 1e-6

kernel_name = 'hybrid_chunk_causal_encoder'


def rmsnorm(x, g):
    xf = x.astype(jnp.float32)
    y = xf * lax.rsqrt(jnp.mean(xf * xf, axis=-1, keepdims=True) + EPS)
    return (y * g.astype(jnp.float32)).astype(x.dtype)


def layernorm(x, g, b):
    xf = x.astype(jnp.float32)
    mu = jnp.mean(xf, axis=-1, keepdims=True)
    xc = xf - mu
    y = xc * lax.rsqrt(jnp.mean(xc * xc, axis=-1, keepdims=True) + EPS)
    return (y * g.astype(jnp.float32) + b.astype(jnp.float32)).astype(x.dtype)


def l2norm(x):
    return x * lax.rsqrt(jnp.sum(x * x, axis=-1, keepdims=True) + EPS)


def causal_depthwise_conv(x, w):
    K, C = w.shape
    return lax.conv_general_dilated(
        x, w[:, None, :].astype(x.dtype), window_strides=(1,), padding=[(K - 1, 0)],
        dimension_numbers=('NWC', 'WIO', 'NWC'), feature_group_count=C)


def swiglu(x, w_gate, w_up, w_down):
    return (jax.nn.silu(x @ w_gate) * (x @ w_up)) @ w_down


def pool_mixer(a, w_pool, scale):
    B, S, _ = a.shape
    grp = a.astype(jnp.float32).reshape(B, S, POOL_GROUPS, POOL_GROUP)
    cs = jnp.cumsum(grp, axis=1)
    pos = jnp.arange(S)
    outs = []
    for gi, w in enumerate(POOL_WINDOWS):
        c = cs[:, :, gi]
        lagged = jnp.pad(c, ((0, 0), (w, 0), (0, 0)))[:, :S]
        cnt = jnp.minimum(pos + 1, w).astype(jnp.float32)[None, :, None]
        outs.append((c - lagged) / cnt - grp[:, :, gi])
    pooled = jnp.stack(outs, axis=2).astype(a.dtype)
    mixed = jnp.einsum('bsgc,gcd->bsgd', pooled, w_pool)
    return mixed.reshape(B, S, POOL_WIDTH) * scale


def gated_delta_rule(q, k, v, beta, g):
    B, S, H, dk = q.shape
    dv = v.shape[-1]
    N = S // CHUNK
    def chunks(t):
        t = jnp.swapaxes(t, 1, 2)
        return t.reshape((B, H, N, CHUNK) + t.shape[3:])
    q, k, v, beta, g = chunks(q), chunks(k), chunks(v), chunks(beta), chunks(g)
    gc = jnp.cumsum(g, axis=-1)
    tril = jnp.tril(jnp.ones((CHUNK, CHUNK), dtype=bool))
    strict = jnp.tril(jnp.ones((CHUNK, CHUNK), dtype=bool), k=-1)
    gamma = jnp.exp(jnp.where(tril, gc[..., :, None] - gc[..., None, :], -jnp.inf))
    kb = k * beta[..., None]
    lmat = jnp.where(strict, jnp.einsum('bhnid,bhnjd->bhnij', kb, k) * gamma, 0.0)
    eye = jnp.eye(CHUNK, dtype=q.dtype)
    rhs = jnp.concatenate([v * beta[..., None], kb * jnp.exp(gc)[..., None]], axis=-1)
    sol = lax.linalg.triangular_solve(eye + lmat, rhs, left_side=True, lower=True,
                                      unit_diagonal=True)
    u, w = sol[..., :dv], sol[..., dv:]
    aqk = jnp.einsum('bhnid,bhnjd->bhnij', q, k) * gamma
    q_dec = q * jnp.exp(gc)[..., None]
    k_dec = k * jnp.exp(gc[..., -1:] - gc)[..., None]
    last = jnp.exp(gc[..., -1])

    def step(state, xs):
        u_i, w_i, qd_i, a_i, kd_i, l_i = xs
        v_new = u_i - jnp.einsum('bhcd,bhde->bhce', w_i, state)
        o_i = (jnp.einsum('bhcd,bhde->bhce', qd_i, state)
               + jnp.einsum('bhij,bhje->bhie', a_i, v_new))
        state = state * l_i[..., None, None] + jnp.einsum('bhcd,bhce->bhde', kd_i, v_new)
        return state, o_i

    xs = tuple(jnp.moveaxis(t, 2, 0) for t in (u, w, q_dec, aqk, k_dec, last))
    s0 = jnp.zeros((B, H, dk, dv), q.dtype)
    _, o = lax.scan(step, s0, xs)
    return o.transpose(1, 0, 3, 2, 4).reshape(B, S, H, dv)


def mixer_ab(h, w_in, pool_w, pool_scale, conv_w, a_log, dt_bias, out_norm, w_out):
    B, S, _ = h.shape
    proj = h @ w_in
    o0 = POOL_WIDTH
    o1 = o0 + 3 * DN_WIDTH
    o2 = o1 + DN_WIDTH
    o3 = o2 + DN_HEADS
    a_in, qkv, z = proj[..., :o0], proj[..., o0:o1], proj[..., o1:o2]
    b_raw, g_raw = proj[..., o2:o3], proj[..., o3:]
    y_a = pool_mixer(a_in, pool_w, pool_scale)
    qkv = jax.nn.silu(causal_depthwise_conv(qkv, conv_w)).astype(jnp.float32)
    q, k, v = (t.reshape(B, S, DN_HEADS, DN_HEAD_DIM) for t in jnp.split(qkv, 3, axis=-1))
    q = l2norm(q) * (DN_HEAD_DIM ** -0.5)
    k = l2norm(k)
    beta = jax.nn.sigmoid(b_raw.astype(jnp.float32))
    g = -jnp.exp(a_log.astype(jnp.float32)) * jax.nn.softplus(
        g_raw.astype(jnp.float32) + dt_bias.astype(jnp.float32))
    o = gated_delta_rule(q, k, v, beta, g)
    o = rmsnorm(o, out_norm) * jax.nn.silu(z.reshape(B, S, DN_HEADS, DN_HEAD_DIM).astype(jnp.float32))
    y_b = o.reshape(B, S, DN_WIDTH).astype(h.dtype)
    return jnp.concatenate([y_a, y_b], axis=-1) @ w_out


def mixer_cd(h, w_in, sgu_norm_g, sgu_norm_b, sgu_w, sgu_bias, conv_w, w_out):
    B, S, _ = h.shape
    proj = h @ w_in
    uv = jax.nn.gelu(proj[..., :2 * SGU_WIDTH])
    u, v = uv[..., :SGU_WIDTH], uv[..., SGU_WIDTH:]
    v = layernorm(v, sgu_norm_g, sgu_norm_b)
    nb = S // SGU_BLOCK
    vb = v.reshape(B, nb, SGU_BLOCK, SGU_HEADS, SGU_HEAD_CH)
    mask = jnp.tril(jnp.ones((SGU_BLOCK, SGU_BLOCK), dtype=bool))
    ws = jnp.where(mask, sgu_w, 0.0).astype(v.dtype)
    mixed = jnp.einsum('hij,bnjhc->bnihc', ws, vb) + sgu_bias.T[None, None, :, :, None]
    y_c = u * mixed.reshape(B, S, SGU_WIDTH)
    sc = proj[..., 2 * SGU_WIDTH:]
    xd, bg, cg = sc[..., :SC_WIDTH], sc[..., SC_WIDTH:2 * SC_WIDTH], sc[..., 2 * SC_WIDTH:]
    y_d = bg * causal_depthwise_conv(cg * xd, conv_w)
    return jnp.concatenate([y_c, y_d], axis=-1) @ w_out


def setup_inputs(seed: int = 0) -> dict:
    key = jax.random.key(seed)
    ks = jax.random.split(key, 32)
    f32 = jnp.float32
    def nrm(k, shape, scale):
        return jax.random.normal(k, shape, f32) * scale
    def gain(k, shape):
        return 1.0 + 0.02 * jax.random.normal(k, shape, f32)
    D, F = D_MODEL, FFN_DIM
    dt = jnp.exp(jax.random.uniform(ks[14], (N_EVEN, DN_HEADS), f32,
                                    minval=math.log(1e-3), maxval=math.log(1e-1)))
    return {
        'x': jax.random.normal(ks[0], (BATCH, SEQ, D), f32),
        'ffn1_norm': gain(ks[1], (DEPTH, D)),
        'ffn1_w_gate': nrm(ks[2], (DEPTH, D, F), D ** -0.5),
        'ffn1_w_up': nrm(ks[3], (DEPTH, D, F), D ** -0.5),
        'ffn1_w_down': nrm(ks[4], (DEPTH, F, D), F ** -0.5),
        'mix_norm': gain(ks[5], (DEPTH, D)),
        'ffn2_norm': gain(ks[6], (DEPTH, D)),
        'ffn2_w_gate': nrm(ks[7], (DEPTH, D, F), D ** -0.5),
        'ffn2_w_up': nrm(ks[8], (DEPTH, D, F), D ** -0.5),
        'ffn2_w_down': nrm(ks[9], (DEPTH, F, D), F ** -0.5),
        'ab_w_in': nrm(ks[10], (N_EVEN, D, AB_IN), D ** -0.5),
        'pool_w': nrm(ks[11], (N_EVEN, POOL_GROUPS, POOL_GROUP, POOL_GROUP), POOL_GROUP ** -0.5),
        'pool_scale': gain(ks[12], (N_EVEN, POOL_WIDTH)),
        'dn_conv_w': nrm(ks[13], (N_EVEN, DN_CONV, 3 * DN_WIDTH), DN_CONV ** -0.5),
        'dn_a_log': jnp.log(jax.random.uniform(ks[15], (N_EVEN, DN_HEADS), f32, minval=1.0, maxval=16.0)),
        'dn_dt_bias': dt + jnp.log(-jnp.expm1(-dt)),
        'dn_out_norm': gain(ks[16], (N_EVEN, DN_HEAD_DIM)),
        'ab_w_out': nrm(ks[17], (N_EVEN, MIX_WIDTH, D), MIX_WIDTH ** -0.5),
        'cd_w_in': nrm(ks[18], (N_ODD, D, CD_IN), D ** -0.5),
        'sgu_norm_g': gain(ks[19], (N_ODD, SGU_WIDTH)),
        'sgu_norm_b': nrm(ks[20], (N_ODD, SGU_WIDTH), 0.02),
        'sgu_w': nrm(ks[21], (N_ODD, SGU_HEADS, SGU_BLOCK, SGU_BLOCK), SGU_BLOCK ** -0.5),
        'sgu_bias': gain(ks[22], (N_ODD, SGU_HEADS, SGU_BLOCK)),
        'sc_conv_w': nrm(ks[23], (N_ODD, SC_CONV, SC_WIDTH), SC_CONV ** -0.5),
        'cd_w_out': nrm(ks[24], (N_ODD, MIX_WIDTH, D), MIX_WIDTH ** -0.5),
        'final_norm': gain(ks[25], (D,)),
    }


def reference(x, ffn1_norm, ffn1_w_gate, ffn1_w_up, ffn1_w_down, mix_norm,
              ffn2_norm, ffn2_w_gate, ffn2_w_up, ffn2_w_down,
              ab_w_in, pool_w, pool_scale, dn_conv_w, dn_a_log, dn_dt_bias, dn_out_norm, ab_w_out,
              cd_w_in, sgu_norm_g, sgu_norm_b, sgu_w, sgu_bias, sc_conv_w, cd_w_out,
              final_norm):
    h = x
    for layer in range(DEPTH):
        h = h + 0.5 * swiglu(rmsnorm(h, ffn1_norm[layer]), ffn1_w_gate[layer],
                             ffn1_w_up[layer], ffn1_w_down[layer])
        hn = rmsnorm(h, mix_norm[layer])
        if layer % 2 == 0:
            e = layer // 2
            h = h + mixer_ab(hn, ab_w_in[e], pool_w[e], pool_scale[e], dn_conv_w[e],
                             dn_a_log[e], dn_dt_bias[e], dn_out_norm[e], ab_w_out[e])
        else:
            o = layer // 2
            h = h + mixer_cd(hn, cd_w_in[o], sgu_norm_g[o], sgu_norm_b[o], sgu_w[o],
                             sgu_bias[o], sc_conv_w[o], cd_w_out[o])
        h = h + 0.5 * swiglu(rmsnorm(h, ffn2_norm[layer]), ffn2_w_gate[layer],
                             ffn2_w_up[layer], ffn2_w_down[layer])
    return rmsnorm(h, final_norm)
```

```python
import numpy as np
from contextlib import ExitStack
import concourse.bass as bass
import concourse.mybir as mybir
from concourse.bass_utils import run_bass_kernel_spmd

F32 = mybir.dt.float32
BF16 = mybir.dt.bfloat16
AF = mybir.ActivationFunctionType
ALU = mybir.AluOpType

P = 128
D = 1024
KC = 8
FF = 2816
EPS = 1e-6
NSLOT = 5
AB_IN = 2568
CD_IN = 2560
POOL_WINDOWS = (2, 4, 8, 16)
GELU_C = 0.7978845608028654

O_NORM = 0
O_PSC = 104
O_DNW = 112
O_ONORM = 208
O_SGG = 210
O_SGB = 218
O_SCW = 226
O_ALOG = 250
O_DTB = 252
O_IDENT = 254
O_MASKA = 382
O_STRICT = 510
O_TRIL = 638
O_RESET = 766
O_INVC = 1022
NSM = 1086

ENGS = ('pe', 'act', 'dve', 'pool', 'sp')
BLK = {'pe': 'tensor', 'act': 'scalar', 'dve': 'vector', 'pool': 'gpsimd', 'sp': 'sync'}


class Tok:
    __slots__ = ('w', 'r')

    def __init__(self):
        self.w = None
        self.r = {}


class Sched:
    def __init__(self, nc, es, same=True):
        self.nc = nc
        self.es = es
        self.same = same
        self.prog = {e: [] for e in ENGS}
        self.seen = {e: {} for e in ENGS}
        self.sems = []
        self.toks = []
        self.esem = {}
        self.cnt = {}
        self.epoch = -1
        self.new_epoch()

    def new_sem(self, name):
        h = self.es.enter_context(self.nc.semaphore(name))
        self.sems.append(h)
        return len(self.sems) - 1

    def tok(self):
        t = Tok()
        self.toks.append(t)
        return t

    def toks2(self, *dims):
        if len(dims) == 1:
            return [self.tok() for _ in range(dims[0])]
        return [self.toks2(*dims[1:]) for _ in range(dims[0])]

    def barrier(self, engs):
        fin = {e: (self.esem[e], self.cnt[e]) for e in engs if self.cnt[e] > 0}
        for e in engs:
            for p, (s, v) in fin.items():
                if self.seen[e].get(s, 0) >= v:
                    continue
                self.seen[e][s] = v
                self.prog[e].append(('w', s, v))

    def new_epoch(self):
        if self.epoch >= 0:
            self.barrier(ENGS)
            old = set(self.esem.values())
            for t in self.toks:
                if t.w is not None and t.w[0] in old:
                    t.w = None
                t.r = {s: v for s, v in t.r.items() if s not in old}
        self.epoch += 1
        self.esem = {e: self.new_sem(f"e{self.epoch}_{e}") for e in ENGS}
        self.cnt = {e: 0 for e in ENGS}

    def op(self, eng, insts, reads=(), writes=(), dma=None):
        deps = {}

        def need(d):
            if d is None:
                return
            s, v = d
            if deps.get(s, 0) < v:
                deps[s] = v
        for t in reads:
            need(t.w)
        for t in writes:
            need(t.w)
            for s, v in t.r.items():
                need((s, v))
        my = self.esem[eng]
        for s, v in deps.items():
            if s == my and (eng == 'pe' or not self.same):
                continue
            if self.seen[eng].get(s, 0) >= v:
                continue
            self.seen[eng][s] = v
            self.prog[eng].append(('w', s, v))
        if dma is None:
            self.cnt[eng] += 1
            tk = (my, self.cnt[eng])
            inc = 1
        else:
            tk = dma
            inc = 16
        if isinstance(insts, tuple):
            insts = [insts]
        self.prog[eng].append(('o', insts, tk[0], inc))
        for t in writes:
            t.w = tk
            t.r = {}
        for t in reads:
            if t.r.get(tk[0], 0) < tk[1]:
                t.r[tk[0]] = tk[1]
        return tk

    def emit(self):
        nc = self.nc
        sems = self.sems
        with nc.Block() as block:
            for e in ENGS:
                def body(eng, prog=self.prog[e]):
                    for it in prog:
                        if it[0] == 'w':
                            eng.wait_ge(sems[it[1]], it[2])
                        else:
                            ins = None
                            for name, kw in it[1]:
                                ins = getattr(eng, name)(**kw)
                            ins.then_inc(sems[it[2]], it[3])
                getattr(block, BLK[e])(body)


class Arena:
    def __init__(self, ap, width):
        self.ap = ap
        self.w = width
        self.off = 0

    def reset(self):
        self.off = 0

    def alloc(self, shape, dt):
        n = int(np.prod(shape))
        nw = (n * (4 if dt == F32 else 2) + 3) // 4
        nw = (nw + 7) // 8 * 8
        a = self.ap[:, self.off:self.off + nw]
        self.off += nw
        assert self.off <= self.w, f"arena overflow {self.off} > {self.w}"
        if dt != F32:
            a = a.bitcast(dt)
        a = a[:, 0:n]
        if len(shape) == 2:
            a = a.rearrange("p (a b) -> p a b", a=shape[0])
        elif len(shape) == 3:
            a = a.rearrange("p (a b c) -> p a b c", a=shape[0], b=shape[1])
        return a


class Builder:
    def __init__(self, NB, S, layers, do_ffn=True, do_mix=True, same=True, IL=4):
        self.IL = IL
        self.NB, self.S, self.layers = NB, S, list(layers)
        self.do_ffn, self.do_mix = do_ffn, do_mix
        self.NT = S // 512
        self.es = ExitStack()
        nc = self.nc = bass.Bass("TRN2", target_bir_lowering=False)
        es = self.es
        dt = lambda name, shape, kind="ExternalInput": nc.dram_tensor(name, shape, F32, kind=kind).ap()
        self.xT = dt("xT", [NB, D, S])
        self.yT = dt("yT", [NB, D, S], "ExternalOutput")
        self.w = {}
        for pre in ("ffn1", "ffn2"):
            self.w[pre + "_w_gate"] = dt(pre + "_w_gate", [4, D, FF])
            self.w[pre + "_w_up"] = dt(pre + "_w_up", [4, D, FF])
            self.w[pre + "_w_down"] = dt(pre + "_w_down", [4, FF, D])
        self.w["ab_w_in"] = dt("ab_w_in", [2, D, AB_IN])
        self.w["ab_w_out"] = dt("ab_w_out", [2, D, D])
        self.w["cd_w_in"] = dt("cd_w_in", [2, D, CD_IN])
        self.w["cd_w_out"] = dt("cd_w_out", [2, D, D])
        self.d_smalls = dt("smalls", [P, NSM])
        self.d_poolw = dt("poolw", [P, 2 * 4 * 128])
        self.d_sguw = dt("sguw", [2, P, 4 * 128])
        self.d_sgub = dt("sgub", [2, P, 4 * 128])

        sb = lambda name, shape, d: es.enter_context(nc.sbuf_tensor(name, shape, d))
        self.h_t = sb("h", [P, KC, S], F32)
        self.h = self.h_t[:]
        self.slots = [sb(f"ws{i}", [P, 4096], BF16) for i in range(NSLOT)]
        self.sm = sb("smalls_sb", [P, NSM], F32)[:]
        self.ones_bf = sb("ones_bf", [P, P], BF16)[:]
        self.ones32 = sb("ones32", [P, P], F32)[:]
        self.poolw = sb("poolw_sb", [P, 2 * 4 * 128], BF16)[:]
        self.ident_bf = sb("ident_bf", [P, P], BF16)[:]
        rem = int(nc.sbuf_bytes_remaining)
        aw = (rem - 1024) // 4 // 8 * 8
        self.arena_w = aw
        self.ar = Arena(sb("arena", [P, aw], F32)[:], aw)
        self.PS = [es.enter_context(nc.psum_tensor(f"ps{i}", [P, 512], F32))[:] for i in range(8)]

        self.sc = Sched(nc, es, same=same)
        sc = self.sc
        self.PT = sc.toks2(8)
        self.HT = sc.toks2(KC, self.NT)
        self.slot_tok = sc.toks2(NSLOT)
        self.slot_sem = [sc.new_sem(f"wl{i}") for i in range(NSLOT)]
        self.slot_cnt = [0] * NSLOT
        self.slot_next = 0
        self.slot_epoch = [-1] * NSLOT
        self.h_sem = [sc.new_sem(f"hs{c}") for c in range(KC)]
        self.h_cnt = [0] * KC
        self.misc_sem = sc.new_sem("misc")
        self.misc_cnt = 0
        self.misc2_sem = sc.new_sem("misc2")
        self.misc2_cnt = 0
        self.const_tok = sc.tok()
        self.bank_rr = 0
        self.dbg_done = set()
        self.dbg_sems = []
        self.nb_range = (0, 7)

    def dbg(self, name, ap, toks):
        if not getattr(self, 'debug', False) or name in self.dbg_done:
            return
        self.dbg_done.add(name)
        shape = [int(v) for v in ap.shape]
        d = self.nc.dram_tensor("dbg_" + name, shape, ap.dtype, kind="ExternalOutput").ap()
        sem = self.sc.new_sem("dbg_" + name)
        self.sc.op('sp', ('dma_start', dict(out=d, in_=ap)), tuple(toks), (), dma=(sem, 16))
        self.dbg_sems.append(sem)

    def A(self, reads, writes, **kw):
        self.sc.op('act', ('activation', kw), reads, writes)

    def V(self, name, reads, writes, **kw):
        self.sc.op('dve', (name, kw), reads, writes)

    def nb(self, lo=None, hi=None):
        if lo is None:
            lo, hi = self.nb_range
        n = hi - lo
        b = lo + (self.bank_rr % n)
        self.bank_rr += 1
        return b

    def sm_col(self, off, n=1):
        return self.sm[:, off:off + n]

    def wload(self, src, view):
        i = self.slot_next % NSLOT
        self.slot_next += 1
        assert (self.slot_cnt[i] == 0 or self.slot_tok[i].r or self.slot_epoch[i] != self.sc.epoch), \
            "weight slot reloaded before any reader was emitted"
        self.slot_epoch[i] = self.sc.epoch
        self.slot_cnt[i] += 16
        dst = view(self.slots[i][:])
        self.sc.op('pool', ('dma_start', dict(out=dst, in_=src)), (), (self.slot_tok[i],),
                   dma=(self.slot_sem[i], self.slot_cnt[i]))
        return i

    def v_k512(self, ncol):
        return lambda a: a.rearrange("p (k f) -> p k f", k=8)[:, :, 0:ncol]

    def slot_k512(self, i):
        return self.slots[i][:].rearrange("p (k f) -> p k f", k=8)

    def phase(self):
        self.sc.barrier(('pe', 'act', 'dve', 'sp'))
        self.ar.reset()

    def emit_rstd(self, srcs, stoks, n, inv_count, eps, out_ap, out_tok, sq, sqt, L, Lt):
        b = self.nb()
        ns = len(srcs)
        for i in range(ns):
            q, qt = sq[i % len(sq)], sqt[i % len(sq)]
            self.A((stoks[i],), (qt,), out=q[:, 0:n], in_=srcs[i], func=AF.Square)
            self.sc.op('pe', ('matmul', dict(out=self.PS[b][:, 0:n], lhsT=self.ones_bf, rhs=q[:, 0:n],
                                             start=(i == 0), stop=(i == ns - 1))),
                       (qt, self.const_tok), (self.PT[b],))
        self.A((self.PT[b],), (Lt,), out=L[:, 0:n], in_=self.PS[b][:, 0:n], func=AF.Ln,
               scale=float(inv_count), bias=float(eps))
        self.A((Lt,), (out_tok,), out=out_ap, in_=L[:, 0:n], func=AF.Exp, scale=-0.5)

    def build(self):
        sc = self.sc
        self.misc_cnt += 16
        sc.op('sp', ('dma_start', dict(out=self.sm, in_=self.d_smalls)), (), (self.const_tok,),
              dma=(self.misc_sem, self.misc_cnt))
        self.misc2_cnt += 16
        pw_tok = sc.tok()
        self.pw_tok = pw_tok
        sc.op('pool', ('dma_start', dict(out=self.poolw, in_=self.d_poolw)), (), (pw_tok,),
              dma=(self.misc2_sem, self.misc2_cnt))
        self.V('memset', (), (self.const_tok,), ap=self.ones_bf, constant=1.0)
        self.V('memset', (), (self.const_tok,), ap=self.ones32, constant=1.0)
        self.V('tensor_copy', (self.const_tok,), (self.const_tok,), out=self.ident_bf,
               in_=self.sm[:, O_IDENT:O_IDENT + P])
        for b in range(self.NB):
            if b > 0:
                sc.new_epoch()
            self.one_batch(b)
        for c in range(KC):
            sc.prog['sp'].append(('w', self.h_sem[c], self.h_cnt[c]))
        for sm_ in self.dbg_sems:
            sc.prog['sp'].append(('w', sm_, 16))
        sc.emit()
        return self.nc

    def one_batch(self, b):
        sc = self.sc
        S = self.S
        for c in range(KC):
            self.h_cnt[c] += 16
            sc.op('sp', ('dma_start', dict(out=self.h[:, c, :], in_=self.xT[b, c * P:(c + 1) * P, :])),
                  (), tuple(self.HT[c]), dma=(self.h_sem[c], self.h_cnt[c]))
        for l in self.layers:
            if self.do_ffn:
                self.ffn(l, 0)
            if self.do_mix:
                if l % 2 == 0:
                    self.mixer_ab(l)
                else:
                    self.mixer_cd(l)
            if self.do_ffn:
                self.ffn(l, 1)
        self.final(b)

    def final(self, b):
        sc = self.sc
        self.phase()
        ar = self.ar
        sq = [ar.alloc([512], BF16) for _ in range(3)]
        sqt = sc.toks2(3)
        L = ar.alloc([512], F32)
        Lt = sc.tok()
        rs = [ar.alloc([512], F32) for _ in range(2)]
        rst = sc.toks2(2)
        for t in range(self.NT):
            tl = slice(t * 512, (t + 1) * 512)
            r, rt = rs[t % 2], rst[t % 2]
            self.emit_rstd([self.h[:, c, tl] for c in range(KC)], [self.HT[c][t] for c in range(KC)], 512,
                           1.0 / D, EPS, r, rt, sq, sqt, L, Lt)
            for c in range(KC):
                self.V('scalar_tensor_tensor', (self.HT[c][t], rt, self.const_tok), (self.HT[c][t],),
                       out=self.h[:, c, tl], in0=self.h[:, c, tl],
                       scalar=self.sm_col(O_NORM + 12 * 8 + c), in1=r, op0=ALU.mult, op1=ALU.mult)
        for c in range(KC):
            self.h_cnt[c] += 16
            sc.op('sp', ('dma_start', dict(out=self.yT[b, c * P:(c + 1) * P, :], in_=self.h[:, c, :])),
                  tuple(self.HT[c]), (), dma=(self.h_sem[c], self.h_cnt[c]))

    def ffn(self, l, which):
        sc, ar, S, NT = self.sc, self.ar, self.S, self.NT
        pre = "ffn1" if which == 0 else "ffn2"
        wg = self.w[pre + "_w_gate"][l].rearrange("(k p) f -> p k f", p=P)
        wu = self.w[pre + "_w_up"][l].rearrange("(k p) f -> p k f", p=P)
        wd = self.w[pre + "_w_down"][l].rearrange("(j p) d -> p j d", p=P)
        nidx = l * 3 + (0 if which == 0 else 2)
        self.phase()
        hn = ar.alloc([KC, S], BF16)
        act = ar.alloc([4, S], BF16)
        sil = [ar.alloc([512], F32) for _ in range(2)]
        silt = sc.toks2(2)
        sq = [ar.alloc([512], BF16) for _ in range(3)]
        sqt = sc.toks2(3)
        L = ar.alloc([512], F32)
        Lt = sc.tok()
        rs = [ar.alloc([512], F32) for _ in range(2)]
        rst = sc.toks2(2)
        HNT = sc.toks2(KC, NT)
        ACTT = sc.toks2(4, NT)
        for t in range(NT):
            tl = slice(t * 512, (t + 1) * 512)
            r, rt = rs[t % 2], rst[t % 2]
            self.emit_rstd([self.h[:, c, tl] for c in range(KC)], [self.HT[c][t] for c in range(KC)], 512,
                           1.0 / D, EPS, r, rt, sq, sqt, L, Lt)
            for c in range(KC):
                self.V('scalar_tensor_tensor', (self.HT[c][t], rt, self.const_tok), (HNT[c][t],),
                       out=hn[:, c, tl], in0=self.h[:, c, tl],
                       scalar=self.sm_col(O_NORM + nidx * 8 + c), in1=r, op0=ALU.mult, op1=ALU.mult)
        groups = [(0, 4), (4, 4), (8, 4), (12, 4), (16, 4), (20, 2)]
        ca = 0
        cc = 0
        for (j0, G) in groups:
            sa = self.wload(wg[:, :, j0 * P:(j0 + G) * P], self.v_k512(G * P))
            sb_ = self.wload(wu[:, :, j0 * P:(j0 + G) * P], self.v_k512(G * P))
            sd = self.wload(wd[:, j0:j0 + G, :],
                            lambda a, G=G: a.rearrange("p (j d) -> p j d", j=4)[:, 0:G, :])
            wa = self.slot_k512(sa)
            wb = self.slot_k512(sb_)
            wdn = self.slots[sd][:].rearrange("p (j d) -> p j d", j=4)
            for j in range(G):
                for t in range(NT):
                    tl = slice(t * 512, (t + 1) * 512)
                    x = ca % 2
                    ca += 1
                    ba, bb = x, 2 + x
                    ins = []
                    for k in range(KC):
                        ins.append(('matmul', dict(out=self.PS[ba], lhsT=wa[:, k, j * P:(j + 1) * P],
                                                   rhs=hn[:, k, tl], start=(k == 0), stop=(k == KC - 1))))
                    for k in range(KC):
                        ins.append(('matmul', dict(out=self.PS[bb], lhsT=wb[:, k, j * P:(j + 1) * P],
                                                   rhs=hn[:, k, tl], start=(k == 0), stop=(k == KC - 1))))
                    sc.op('pe', ins, [self.slot_tok[sa], self.slot_tok[sb_]] + [HNT[k][t] for k in range(KC)],
                          (self.PT[ba], self.PT[bb]))
                    self.A((self.PT[ba],), (silt[x],), out=sil[x], in_=self.PS[ba], func=AF.Silu)
                    self.V('tensor_tensor', (silt[x], self.PT[bb]), (ACTT[j][t],),
                           out=act[:, j, tl], in0=sil[x], in1=self.PS[bb], op=ALU.mult)
            for t in range(NT):
                tl = slice(t * 512, (t + 1) * 512)
                for c in range(KC):
                    y = 4 + (cc % 2)
                    cc += 1
                    ins = []
                    for j in range(G):
                        ins.append(('matmul', dict(out=self.PS[y], lhsT=wdn[:, j, c * P:(c + 1) * P],
                                                   rhs=act[:, j, tl], start=(j == 0), stop=(j == G - 1))))
                    sc.op('pe', ins, [self.slot_tok[sd]] + [ACTT[j][t] for j in range(G)], (self.PT[y],))
                    self.V('scalar_tensor_tensor', (self.PT[y], self.HT[c][t]), (self.HT[c][t],),
                           out=self.h[:, c, tl], in0=self.PS[y], scalar=0.5, in1=self.h[:, c, tl],
                           op0=ALU.mult, op1=ALU.add)

    def tile_norm(self, l, T0, TT, hn_t, hn_tok, sq, sqt, L, Lt, rs, rst):
        t5 = T0 // 512
        tl = slice(T0, T0 + TT)
        self.emit_rstd([self.h[:, c, tl] for c in range(KC)], [self.HT[c][t5] for c in range(KC)], TT,
                       1.0 / D, EPS, rs[:, 0:TT], rst, sq, sqt, L, Lt)
        for c in range(KC):
            self.V('scalar_tensor_tensor', (self.HT[c][t5], rst, self.const_tok), (hn_tok,),
                   out=hn_t[:, c, :], in0=self.h[:, c, tl],
                   scalar=self.sm_col(O_NORM + (l * 3 + 1) * 8 + c), in1=rs[:, 0:TT], op0=ALU.mult, op1=ALU.mult)

    def proj(self, slot, coff, ncol, hn_t, hn_tok, TT, bank=None):
        b = self.nb() if bank is None else bank
        w = self.slot_k512(slot)
        ins = []
        for k in range(KC):
            ins.append(('matmul', dict(out=self.PS[b][0:ncol, 0:TT], lhsT=w[:, k, coff:coff + ncol],
                                       rhs=hn_t[:, k, :], start=(k == 0), stop=(k == KC - 1))))
        self.sc.op('pe', ins, (self.slot_tok[slot], hn_tok), (self.PT[b],))
        return b

    def out_proj(self, wout, o_or_e, T0, TT, y, ytoks):
        t5 = T0 // 512
        tl = slice(T0, T0 + TT)
        for c in range(KC):
            b = self.nb()
            w = self.slot_k512(wout[c // 4])
            ins = []
            for k in range(KC):
                ins.append(('matmul', dict(out=self.PS[b][:, 0:TT], lhsT=w[:, k, (c % 4) * P:(c % 4 + 1) * P],
                                           rhs=y[:, k, :], start=(k == 0), stop=(k == KC - 1))))
            self.sc.op('pe', ins, [self.slot_tok[wout[c // 4]]] + list(ytoks), (self.PT[b],))
            self.V('tensor_tensor', (self.PT[b], self.HT[c][t5]), (self.HT[c][t5],),
                   out=self.h[:, c, tl], in0=self.PS[b][:, 0:TT], in1=self.h[:, c, tl], op=ALU.add)

    def gelu2(self, b, n, out, out_tok, tA, tAt, tB, tBt):
        ps = self.PS[b][:, 0:n]
        pt = self.PT[b]
        self.A((pt,), (tAt,), out=tA, in_=ps, func=AF.Square)
        self.V('tensor_scalar', (tAt,), (tAt,), out=tA, in0=tA, scalar1=0.044715, scalar2=1.0,
               op0=ALU.mult, op1=ALU.add)
        self.V('tensor_tensor', (tAt, pt), (tBt,), out=tB, in0=tA, in1=ps, op=ALU.mult)
        self.A((tBt,), (tBt,), out=tB, in_=tB, func=AF.Tanh, scale=GELU_C)
        self.V('scalar_tensor_tensor', (tBt, pt), (out_tok,), out=out, in0=tB, scalar=1.0, in1=ps,
               op0=ALU.add, op1=ALU.mult)

    def mixer_cd(self, l):
        sc, ar, S = self.sc, self.ar, self.S
        o = l // 2
        TT = 512
        win = self.w["cd_w_in"][o].rearrange("(k p) f -> p k f", p=P)
        wout = self.w["cd_w_out"][o].rearrange("(k p) f -> p k f", p=P)
        self.phase()
        self.nb_range = (0, 5)
        tk = sc.tok
        biasB = ar.alloc([4, P], F32)
        wnat = ar.alloc([4, P], F32)
        wsT = ar.alloc([4, P], BF16)
        prm_tok = tk()
        self.misc_cnt += 16
        sc.op('sp', ('dma_start', dict(out=biasB.rearrange("p a b -> p (a b)"), in_=self.d_sgub[o])),
              (), (prm_tok,), dma=(self.misc_sem, self.misc_cnt))
        wn_tok = tk()
        self.misc2_cnt += 16
        sc.op('sp', ('dma_start', dict(out=wnat.rearrange("p a b -> p (a b)"), in_=self.d_sguw[o])),
              (), (wn_tok,), dma=(self.misc2_sem, self.misc2_cnt))
        wsT_tok = tk()
        bt = self.nb()
        for hh in range(4):
            self.V('tensor_tensor', (wn_tok, self.const_tok), (wn_tok,), out=wnat[:, hh, :], in0=wnat[:, hh, :],
                   in1=self.sm[:, O_TRIL:O_TRIL + P], op=ALU.mult)
            sc.op('pe', ('transpose', dict(out=self.PS[bt][:, hh * P:(hh + 1) * P], in_=wnat[:, hh, :],
                                           identity=self.sm[:, O_IDENT:O_IDENT + P])),
                  (wn_tok, self.const_tok), (self.PT[bt],))
        self.A((self.PT[bt],), (wsT_tok,), out=wsT.rearrange("p a b -> p (a b)"), in_=self.PS[bt], func=AF.Copy)
        xc = ar.alloc([4, TT + 2], F32)
        xct = sc.toks2(4)
        for ch in range(4):
            self.V('memset', (), (xct[ch],), ap=xc[:, ch, 0:2], constant=0.0)
        hn_t = ar.alloc([KC, TT], BF16)
        hn_tok = tk()
        sq = [ar.alloc([TT], BF16) for _ in range(3)]
        sqt = sc.toks2(3)
        L = ar.alloc([TT], F32)
        Lt = tk()
        rs = ar.alloc([TT], F32)
        rst = tk()
        vg = [ar.alloc([TT], F32) for _ in range(4)]
        vgt = sc.toks2(4)
        tA = [ar.alloc([TT], F32) for _ in range(2)]
        tAt = sc.toks2(2)
        tB = [ar.alloc([TT], F32) for _ in range(2)]
        tBt = sc.toks2(2)
        mean = ar.alloc([TT], F32)
        mean_t = tk()
        m2 = ar.alloc([TT], F32)
        m2_t = tk()
        rstd = ar.alloc([TT], F32)
        rstd_t = tk()
        vtok = ar.alloc([4, 512], BF16)
        vtok_t = sc.toks2(4)
        ug = [ar.alloc([TT], F32) for _ in range(2)]
        ugt = sc.toks2(2)
        tmp = [ar.alloc([TT], F32) for _ in range(2)]
        tmpt = sc.toks2(2)
        acc = [ar.alloc([TT], F32) for _ in range(2)]
        acct = sc.toks2(2)
        y = ar.alloc([KC, TT], BF16)
        yt = sc.toks2(KC)
        gi = 0
        for tau in range(S // TT):
            T0 = tau * TT
            self.tile_norm(l, T0, TT, hn_t, hn_tok, sq, sqt, L, Lt, rs, rst)
            wi = [self.wload(win[:, :, g * 512:(g + 1) * 512], self.v_k512(512)) for g in range(5)]

            def pj(m):
                return self.proj(wi[m // 4], (m % 4) * P, P, hn_t, hn_tok, TT)
            s1, s2 = 5, 6
            for hh in range(4):
                b = pj(4 + hh)
                x = gi % 2
                gi += 1
                self.gelu2(b, TT, vg[hh], vgt[hh], tA[x], tAt[x], tB[x], tBt[x])
                self.A((vgt[hh],), (tAt[x],), out=tA[x], in_=vg[hh], func=AF.Square)
                sc.op('pe', ('matmul', dict(out=self.PS[s1], lhsT=self.ones32, rhs=vg[hh], start=(hh == 0),
                                            stop=(hh == 3))), (vgt[hh], self.const_tok), (self.PT[s1],))
                sc.op('pe', ('matmul', dict(out=self.PS[s2], lhsT=self.ones32, rhs=tA[x], start=(hh == 0),
                                            stop=(hh == 3))), (tAt[x], self.const_tok), (self.PT[s2],))
            self.dbg("hn", hn_t, (hn_tok,))
            self.dbg("vg0_pre", vg[0], (vgt[0],))
            self.A((self.PT[s1],), (mean_t,), out=mean, in_=self.PS[s1], func=AF.Identity, scale=1.0 / 512)
            self.dbg("mean", mean, (mean_t,))
            self.V('tensor_tensor', (mean_t,), (m2_t,), out=m2, in0=mean, in1=mean, op=ALU.mult)
            self.V('scalar_tensor_tensor', (self.PT[s2], m2_t), (m2_t,), out=m2, in0=self.PS[s2],
                   scalar=1.0 / 512, in1=m2, op0=ALU.mult, op1=ALU.subtract)
            self.A((m2_t,), (m2_t,), out=m2, in_=m2, func=AF.Ln, scale=1.0, bias=4.0 * EPS)
            self.A((m2_t,), (rstd_t,), out=rstd, in_=m2, func=AF.Exp, scale=-0.5)
            for hh in range(4):
                self.V('tensor_tensor', (vgt[hh], mean_t), (vgt[hh],), out=vg[hh], in0=vg[hh], in1=mean,
                       op=ALU.subtract)
                self.V('tensor_tensor', (vgt[hh], rstd_t), (vgt[hh],), out=vg[hh], in0=vg[hh], in1=rstd,
                       op=ALU.mult)
                self.V('tensor_scalar', (vgt[hh], self.const_tok), (vgt[hh],), out=vg[hh], in0=vg[hh],
                       scalar1=self.sm_col(O_SGG + o * 4 + hh), scalar2=self.sm_col(O_SGB + o * 4 + hh),
                       op0=ALU.mult, op1=ALU.add)
            self.dbg("rstd", rstd, (rstd_t,))
            self.dbg("vg0_ln", vg[0], (vgt[0],))
            self.dbg("wsT", wsT, (wsT_tok,))
            for blk in range(4):
                bt = self.nb()
                for hh in range(4):
                    sc.op('pe', ('transpose', dict(out=self.PS[bt][:, hh * P:(hh + 1) * P],
                                                   in_=vg[hh][:, blk * P:(blk + 1) * P],
                                                   identity=self.sm[:, O_IDENT:O_IDENT + P])),
                          (vgt[hh], self.const_tok), (self.PT[bt],))
                self.A((self.PT[bt],), (vtok_t[blk],), out=vtok[:, blk, :], in_=self.PS[bt], func=AF.Copy)
            for hh in range(4):
                b = pj(hh)
                x = gi % 2
                gi += 1
                self.gelu2(b, TT, ug[x], ugt[x], tA[x], tAt[x], tB[x], tBt[x])
                bm = self.nb()
                for blk in range(4):
                    sc.op('pe', ('matmul', dict(out=self.PS[bm][:, blk * P:(blk + 1) * P],
                                                lhsT=vtok[:, blk, hh * P:(hh + 1) * P], rhs=wsT[:, hh, :],
                                                start=True, stop=True)),
                          (vtok_t[blk], wsT_tok), (self.PT[bm],))
                self.V('tensor_tensor', (self.PT[bm], prm_tok), (tmpt[x],),
                       out=tmp[x].rearrange("p (a b) -> p a b", a=4),
                       in0=self.PS[bm].rearrange("p (a b) -> p a b", a=4),
                       in1=biasB[:, hh:hh + 1, :].to_broadcast([P, 4, P]), op=ALU.add)
                self.dbg("vtok", vtok, vtok_t)
                self.dbg("tmp0", tmp[x], (tmpt[x],))
                self.dbg("ug0", ug[x], (ugt[x],))
                self.V('scalar_tensor_tensor', (tmpt[x], ugt[x]), (yt[hh],), out=y[:, hh, :], in0=tmp[x],
                       scalar=0.5, in1=ug[x], op0=ALU.mult, op1=ALU.mult)
            for ch in range(4):
                bx = pj(8 + ch)
                bg_ = pj(12 + ch)
                bc = pj(16 + ch)
                x = gi % 2
                gi += 1
                self.A((self.PT[bx],), (tmpt[x],), out=tmp[x], in_=self.PS[bx], func=AF.Copy)
                self.V('tensor_tensor', (self.PT[bc], tmpt[x]), (xct[ch],), out=xc[:, ch, 2:TT + 2],
                       in0=self.PS[bc], in1=tmp[x], op=ALU.mult)
                wcol = lambda i: self.sm_col(O_SCW + (o * 4 + ch) * 3 + i)
                self.V('tensor_scalar', (xct[ch], self.const_tok), (acct[x],), out=acc[x], in0=xc[:, ch, 0:TT],
                       scalar1=wcol(0), scalar2=None, op0=ALU.mult)
                for i in (1, 2):
                    self.V('scalar_tensor_tensor', (xct[ch], acct[x], self.const_tok), (acct[x],), out=acc[x],
                           in0=xc[:, ch, i:TT + i], scalar=wcol(i), in1=acc[x], op0=ALU.mult, op1=ALU.add)
                self.V('tensor_tensor', (acct[x], self.PT[bg_]), (yt[4 + ch],), out=y[:, 4 + ch, :], in0=acc[x],
                       in1=self.PS[bg_], op=ALU.mult)
                self.A((xct[ch],), (xct[ch],), out=xc[:, ch, 0:2], in_=xc[:, ch, TT:TT + 2], func=AF.Copy)
            self.dbg("y", y, yt)
            wo = [self.wload(wout[:, :, g * 512:(g + 1) * 512], self.v_k512(512)) for g in range(2)]
            self.out_proj(wo, o, T0, TT, y, yt)
        self.nb_range = (0, 7)

    def mixer_ab(self, l):
        sc, ar, S = self.sc, self.ar, self.S
        e = l // 2
        TT = 256
        NCK = 2
        IL = self.IL
        NGB = 8 - IL
        self.nb_range = (0, NGB)
        win = self.w["ab_w_in"][e].rearrange("(k p) f -> p k f", p=P)
        wout = self.w["ab_w_out"][e].rearrange("(k p) f -> p k f", p=P)
        self.phase()
        tk = sc.tok
        ident = self.sm[:, O_IDENT:O_IDENT + P]
        f32 = lambda *s: ar.alloc(list(s), F32)
        b16 = lambda *s: ar.alloc(list(s), BF16)

        class Pl:
            def __init__(s_, shape, dt, n):
                s_.free = [(ar.alloc(shape, dt), sc.tok()) for _ in range(n)]
                s_.n = n
                s_.low = n

            def get(s_):
                assert s_.free, "pool empty"
                it = s_.free.pop(0)
                s_.low = min(s_.low, len(s_.free))
                return it

            def put(s_, *items):
                for it in items:
                    s_.free.append(it)
        S32 = f32(4, P)
        S16 = b16(4, P)
        S32t = sc.toks2(4)
        S16t = sc.toks2(4)
        qcar = b16(12, 4)
        qcart = sc.toks2(12)
        DG = b16(48, P)
        DGt = tk()
        for ci in range(12):
            for i in range(4):
                self.V('tensor_scalar', (self.const_tok,), (DGt,), out=DG[:, ci * 4 + i, :], in0=self.ident_bf,
                       scalar1=self.sm_col(O_DNW + (e * 12 + ci) * 4 + i), scalar2=None, op0=ALU.mult)
        pcar = f32(4, 16)
        pcart = sc.toks2(4)
        self.V('memset', (), tuple(S32t), ap=S32, constant=0.0)
        self.V('memset', (), tuple(S16t), ap=S16, constant=0.0)
        self.V('memset', (), tuple(qcart), ap=qcar, constant=0.0)
        self.V('memset', (), tuple(pcart), ap=pcar, constant=0.0)
        nA = f32(1)
        nAt = tk()
        self.A((self.const_tok,), (nAt,), out=nA, in_=self.sm_col(O_ALOG + e), func=AF.Exp)
        self.V('tensor_scalar', (nAt,), (nAt,), out=nA, in0=nA, scalar1=-1.0, scalar2=None, op0=ALU.mult)
        hn_t = b16(KC, TT)
        hn_tok = tk()
        rs = f32(TT)
        rst = tk()
        X8, X8t = f32(TT), tk()
        W8, W8t = f32(TT), tk()
        G8, G8t = f32(TT), tk()
        GC8, GC8t = f32(TT), tk()
        abuf, abuft = f32(TT + 16), tk()
        sA, sAt = f32(TT + 16), tk()
        sB, sBt = f32(TT + 16), tk()
        pooled, pooledt = b16(TT), tk()
        t16, t16t = f32(16), tk()
        LASTC = f32(4, NCK)
        LASTCt = sc.toks2(4)
        gcol = f32(4, NCK)
        gcolt = sc.toks2(4)
        junk = [b16(P) for _ in range(4)]
        junkt = sc.toks2(4)
        y = b16(KC, TT)
        yt = sc.toks2(KC)
        FP = Pl([264], F32, 7 * IL + 2)
        BP = Pl([TT], BF16, 6 * IL + 3)
        HP = Pl([P], F32, 2 * IL)
        QP = Pl([P], BF16, 2 * IL + 2)
        RP = Pl([264], BF16, IL + 2)

        def rstd(src, stok, n, inv_count, out_ap, out_tok):
            q = BP.get()
            Lb = FP.get()
            self.emit_rstd([src], [stok], n, inv_count, EPS, out_ap, out_tok, [q[0]], [q[1]], Lb[0], Lb[1])
            BP.put(q)
            FP.put(Lb)
        v2 = lambda a: a.rearrange("p (a b) -> p a b", a=NCK)
        r8 = lambda a: a[0:8, 0:TT]

        for tau in range(S // TT):
            T0 = tau * TT
            q0 = BP.get()
            q1 = BP.get()
            q2 = BP.get()
            Lb = FP.get()
            self.tile_norm(l, T0, TT, hn_t, hn_tok, [q0[0], q1[0], q2[0]], [q0[1], q1[1], q2[1]], Lb[0], Lb[1],
                           rs, rst)
            BP.put(q0, q1, q2)
            FP.put(Lb)
            w_bg = self.wload(win[:, :, 2560:2568], self.v_k512(8))
            w_a = self.wload(win[:, :, 0:512], self.v_k512(512))
            w_q = self.wload(win[:, :, 512:1024], self.v_k512(512))
            w_k = self.wload(win[:, :, 1024:1536], self.v_k512(512))
            w_v = self.wload(win[:, :, 1536:2048], self.v_k512(512))
            b = self.proj(w_bg, 0, 8, hn_t, hn_tok, TT)
            w_z = self.wload(win[:, :, 2048:2560], self.v_k512(512))
            self.V('tensor_scalar', (self.PT[b], self.const_tok), (X8t,), out=r8(X8), in0=self.PS[b][0:8, 0:TT],
                   scalar1=self.sm[0:8, O_DTB + e:O_DTB + e + 1], scalar2=None, op0=ALU.add)
            self.A((X8t,), (W8t,), out=r8(W8), in_=r8(X8), func=AF.Abs)
            self.A((W8t,), (W8t,), out=r8(W8), in_=r8(W8), func=AF.Exp, scale=-1.0)
            self.A((W8t,), (W8t,), out=r8(W8), in_=r8(W8), func=AF.Ln, scale=1.0, bias=1.0)
            self.V('scalar_tensor_tensor', (X8t, W8t), (W8t,), out=r8(W8), in0=r8(X8), scalar=0.0, in1=r8(W8),
                   op0=ALU.max, op1=ALU.add)
            self.V('tensor_tensor', (X8t, W8t), (X8t,), out=r8(X8), in0=r8(X8), in1=r8(W8), op=ALU.subtract)
            self.A((X8t,), (X8t,), out=r8(X8), in_=r8(X8), func=AF.Exp)
            self.V('tensor_scalar', (W8t, nAt), (G8t,), out=r8(G8), in0=r8(W8), scalar1=nA[0:8, 0:1], scalar2=None,
                   op0=ALU.mult)
            self.V('tensor_tensor_scan', (G8t, self.const_tok), (GC8t,), out=r8(GC8),
                   data0=self.sm[0:8, O_RESET:O_RESET + TT], data1=r8(G8), initial=0.0, op0=ALU.mult, op1=ALU.add)

            def pool_gen():
                for g in range(4):
                    wdw = POOL_WINDOWS[g]
                    bp = self.proj(w_a, g * P, P, hn_t, hn_tok, TT)
                    self.A((pcart[g],), (abuft,), out=abuf[:, 0:16], in_=pcar[:, g, :], func=AF.Copy)
                    self.A((self.PT[bp],), (abuft,), out=abuf[:, 16:16 + TT], in_=self.PS[bp][:, 0:TT],
                           func=AF.Copy)
                    cur, curt, lo = abuf, abuft, 0
                    W_ = TT + 16
                    for k in range(g + 1):
                        sh = 1 << k
                        nxt, nxtt = (sA, sAt) if k % 2 == 0 else (sB, sBt)
                        self.V('tensor_tensor', (curt,), (nxtt,), out=nxt[:, lo + sh:W_], in0=cur[:, lo + sh:W_],
                               in1=cur[:, lo:W_ - sh], op=ALU.add)
                        cur, curt, lo = nxt, nxtt, lo + sh
                    self.V('scalar_tensor_tensor', (curt, abuft), (pooledt,), out=pooled, in0=cur[:, 16:W_],
                           scalar=1.0 / wdw, in1=abuf[:, 16:W_], op0=ALU.mult, op1=ALU.subtract)
                    if tau == 0:
                        self.V('tensor_tensor', (curt, self.const_tok), (t16t,), out=t16, in0=cur[:, 16:32],
                               in1=self.sm[:, O_INVC + g * 16:O_INVC + (g + 1) * 16], op=ALU.mult)
                        self.V('tensor_tensor', (t16t, abuft, pooledt), (pooledt,), out=pooled[:, 0:16], in0=t16,
                               in1=abuf[:, 16:32], op=ALU.subtract)
                    self.A((abuft,), (pcart[g],), out=pcar[:, g, :], in_=abuf[:, TT:TT + 16], func=AF.Copy)
                    bq = self.nb()
                    sc.op('pe', ('matmul', dict(out=self.PS[bq][:, 0:TT],
                                                lhsT=self.poolw[:, (e * 4 + g) * P:(e * 4 + g + 1) * P], rhs=pooled,
                                                start=True, stop=True)), (pooledt, self.pw_tok), (self.PT[bq],))
                    self.A((self.PT[bq], self.const_tok), (yt[g],), out=y[:, g, :], in_=self.PS[bq][:, 0:TT],
                           func=AF.Identity, scale=self.sm_col(O_PSC + e * 4 + g))
                    yield

            def head(hd, CH):
                XS = []
                for idx, wsl in enumerate((w_q, w_k, w_v)):
                    ci = idx * 4 + hd
                    raw, rawt = RP.get()
                    xs = FP.get()
                    bp = self.proj(wsl, hd * P, P, hn_t, hn_tok, TT)
                    self.A((qcart[ci],), (rawt,), out=raw[:, 0:3], in_=qcar[:, ci, 0:3], func=AF.Copy)
                    self.A((self.PT[bp],), (rawt,), out=raw[:, 3:3 + TT], in_=self.PS[bp][:, 0:TT], func=AF.Copy)
                    bc = self.nb()
                    sc.op('pe', [('matmul', dict(out=self.PS[bc][:, 0:TT], lhsT=DG[:, ci * 4 + i, :],
                                                 rhs=raw[:, i:TT + i], start=(i == 0), stop=(i == 3)))
                                 for i in range(4)], (rawt, DGt), (self.PT[bc],))
                    self.A((rawt,), (qcart[ci],), out=qcar[:, ci, 0:3], in_=raw[:, TT:TT + 3], func=AF.Copy)
                    self.A((self.PT[bc],), (xs[1],), out=xs[0][:, 0:TT], in_=self.PS[bc][:, 0:TT], func=AF.Silu)
                    RP.put((raw, rawt))
                    XS.append(xs)
                    yield
                RQ = FP.get()
                RK = FP.get()
                rstd(XS[0][0][:, 0:TT], XS[0][1], TT, 1.0, RQ[0][:, 0:TT], RQ[1])
                rstd(XS[1][0][:, 0:TT], XS[1][1], TT, 1.0, RK[0][:, 0:TT], RK[1])
                yield
                KN, KNt = BP.get()
                QN, QNt = BP.get()
                self.V('tensor_tensor', (XS[1][1], RK[1]), (KNt,), out=KN, in0=XS[1][0][:, 0:TT], in1=RK[0][:, 0:TT],
                       op=ALU.mult)
                self.V('scalar_tensor_tensor', (XS[0][1], RQ[1]), (QNt,), out=QN, in0=XS[0][0][:, 0:TT],
                       scalar=float(P ** -0.5), in1=RQ[0][:, 0:TT], op0=ALU.mult, op1=ALU.mult)
                FP.put(RQ, RK, XS[0], XS[1])
                yield
                GS, GSt = FP.get()
                BS, BSt = FP.get()
                self.V('tensor_scalar', (GC8t, self.const_tok), (GSt,), out=r8(GS), in0=r8(GC8),
                       scalar1=self.sm[0:8, O_IDENT + 4 + hd:O_IDENT + 5 + hd], scalar2=None, op0=ALU.mult)
                self.V('tensor_scalar', (X8t, self.const_tok), (BSt,), out=r8(BS), in0=r8(X8),
                       scalar1=self.sm[0:8, O_IDENT + hd:O_IDENT + hd + 1], scalar2=None, op0=ALU.mult)
                b1 = self.nb()
                sc.op('pe', [('matmul', dict(out=self.PS[b1][:, 0:TT], lhsT=self.ones32[0:8, :], rhs=r8(GS),
                                             start=True, stop=True)),
                             ('matmul', dict(out=self.PS[b1][:, TT:2 * TT], lhsT=self.ones32[0:8, :], rhs=r8(BS),
                                             start=True, stop=True))],
                      (GSt, BSt, self.const_tok), (self.PT[b1],))
                FP.put((GS, GSt), (BS, BSt))
                GCBb = FP.get()
                EGCb = FP.get()
                BETABb = FP.get()
                GCB, GCBt = GCBb[0][:, 0:TT], GCBb[1]
                EGC, EGCt = EGCb[0][:, 0:TT], EGCb[1]
                BETAB, BETABt = BETABb[0][:, 0:TT], BETABb[1]
                self.A((self.PT[b1],), (GCBt,), out=GCB, in_=self.PS[b1][:, 0:TT], func=AF.Copy)
                self.A((self.PT[b1],), (EGCt,), out=EGC, in_=self.PS[b1][:, 0:TT], func=AF.Exp)
                self.A((self.PT[b1],), (BETABt,), out=BETAB, in_=self.PS[b1][:, TT:2 * TT], func=AF.Copy)
                yield
                DLb = FP.get()
                BEb = FP.get()
                BSTb = FP.get()
                DL, DLt = DLb[0][:, 0:TT], DLb[1]
                BE, BEt = BEb[0][:, 0:TT], BEb[1]
                BST, BSTt = BSTb[0][:, 0:TT], BSTb[1]
                g3 = v2(GCB)
                self.V('tensor_tensor', (GCBt,), (DLt,), out=v2(DL),
                       in0=g3[:, :, P - 1:P].to_broadcast([P, NCK, P]), in1=g3, op=ALU.subtract)
                self.A((DLt,), (DLt,), out=DL, in_=DL, func=AF.Exp)
                self.V('tensor_copy', (EGCt,), (LASTCt[hd],), out=LASTC[:, hd, :], in_=EGC[:, P - 1:TT:P])
                self.V('tensor_tensor', (BETABt, EGCt), (BEt,), out=BE, in0=BETAB, in1=EGC, op=ALU.mult)
                self.V('tensor_tensor', (BETABt, self.const_tok), (BSTt,), out=v2(BST), in0=v2(BETAB),
                       in1=self.sm[:, O_STRICT:O_STRICT + P].unsqueeze(1).to_broadcast([P, NCK, P]), op=ALU.mult)
                yield
                KBG, KBGt = BP.get()
                QDEC, QDECt = BP.get()
                KDTb = FP.get()
                VBTb = FP.get()
                KDT, KDTt = KDTb[0][:, 0:TT], KDTb[1]
                VBT, VBTt = VBTb[0][:, 0:TT], VBTb[1]
                self.V('tensor_tensor', (KNt, BEt), (KBGt,), out=KBG, in0=KN, in1=BE, op=ALU.mult)
                self.V('tensor_tensor', (QNt, EGCt), (QDECt,), out=QDEC, in0=QN, in1=EGC, op=ALU.mult)
                self.V('tensor_tensor', (KNt, DLt), (KDTt,), out=KDT, in0=KN, in1=DL, op=ALU.mult)
                self.V('tensor_tensor', (XS[2][1], BETABt), (VBTt,), out=VBT, in0=XS[2][0][:, 0:TT], in1=BETAB,
                       op=ALU.mult)
                FP.put(BEb, DLb, XS[2], EGCb, BETABb)
                yield
                btp = self.nb()
                ins = []
                for ck in range(NCK):
                    ins.append(('transpose', dict(out=self.PS[btp][:, ck * P:(ck + 1) * P],
                                                  in_=KDT[:, ck * P:(ck + 1) * P], identity=ident)))
                for ck in range(NCK):
                    ins.append(('transpose', dict(out=self.PS[btp][:, (NCK + ck) * P:(NCK + ck + 1) * P],
                                                  in_=VBT[:, ck * P:(ck + 1) * P], identity=ident)))
                sc.op('pe', ins, (KDTt, VBTt, self.const_tok), (self.PT[btp],))
                KDECb = BP.get()
                VBb = FP.get()
                KDEC, KDECt = KDECb[0], KDECb[1]
                VB, VBt = VBb[0][:, 0:TT], VBb[1]
                self.A((self.PT[btp],), (KDECt,), out=KDEC, in_=self.PS[btp][:, 0:NCK * P], func=AF.Copy)
                self.A((self.PT[btp],), (VBt,), out=VB, in_=self.PS[btp][:, NCK * P:2 * NCK * P], func=AF.Copy)
                FP.put(KDTb, VBTb)
                yield
                ATb = BP.get()
                AT, ATt = v2(ATb[0]), ATb[1]
                N0b = FP.get()
                Qb = FP.get()
                M0b = FP.get()
                Nb = [v2(N0b[0][:, 0:TT]), None]
                Nbt = [N0b[1], None]
                Mb = [v2(M0b[0][:, 0:TT]), None]
                Mbt = [M0b[1], None]
                Q, Qt = v2(Qb[0][:, 0:TT]), Qb[1]
                for ck in range(NCK):
                    cs = slice(ck * P, (ck + 1) * P)
                    E1b = HP.get()
                    GTBb = HP.get()
                    E1, E1t = E1b
                    GTB, GTBt = GTBb
                    self.V('scalar_tensor_tensor', (GCBt, self.const_tok), (junkt[hd], gcolt[hd]), out=junk[hd],
                           in0=GCB[:, cs], scalar=1.0, in1=ident, op0=ALU.mult, op1=ALU.mult,
                           accum_out=gcol[:, hd, ck:ck + 1])
                    self.V('scalar_tensor_tensor', (GCBt, gcolt[hd], self.const_tok), (E1t,), out=E1,
                           in0=GCB[:, cs], scalar=gcol[:, hd, ck:ck + 1], in1=self.sm[:, O_MASKA:O_MASKA + P],
                           op0=ALU.subtract, op1=ALU.add)
                    self.A((E1t,), (E1t,), out=E1, in_=E1, func=AF.Exp)
                    b2 = self.nb()
                    sc.op('pe', [('matmul', dict(out=self.PS[b2][:, 0:P], lhsT=KN[:, cs], rhs=KN[:, cs],
                                                 start=True, stop=True)),
                                 ('matmul', dict(out=self.PS[b2][:, P:2 * P], lhsT=KN[:, cs], rhs=QN[:, cs],
                                                 start=True, stop=True))],
                          (KNt, QNt), (self.PT[b2],))
                    self.V('tensor_tensor', (self.PT[b2], E1t), (ATt,), out=AT[:, ck, :],
                           in0=self.PS[b2][:, P:2 * P], in1=E1, op=ALU.mult)
                    self.V('tensor_tensor', (E1t, BSTt), (GTBt,), out=GTB, in0=E1, in1=BST[:, cs], op=ALU.mult)
                    self.V('tensor_tensor', (self.PT[b2], GTBt), (Nbt[0],), out=Nb[0][:, ck, :],
                           in0=self.PS[b2][:, 0:P], in1=GTB, op=ALU.mult)
                    self.V('scalar_tensor_tensor', (Nbt[0], self.const_tok), (Qt,), out=Q[:, ck, :],
                           in0=Nb[0][:, ck, :], scalar=-1.0, in1=ident, op0=ALU.mult, op1=ALU.add)
                    HP.put(E1b, GTBb)
                    yield
                bl = self.nb()
                sc.op('pe', [('transpose', dict(out=self.PS[bl][:, ck * P:(ck + 1) * P], in_=Nb[0][:, ck, :],
                                                identity=ident)) for ck in range(NCK)],
                      (Nbt[0], self.const_tok), (self.PT[bl],))
                self.A((self.PT[bl],), (Mbt[0],), out=Mb[0].rearrange("p a b -> p (a b)"),
                       in_=self.PS[bl][:, 0:NCK * P], func=AF.Copy)
                FP.put(GCBb, BSTb)
                BP.put((KN, KNt), (QN, QNt))
                M1b = FP.get()
                N1b = FP.get()
                Mb[1], Mbt[1] = v2(M1b[0][:, 0:TT]), M1b[1]
                Nb[1], Nbt[1] = v2(N1b[0][:, 0:TT]), N1b[1]
                TTb = BP.get()
                TTm, TTt = v2(TTb[0]), TTb[1]
                yield
                fl = lambda a: a.rearrange("p a b -> p (a b)")
                for k in range(6):
                    c0, c1 = k % 2, (k + 1) % 2
                    pm = self.nb()
                    sc.op('pe', [('matmul', dict(out=self.PS[pm][:, ck * P:(ck + 1) * P], lhsT=Nb[c0][:, ck, :],
                                                 rhs=Mb[c0][:, ck, :], start=True, stop=True))
                                 for ck in range(NCK)], (Nbt[c0], Mbt[c0]), (self.PT[pm],))
                    if k < 5:
                        pn = self.nb()
                        sc.op('pe', [('matmul', dict(out=self.PS[pn][:, ck * P:(ck + 1) * P], lhsT=Mb[c0][:, ck, :],
                                                     rhs=Nb[c0][:, ck, :], start=True, stop=True))
                                     for ck in range(NCK)], (Nbt[c0], Mbt[c0]), (self.PT[pn],))
                    self.A((self.PT[pm],), (Mbt[c1],), out=fl(Mb[c1]), in_=self.PS[pm][:, 0:NCK * P], func=AF.Copy)
                    if k < 5:
                        self.V('tensor_copy', (self.PT[pn],), (Nbt[c1],), out=fl(Nb[c1]),
                               in_=self.PS[pn][:, 0:NCK * P])
                    yield
                    pq = self.nb()
                    sc.op('pe', [('matmul', dict(out=self.PS[pq][:, ck * P:(ck + 1) * P], lhsT=Mb[c1][:, ck, :],
                                                 rhs=Q[:, ck, :], start=True, stop=True))
                                 for ck in range(NCK)], (Mbt[c1], Qt), (self.PT[pq],))
                    if k < 5:
                        self.V('tensor_tensor', (self.PT[pq], Qt), (Qt,), out=fl(Q), in0=fl(Q),
                               in1=self.PS[pq][:, 0:NCK * P], op=ALU.add)
                    else:
                        self.V('tensor_tensor', (self.PT[pq], Qt), (TTt,), out=fl(TTm), in0=fl(Q),
                               in1=self.PS[pq][:, 0:NCK * P], op=ALU.add)
                    yield
                FP.put(N0b, M0b, M1b, N1b, Qb)
                OSb = FP.get()
                OS, OSt = OSb[0][:, 0:TT], OSb[1]
                for ck in range(NCK):
                    cs = slice(ck * P, (ck + 1) * P)
                    R, Rt = QP.get()
                    VN, VNt = QP.get()
                    sc.op('pe', ('matmul', dict(out=self.PS[CH][:, 0:P], lhsT=KBG[:, cs], rhs=S16[:, hd, :],
                                                start=True, stop=True)), (KBGt, S16t[hd]), (self.PT[CH],))
                    self.V('tensor_tensor', (VBt, self.PT[CH]), (Rt,), out=R, in0=v2(VB)[:, ck, :],
                           in1=self.PS[CH][:, 0:P], op=ALU.subtract)
                    yield
                    sc.op('pe', ('matmul', dict(out=self.PS[CH][:, P:2 * P], lhsT=TTm[:, ck, :], rhs=R,
                                                start=True, stop=True)), (TTt, Rt), (self.PT[CH],))
                    self.A((self.PT[CH],), (VNt,), out=VN, in_=self.PS[CH][:, P:2 * P], func=AF.Copy)
                    yield
                    sc.op('pe', [('matmul', dict(out=self.PS[CH][:, 2 * P:3 * P], lhsT=S16[:, hd, :],
                                                 rhs=QDEC[:, cs], start=True, stop=False)),
                                 ('matmul', dict(out=self.PS[CH][:, 2 * P:3 * P], lhsT=VN, rhs=AT[:, ck, :],
                                                 start=False, stop=True)),
                                 ('matmul', dict(out=self.PS[CH][:, 3 * P:4 * P], lhsT=v2(KDEC)[:, ck, :], rhs=VN,
                                                 start=True, stop=True))],
                          (S16t[hd], QDECt, VNt, ATt, KDECt), (self.PT[CH],))
                    self.V('scalar_tensor_tensor', (S32t[hd], LASTCt[hd], self.PT[CH]), (S32t[hd],),
                           out=S32[:, hd, :], in0=S32[:, hd, :], scalar=LASTC[:, hd, ck:ck + 1],
                           in1=self.PS[CH][:, 3 * P:4 * P], op0=ALU.mult, op1=ALU.add)
                    self.V('tensor_copy', (self.PT[CH],), (OSt,), out=OS[:, cs], in_=self.PS[CH][:, 2 * P:3 * P])
                    self.A((S32t[hd],), (S16t[hd],), out=S16[:, hd, :], in_=S32[:, hd, :], func=AF.Copy)
                    QP.put((R, Rt), (VN, VNt))
                    yield
                BP.put((KBG, KBGt), (QDEC, QDECt), KDECb, ATb, TTb)
                FP.put(VBb)
                ROb = FP.get()
                RO, ROt = ROb[0][:, 0:TT], ROb[1]
                rstd(OS, OSt, TT, 1.0 / P, RO, ROt)
                ZSb = FP.get()
                ZS, ZSt = ZSb[0][:, 0:TT], ZSb[1]
                bz = self.proj(w_z, hd * P, P, hn_t, hn_tok, TT)
                self.A((self.PT[bz],), (ZSt,), out=ZS, in_=self.PS[bz][:, 0:TT], func=AF.Silu)
                yield
                Tmb = FP.get()
                Tm, Tmt = Tmb[0][:, 0:TT], Tmb[1]
                self.V('scalar_tensor_tensor', (OSt, ROt, self.const_tok), (Tmt,), out=Tm, in0=OS,
                       scalar=self.sm_col(O_ONORM + e), in1=RO, op0=ALU.mult, op1=ALU.mult)
                self.V('tensor_tensor', (Tmt, ZSt), (yt[4 + hd],), out=y[:, 4 + hd, :], in0=Tm, in1=ZS, op=ALU.mult)
                FP.put(OSb, ROb, ZSb, Tmb)

            for h0 in range(0, 4, IL):
                gens = [head(h0 + i, NGB + i) for i in range(IL)]
                if h0 == 0:
                    gens.append(pool_gen())
                while gens:
                    for g_ in list(gens):
                        try:
                            next(g_)
                        except StopIteration:
                            gens.remove(g_)
            wo = [self.wload(wout[:, :, g * 512:(g + 1) * 512], self.v_k512(512)) for g in range(2)]
            self.out_proj(wo, e, T0, TT, y, yt)
        self.pool_low = (FP.low, BP.low, HP.low, QP.low)
        self.nb_range = (0, 7)


def make_smalls(inp):
    sm = np.zeros((P, NSM), np.float32)
    norms = np.concatenate([np.stack([inp['ffn1_norm'][l], inp['mix_norm'][l], inp['ffn2_norm'][l]]) for l in range(4)]
                           + [inp['final_norm'][None]], 0)
    sm[:, O_NORM:O_NORM + 104] = norms.reshape(13, 8, P).transpose(2, 0, 1).reshape(P, 104)
    sm[:, O_PSC:O_PSC + 8] = inp['pool_scale'].reshape(2, 4, P).transpose(2, 0, 1).reshape(P, 8)
    sm[:, O_DNW:O_DNW + 96] = inp['dn_conv_w'].reshape(2, 4, 12, P).transpose(3, 0, 2, 1).reshape(P, 96)
    sm[:, O_ONORM:O_ONORM + 2] = inp['dn_out_norm'].T
    sm[:, O_SGG:O_SGG + 8] = inp['sgu_norm_g'].reshape(2, 4, P).transpose(2, 0, 1).reshape(P, 8)
    sm[:, O_SGB:O_SGB + 8] = inp['sgu_norm_b'].reshape(2, 4, P).transpose(2, 0, 1).reshape(P, 8)
    sm[:, O_SCW:O_SCW + 24] = inp['sc_conv_w'].reshape(2, 3, 4, P).transpose(3, 0, 2, 1).reshape(P, 24)
    sm[4:8, O_ALOG:O_ALOG + 2] = inp['dn_a_log'].T
    sm[4:8, O_DTB:O_DTB + 2] = inp['dn_dt_bias'].T
    pp = np.arange(P)[:, None]
    ff = np.arange(P)[None, :]
    sm[:, O_IDENT:O_IDENT + P] = (pp == ff)
    sm[:, O_MASKA:O_MASKA + P] = np.where(ff >= pp, 0.0, -30000.0)
    sm[:, O_STRICT:O_STRICT + P] = (ff > pp)
    sm[:, O_TRIL:O_TRIL + P] = (ff <= pp)
    sm[:, O_RESET:O_RESET + 256] = (np.arange(256) % P != 0)[None, :]
    for g, w in enumerate(POOL_WINDOWS):
        sm[:, O_INVC + g * 16:O_INVC + (g + 1) * 16] = (1.0 / np.minimum(np.arange(16) + 1, w))[None, :]
    return sm


def make_in_maps(inp, n_cores, NB):
    inp = {k: np.asarray(v) for k, v in inp.items()}
    x = inp['x']
    shared = {}
    for k in ("ffn1_w_gate", "ffn1_w_up", "ffn1_w_down", "ffn2_w_gate", "ffn2_w_up", "ffn2_w_down",
              "ab_w_in", "ab_w_out", "cd_w_in", "cd_w_out"):
        shared[k] = np.ascontiguousarray(inp[k], dtype=np.float32)
    shared["smalls"] = make_smalls(inp)
    shared["poolw"] = np.ascontiguousarray(inp['pool_w'].transpose(2, 0, 1, 3).reshape(P, 2 * 4 * P), dtype=np.float32)
    shared["sguw"] = np.ascontiguousarray(inp['sgu_w'].transpose(0, 2, 1, 3).reshape(2, P, 4 * P), dtype=np.float32)
    shared["sgub"] = np.ascontiguousarray(
        np.broadcast_to(inp['sgu_bias'][:, None, :, :], (2, P, 4, P)).reshape(2, P, 4 * P), dtype=np.float32)
    maps = []
    for c in range(n_cores):
        m = dict(shared)
        m["xT"] = np.ascontiguousarray(x[c * NB:(c + 1) * NB].transpose(0, 2, 1))
        maps.append(m)
    return maps


def kernel(**inputs):
    n_cores = 8
    x = np.asarray(inputs['x'])
    B, S, _ = x.shape
    NB = B // n_cores
    bld = Builder(NB, S, range(4))
    nc = bld.build()
    maps = make_in_maps(inputs, n_cores, NB)
    res = run_bass_kernel_spmd(nc, maps, core_ids=list(range(n_cores)))
    out = np.empty((B, S, D), np.float32)
    for c in range(n_cores):
        out[c * NB:(c + 1) * NB] = res.results[c]["yT"].transpose(0, 2, 1)
    return out
```

```python
import numpy as np
from contextlib import ExitStack
import concourse.bass as bass
import concourse.mybir as mybir
from concourse.bass_utils import run_bass_kernel_spmd

F32 = mybir.dt.float32
BF16 = mybir.dt.bfloat16
AF = mybir.ActivationFunctionType
ALU = mybir.AluOpType

P = 128
D = 1024
KC = 8
FF = 2816
EPS = 1e-6
NSLOT = 6
AB_IN = 2568
CD_IN = 2560
POOL_WINDOWS = (2, 4, 8, 16)
GELU_C = 0.7978845608028654

O_NORM = 0
O_PSC = 104
O_DNW = 112
O_ONORM = 208
O_SGG = 210
O_SGB = 218
O_SCW = 226
O_ALOG = 250
O_DTB = 252
O_IDENT = 254
O_MASKA = 382
O_STRICT = 510
O_TRIL = 638
O_RESET = 766
O_INVC = 1022
NSM = 1086

ENGS = ('pe', 'act', 'dve', 'pool', 'sp')
BLK = {'pe': 'tensor', 'act': 'scalar', 'dve': 'vector', 'pool': 'gpsimd', 'sp': 'sync'}


class Tok:
    __slots__ = ('w', 'r')

    def __init__(self):
        self.w = None
        self.r = {}


class Sched:
    def __init__(self, nc, es, same=True):
        self.nc = nc
        self.es = es
        self.same = same
        self.prog = {e: [] for e in ENGS}
        self.seen = {e: {} for e in ENGS}
        self.sems = []
        self.toks = []
        self.esem = {}
        self.cnt = {}
        self.epoch = -1
        self.new_epoch()

    def new_sem(self, name):
        h = self.es.enter_context(self.nc.semaphore(name))
        self.sems.append(h)
        return len(self.sems) - 1

    def tok(self):
        t = Tok()
        self.toks.append(t)
        return t

    def toks2(self, *dims):
        if len(dims) == 1:
            return [self.tok() for _ in range(dims[0])]
        return [self.toks2(*dims[1:]) for _ in range(dims[0])]

    def barrier(self, engs):
        fin = {e: (self.esem[e], self.cnt[e]) for e in engs if self.cnt[e] > 0}
        for e in engs:
            for p, (s, v) in fin.items():
                if self.seen[e].get(s, 0) >= v:
                    continue
                self.seen[e][s] = v
                self.prog[e].append(('w', s, v))

    def new_epoch(self):
        if self.epoch >= 0:
            self.barrier(ENGS)
            old = set(self.esem.values())
            for t in self.toks:
                if t.w is not None and t.w[0] in old:
                    t.w = None
                t.r = {s: v for s, v in t.r.items() if s not in old}
        self.epoch += 1
        self.esem = {e: self.new_sem(f"e{self.epoch}_{e}") for e in ENGS}
        self.cnt = {e: 0 for e in ENGS}

    def op(self, eng, insts, reads=(), writes=(), dma=None):
        deps = {}

        def need(d):
            if d is None:
                return
            s, v = d
            if deps.get(s, 0) < v:
                deps[s] = v
        for t in reads:
            need(t.w)
        for t in writes:
            need(t.w)
            for s, v in t.r.items():
                need((s, v))
        my = self.esem[eng]
        for s, v in deps.items():
            if s == my and (eng == 'pe' or not self.same):
                continue
            if self.seen[eng].get(s, 0) >= v:
                continue
            self.seen[eng][s] = v
            self.prog[eng].append(('w', s, v))
        if dma is None:
            self.cnt[eng] += 1
            tk = (my, self.cnt[eng])
            inc = 1
        else:
            tk = dma
            inc = 16
        if isinstance(insts, tuple):
            insts = [insts]
        self.prog[eng].append(('o', insts, tk[0], inc))
        for t in writes:
            t.w = tk
            t.r = {}
        for t in reads:
            if t.r.get(tk[0], 0) < tk[1]:
                t.r[tk[0]] = tk[1]
        return tk

    def emit(self):
        nc = self.nc
        sems = self.sems
        with nc.Block() as block:
            for e in ENGS:
                def body(eng, prog=self.prog[e]):
                    for it in prog:
                        if it[0] == 'w':
                            eng.wait_ge(sems[it[1]], it[2])
                        else:
                            ins = None
                            for name, kw in it[1]:
                                ins = getattr(eng, name)(**kw)
                            ins.then_inc(sems[it[2]], it[3])
                getattr(block, BLK[e])(body)


class Arena:
    def __init__(self, ap, width):
        self.ap = ap
        self.w = width
        self.off = 0

    def reset(self):
        self.off = 0

    def alloc(self, shape, dt):
        n = int(np.prod(shape))
        nw = (n * (4 if dt == F32 else 2) + 3) // 4
        nw = (nw + 7) // 8 * 8
        a = self.ap[:, self.off:self.off + nw]
        self.off += nw
        assert self.off <= self.w, f"arena overflow {self.off} > {self.w}"
        if dt != F32:
            a = a.bitcast(dt)
        a = a[:, 0:n]
        if len(shape) == 2:
            a = a.rearrange("p (a b) -> p a b", a=shape[0])
        elif len(shape) == 3:
            a = a.rearrange("p (a b c) -> p a b c", a=shape[0], b=shape[1])
        return a


class Builder:
    def __init__(self, NB, S, layers, do_ffn=True, do_mix=True, same=True, IL=4):
        self.IL = IL
        self.NB, self.S, self.layers = NB, S, list(layers)
        self.do_ffn, self.do_mix = do_ffn, do_mix
        self.NT = S // 512
        self.es = ExitStack()
        nc = self.nc = bass.Bass("TRN2", target_bir_lowering=False)
        es = self.es
        dt = lambda name, shape, kind="ExternalInput": nc.dram_tensor(name, shape, F32, kind=kind).ap()
        self.xT = dt("xT", [NB, D, S])
        self.yT = dt("yT", [NB, D, S], "ExternalOutput")
        self.w = {}
        for pre in ("ffn1", "ffn2"):
            self.w[pre + "_w_gate"] = dt(pre + "_w_gate", [4, D, FF])
            self.w[pre + "_w_up"] = dt(pre + "_w_up", [4, D, FF])
            self.w[pre + "_w_down"] = dt(pre + "_w_down", [4, FF, D])
        self.w["ab_w_in"] = dt("ab_w_in", [2, D, AB_IN])
        self.w["ab_w_out"] = dt("ab_w_out", [2, D, D])
        self.w["cd_w_in"] = dt("cd_w_in", [2, D, CD_IN])
        self.w["cd_w_out"] = dt("cd_w_out", [2, D, D])
        self.d_smalls = dt("smalls", [P, NSM])
        self.d_poolw = dt("poolw", [P, 2 * 4 * 128])
        self.d_sguw = dt("sguw", [2, P, 4 * 128])
        self.d_sgub = dt("sgub", [2, P, 4 * 128])

        sb = lambda name, shape, d: es.enter_context(nc.sbuf_tensor(name, shape, d))
        self.h_t = sb("h", [P, KC, S], F32)
        self.h = self.h_t[:]
        self.slots = [sb(f"ws{i}", [P, 4096], BF16) for i in range(NSLOT)]
        self.sm = sb("smalls_sb", [P, NSM], F32)[:]
        self.ones_bf = sb("ones_bf", [P, P], BF16)[:]
        self.ones32 = sb("ones32", [P, P], F32)[:]
        self.poolw = sb("poolw_sb", [P, 2 * 4 * 128], BF16)[:]
        self.ident_bf = sb("ident_bf", [P, P], BF16)[:]
        rem = int(nc.sbuf_bytes_remaining)
        aw = (rem - 1024) // 4 // 8 * 8
        self.arena_w = aw
        self.ar = Arena(sb("arena", [P, aw], F32)[:], aw)
        self.PS = [es.enter_context(nc.psum_tensor(f"ps{i}", [P, 512], F32))[:] for i in range(8)]

        self.sc = Sched(nc, es, same=same)
        sc = self.sc
        self.PT = sc.toks2(8)
        self.HT = sc.toks2(KC, self.NT)
        self.slot_tok = sc.toks2(NSLOT)
        self.slot_sem = [sc.new_sem(f"wl{i}") for i in range(NSLOT)]
        self.slot_cnt = [0] * NSLOT
        self.slot_next = 0
        self.slot_epoch = [-1] * NSLOT
        self.h_sem = [sc.new_sem(f"hs{c}") for c in range(KC)]
        self.h_cnt = [0] * KC
        self.misc_sem = sc.new_sem("misc")
        self.misc_cnt = 0
        self.misc2_sem = sc.new_sem("misc2")
        self.misc2_cnt = 0
        self.wres_sem = sc.new_sem("wres")
        self.wres_cnt = 0
        self.const_tok = sc.tok()
        self.bank_rr = 0
        self.dbg_done = set()
        self.dbg_sems = []
        self.nb_range = (0, 7)

    def dbg(self, name, ap, toks):
        if not getattr(self, 'debug', False) or name in self.dbg_done:
            return
        self.dbg_done.add(name)
        shape = [int(v) for v in ap.shape]
        d = self.nc.dram_tensor("dbg_" + name, shape, ap.dtype, kind="ExternalOutput").ap()
        sem = self.sc.new_sem("dbg_" + name)
        self.sc.op('sp', ('dma_start', dict(out=d, in_=ap)), tuple(toks), (), dma=(sem, 16))
        self.dbg_sems.append(sem)

    def A(self, reads, writes, **kw):
        self.sc.op('act', ('activation', kw), reads, writes)

    def V(self, name, reads, writes, **kw):
        self.sc.op('dve', (name, kw), reads, writes)

    def nb(self, lo=None, hi=None):
        if lo is None:
            lo, hi = self.nb_range
        n = hi - lo
        b = lo + (self.bank_rr % n)
        self.bank_rr += 1
        return b

    def sm_col(self, off, n=1):
        return self.sm[:, off:off + n]

    def wload(self, src, view):
        i = self.slot_next % NSLOT
        self.slot_next += 1
        assert (self.slot_cnt[i] == 0 or self.slot_tok[i].r or self.slot_epoch[i] != self.sc.epoch), \
            "weight slot reloaded before any reader was emitted"
        self.slot_epoch[i] = self.sc.epoch
        self.slot_cnt[i] += 16
        dst = view(self.slots[i][:])
        self.sc.op('pool', ('dma_start', dict(out=dst, in_=src)), (), (self.slot_tok[i],),
                   dma=(self.slot_sem[i], self.slot_cnt[i]))
        return i

    def v_k512(self, ncol):
        return lambda a: a.rearrange("p (k f) -> p k f", k=8)[:, :, 0:ncol]

    def slot_k512(self, i):
        return self.slots[i][:].rearrange("p (k f) -> p k f", k=8)

    def phase(self, pool=False):
        self.sc.barrier(('pe', 'act', 'dve', 'sp', 'pool') if pool else ('pe', 'act', 'dve', 'sp'))
        self.ar.reset()

    def wres(self, srcs, views):
        tokn = self.sc.tok()
        for src, dst in zip(srcs, views):
            self.wres_cnt += 16
            self.sc.op('pool', ('dma_start', dict(out=dst, in_=src)), (), (tokn,),
                       dma=(self.wres_sem, self.wres_cnt))
        return tokn

    def _w(self, slot):
        if isinstance(slot, int):
            return self.slot_k512(slot), self.slot_tok[slot]
        return slot

    def emit_rstd(self, srcs, stoks, n, inv_count, eps, out_ap, out_tok, sq, sqt, L, Lt):
        b = self.nb()
        ns = len(srcs)
        for i in range(ns):
            q, qt = sq[i % len(sq)], sqt[i % len(sq)]
            self.A((stoks[i],), (qt,), out=q[:, 0:n], in_=srcs[i], func=AF.Square)
            self.sc.op('pe', ('matmul', dict(out=self.PS[b][:, 0:n], lhsT=self.ones_bf, rhs=q[:, 0:n],
                                             start=(i == 0), stop=(i == ns - 1))),
                       (qt, self.const_tok), (self.PT[b],))
        self.A((self.PT[b],), (Lt,), out=L[:, 0:n], in_=self.PS[b][:, 0:n], func=AF.Ln,
               scale=float(inv_count), bias=float(eps))
        self.A((Lt,), (out_tok,), out=out_ap, in_=L[:, 0:n], func=AF.Exp, scale=-0.5)

    def build(self):
        sc = self.sc
        self.misc_cnt += 16
        sc.op('sp', ('dma_start', dict(out=self.sm, in_=self.d_smalls)), (), (self.const_tok,),
              dma=(self.misc_sem, self.misc_cnt))
        self.misc2_cnt += 16
        pw_tok = sc.tok()
        self.pw_tok = pw_tok
        sc.op('pool', ('dma_start', dict(out=self.poolw, in_=self.d_poolw)), (), (pw_tok,),
              dma=(self.misc2_sem, self.misc2_cnt))
        self.V('memset', (), (self.const_tok,), ap=self.ones_bf, constant=1.0)
        self.V('memset', (), (self.const_tok,), ap=self.ones32, constant=1.0)
        self.V('tensor_copy', (self.const_tok,), (self.const_tok,), out=self.ident_bf,
               in_=self.sm[:, O_IDENT:O_IDENT + P])
        for b in range(self.NB):
            if b > 0:
                sc.new_epoch()
            self.one_batch(b)
        for c in range(KC):
            sc.prog['sp'].append(('w', self.h_sem[c], self.h_cnt[c]))
        for sm_ in self.dbg_sems:
            sc.prog['sp'].append(('w', sm_, 16))
        sc.emit()
        return self.nc

    def one_batch(self, b):
        sc = self.sc
        S = self.S
        for c in range(KC):
            self.h_cnt[c] += 16
            sc.op('sp', ('dma_start', dict(out=self.h[:, c, :], in_=self.xT[b, c * P:(c + 1) * P, :])),
                  (), tuple(self.HT[c]), dma=(self.h_sem[c], self.h_cnt[c]))
        for l in self.layers:
            if self.do_ffn:
                self.ffn(l, 0)
            if self.do_mix:
                if l % 2 == 0:
                    self.mixer_ab(l)
                else:
                    self.mixer_cd(l)
            if self.do_ffn:
                self.ffn(l, 1)
        self.final(b)

    def final(self, b):
        sc = self.sc
        self.phase()
        ar = self.ar
        sq = [ar.alloc([512], BF16) for _ in range(3)]
        sqt = sc.toks2(3)
        L = ar.alloc([512], F32)
        Lt = sc.tok()
        rs = [ar.alloc([512], F32) for _ in range(2)]
        rst = sc.toks2(2)
        for t in range(self.NT):
            tl = slice(t * 512, (t + 1) * 512)
            r, rt = rs[t % 2], rst[t % 2]
            self.emit_rstd([self.h[:, c, tl] for c in range(KC)], [self.HT[c][t] for c in range(KC)], 512,
                           1.0 / D, EPS, r, rt, sq, sqt, L, Lt)
            for c in range(KC):
                self.V('scalar_tensor_tensor', (self.HT[c][t], rt, self.const_tok), (self.HT[c][t],),
                       out=self.h[:, c, tl], in0=self.h[:, c, tl],
                       scalar=self.sm_col(O_NORM + 12 * 8 + c), in1=r, op0=ALU.mult, op1=ALU.mult)
        for c in range(KC):
            self.h_cnt[c] += 16
            sc.op('sp', ('dma_start', dict(out=self.yT[b, c * P:(c + 1) * P, :], in_=self.h[:, c, :])),
                  tuple(self.HT[c]), (), dma=(self.h_sem[c], self.h_cnt[c]))

    def ffn(self, l, which):
        sc, ar, S, NT = self.sc, self.ar, self.S, self.NT
        pre = "ffn1" if which == 0 else "ffn2"
        wg = self.w[pre + "_w_gate"][l].rearrange("(k p) f -> p k f", p=P)
        wu = self.w[pre + "_w_up"][l].rearrange("(k p) f -> p k f", p=P)
        wd = self.w[pre + "_w_down"][l].rearrange("(j p) d -> p j d", p=P)
        nidx = l * 3 + (0 if which == 0 else 2)
        self.phase()
        hn = ar.alloc([KC, S], BF16)
        act = ar.alloc([4, S], BF16)
        sil = [ar.alloc([512], F32) for _ in range(2)]
        silt = sc.toks2(2)
        sq = [ar.alloc([512], BF16) for _ in range(3)]
        sqt = sc.toks2(3)
        L = ar.alloc([512], F32)
        Lt = sc.tok()
        rs = [ar.alloc([512], F32) for _ in range(2)]
        rst = sc.toks2(2)
        HNT = sc.toks2(KC, NT)
        ACTT = sc.toks2(4, NT)
        for t in range(NT):
            tl = slice(t * 512, (t + 1) * 512)
            r, rt = rs[t % 2], rst[t % 2]
            self.emit_rstd([self.h[:, c, tl] for c in range(KC)], [self.HT[c][t] for c in range(KC)], 512,
                           1.0 / D, EPS, r, rt, sq, sqt, L, Lt)
            for c in range(KC):
                self.V('scalar_tensor_tensor', (self.HT[c][t], rt, self.const_tok), (HNT[c][t],),
                       out=hn[:, c, tl], in0=self.h[:, c, tl],
                       scalar=self.sm_col(O_NORM + nidx * 8 + c), in1=r, op0=ALU.mult, op1=ALU.mult)
        groups = [(0, 4), (4, 4), (8, 4), (12, 4), (16, 4), (20, 2)]
        ca = 0
        cc = 0
        for (j0, G) in groups:
            sa = self.wload(wg[:, :, j0 * P:(j0 + G) * P], self.v_k512(G * P))
            sb_ = self.wload(wu[:, :, j0 * P:(j0 + G) * P], self.v_k512(G * P))
            sd = self.wload(wd[:, j0:j0 + G, :],
                            lambda a, G=G: a.rearrange("p (j d) -> p j d", j=4)[:, 0:G, :])
            wa = self.slot_k512(sa)
            wb = self.slot_k512(sb_)
            wdn = self.slots[sd][:].rearrange("p (j d) -> p j d", j=4)
            for j in range(G):
                for t in range(NT):
                    tl = slice(t * 512, (t + 1) * 512)
                    x = ca % 2
                    ca += 1
                    ba, bb = x, 2 + x
                    ins = []
                    for k in range(KC):
                        ins.append(('matmul', dict(out=self.PS[ba], lhsT=wa[:, k, j * P:(j + 1) * P],
                                                   rhs=hn[:, k, tl], start=(k == 0), stop=(k == KC - 1))))
                    for k in range(KC):
                        ins.append(('matmul', dict(out=self.PS[bb], lhsT=wb[:, k, j * P:(j + 1) * P],
                                                   rhs=hn[:, k, tl], start=(k == 0), stop=(k == KC - 1))))
                    sc.op('pe', ins, [self.slot_tok[sa], self.slot_tok[sb_]] + [HNT[k][t] for k in range(KC)],
                          (self.PT[ba], self.PT[bb]))
                    self.A((self.PT[ba],), (silt[x],), out=sil[x], in_=self.PS[ba], func=AF.Silu)
                    self.V('tensor_tensor', (silt[x], self.PT[bb]), (ACTT[j][t],),
                           out=act[:, j, tl], in0=sil[x], in1=self.PS[bb], op=ALU.mult)
            for t in range(NT):
                tl = slice(t * 512, (t + 1) * 512)
                for c in range(KC):
                    y = 4 + (cc % 2)
                    cc += 1
                    ins = []
                    for j in range(G):
                        ins.append(('matmul', dict(out=self.PS[y], lhsT=wdn[:, j, c * P:(c + 1) * P],
                                                   rhs=act[:, j, tl], start=(j == 0), stop=(j == G - 1))))
                    sc.op('pe', ins, [self.slot_tok[sd]] + [ACTT[j][t] for j in range(G)], (self.PT[y],))
                    self.V('scalar_tensor_tensor', (self.PT[y], self.HT[c][t]), (self.HT[c][t],),
                           out=self.h[:, c, tl], in0=self.PS[y], scalar=0.5, in1=self.h[:, c, tl],
                           op0=ALU.mult, op1=ALU.add)

    def tile_norm(self, l, T0, TT, hn_t, hn_tok, sq, sqt, L, Lt, rs, rst):
        t5 = T0 // 512
        tl = slice(T0, T0 + TT)
        self.emit_rstd([self.h[:, c, tl] for c in range(KC)], [self.HT[c][t5] for c in range(KC)], TT,
                       1.0 / D, EPS, rs[:, 0:TT], rst, sq, sqt, L, Lt)
        for c in range(KC):
            self.V('scalar_tensor_tensor', (self.HT[c][t5], rst, self.const_tok), (hn_tok,),
                   out=hn_t[:, c, :], in0=self.h[:, c, tl],
                   scalar=self.sm_col(O_NORM + (l * 3 + 1) * 8 + c), in1=rs[:, 0:TT], op0=ALU.mult, op1=ALU.mult)

    def proj(self, slot, coff, ncol, hn_t, hn_tok, TT, bank=None):
        b = self.nb() if bank is None else bank
        w, wt = self._w(slot)
        ins = []
        for k in range(KC):
            ins.append(('matmul', dict(out=self.PS[b][0:ncol, 0:TT], lhsT=w[:, k, coff:coff + ncol],
                                       rhs=hn_t[:, k, :], start=(k == 0), stop=(k == KC - 1))))
        self.sc.op('pe', ins, (wt, hn_tok), (self.PT[b],))
        return b

    def out_proj(self, wout, o_or_e, T0, TT, y, ytoks):
        t5 = T0 // 512
        tl = slice(T0, T0 + TT)
        for c in range(KC):
            b = self.nb()
            w = self.slot_k512(wout[c // 4])
            ins = []
            for k in range(KC):
                ins.append(('matmul', dict(out=self.PS[b][:, 0:TT], lhsT=w[:, k, (c % 4) * P:(c % 4 + 1) * P],
                                           rhs=y[:, k, :], start=(k == 0), stop=(k == KC - 1))))
            self.sc.op('pe', ins, [self.slot_tok[wout[c // 4]]] + list(ytoks), (self.PT[b],))
            self.V('tensor_tensor', (self.PT[b], self.HT[c][t5]), (self.HT[c][t5],),
                   out=self.h[:, c, tl], in0=self.PS[b][:, 0:TT], in1=self.h[:, c, tl], op=ALU.add)

    def gelu2(self, b, n, out, out_tok, tA, tAt, tB, tBt):
        ps = self.PS[b][:, 0:n]
        pt = self.PT[b]
        self.A((pt,), (tAt,), out=tA, in_=ps, func=AF.Square)
        self.V('tensor_scalar', (tAt,), (tAt,), out=tA, in0=tA, scalar1=0.044715, scalar2=1.0,
               op0=ALU.mult, op1=ALU.add)
        self.V('tensor_tensor', (tAt, pt), (tBt,), out=tB, in0=tA, in1=ps, op=ALU.mult)
        self.A((tBt,), (tBt,), out=tB, in_=tB, func=AF.Tanh, scale=GELU_C)
        self.V('scalar_tensor_tensor', (tBt, pt), (out_tok,), out=out, in0=tB, scalar=1.0, in1=ps,
               op0=ALU.add, op1=ALU.mult)

    def mixer_cd(self, l):
        sc, ar, S = self.sc, self.ar, self.S
        o = l // 2
        TT = 512
        win = self.w["cd_w_in"][o].rearrange("(k p) f -> p k f", p=P)
        wout = self.w["cd_w_out"][o].rearrange("(k p) f -> p k f", p=P)
        self.phase()
        self.nb_range = (0, 5)
        tk = sc.tok
        biasB = ar.alloc([4, P], F32)
        wnat = ar.alloc([4, P], F32)
        wsT = ar.alloc([4, P], BF16)
        prm_tok = tk()
        self.misc_cnt += 16
        sc.op('sp', ('dma_start', dict(out=biasB.rearrange("p a b -> p (a b)"), in_=self.d_sgub[o])),
              (), (prm_tok,), dma=(self.misc_sem, self.misc_cnt))
        wn_tok = tk()
        self.misc2_cnt += 16
        sc.op('sp', ('dma_start', dict(out=wnat.rearrange("p a b -> p (a b)"), in_=self.d_sguw[o])),
              (), (wn_tok,), dma=(self.misc2_sem, self.misc2_cnt))
        wsT_tok = tk()
        bt = self.nb()
        for hh in range(4):
            self.V('tensor_tensor', (wn_tok, self.const_tok), (wn_tok,), out=wnat[:, hh, :], in0=wnat[:, hh, :],
                   in1=self.sm[:, O_TRIL:O_TRIL + P], op=ALU.mult)
            sc.op('pe', ('transpose', dict(out=self.PS[bt][:, hh * P:(hh + 1) * P], in_=wnat[:, hh, :],
                                           identity=self.sm[:, O_IDENT:O_IDENT + P])),
                  (wn_tok, self.const_tok), (self.PT[bt],))
        self.A((self.PT[bt],), (wsT_tok,), out=wsT.rearrange("p a b -> p (a b)"), in_=self.PS[bt], func=AF.Copy)
        xc = ar.alloc([4, TT + 2], F32)
        xct = sc.toks2(4)
        for ch in range(4):
            self.V('memset', (), (xct[ch],), ap=xc[:, ch, 0:2], constant=0.0)
        hn_t = ar.alloc([KC, TT], BF16)
        hn_tok = tk()
        sq = [ar.alloc([TT], BF16) for _ in range(3)]
        sqt = sc.toks2(3)
        L = ar.alloc([TT], F32)
        Lt = tk()
        rs = ar.alloc([TT], F32)
        rst = tk()
        vg = [ar.alloc([TT], F32) for _ in range(4)]
        vgt = sc.toks2(4)
        tA = [ar.alloc([TT], F32) for _ in range(2)]
        tAt = sc.toks2(2)
        tB = [ar.alloc([TT], F32) for _ in range(2)]
        tBt = sc.toks2(2)
        mean = ar.alloc([TT], F32)
        mean_t = tk()
        m2 = ar.alloc([TT], F32)
        m2_t = tk()
        rstd = ar.alloc([TT], F32)
        rstd_t = tk()
        vtok = ar.alloc([4, 512], BF16)
        vtok_t = sc.toks2(4)
        ug = [ar.alloc([TT], F32) for _ in range(2)]
        ugt = sc.toks2(2)
        tmp = [ar.alloc([TT], F32) for _ in range(2)]
        tmpt = sc.toks2(2)
        acc = [ar.alloc([TT], F32) for _ in range(2)]
        acct = sc.toks2(2)
        y = ar.alloc([KC, TT], BF16)
        yt = sc.toks2(KC)
        gi = 0
        for tau in range(S // TT):
            T0 = tau * TT
            self.tile_norm(l, T0, TT, hn_t, hn_tok, sq, sqt, L, Lt, rs, rst)
            wi = [self.wload(win[:, :, g * 512:(g + 1) * 512], self.v_k512(512)) for g in range(5)]

            def pj(m):
                return self.proj(wi[m // 4], (m % 4) * P, P, hn_t, hn_tok, TT)
            s1, s2 = 5, 6
            for hh in range(4):
                b = pj(4 + hh)
                x = gi % 2
                gi += 1
                self.gelu2(b, TT, vg[hh], vgt[hh], tA[x], tAt[x], tB[x], tBt[x])
                self.A((vgt[hh],), (tAt[x],), out=tA[x], in_=vg[hh], func=AF.Square)
                sc.op('pe', ('matmul', dict(out=self.PS[s1], lhsT=self.ones32, rhs=vg[hh], start=(hh == 0),
                                            stop=(hh == 3))), (vgt[hh], self.const_tok), (self.PT[s1],))
                sc.op('pe', ('matmul', dict(out=self.PS[s2], lhsT=self.ones32, rhs=tA[x], start=(hh == 0),
                                            stop=(hh == 3))), (tAt[x], self.const_tok), (self.PT[s2],))
            self.dbg("hn", hn_t, (hn_tok,))
            self.dbg("vg0_pre", vg[0], (vgt[0],))
            self.A((self.PT[s1],), (mean_t,), out=mean, in_=self.PS[s1], func=AF.Identity, scale=1.0 / 512)
            self.dbg("mean", mean, (mean_t,))
            self.V('tensor_tensor', (mean_t,), (m2_t,), out=m2, in0=mean, in1=mean, op=ALU.mult)
            self.V('scalar_tensor_tensor', (self.PT[s2], m2_t), (m2_t,), out=m2, in0=self.PS[s2],
                   scalar=1.0 / 512, in1=m2, op0=ALU.mult, op1=ALU.subtract)
            self.A((m2_t,), (m2_t,), out=m2, in_=m2, func=AF.Ln, scale=1.0, bias=4.0 * EPS)
            self.A((m2_t,), (rstd_t,), out=rstd, in_=m2, func=AF.Exp, scale=-0.5)
            for hh in range(4):
                self.V('tensor_tensor', (vgt[hh], mean_t), (vgt[hh],), out=vg[hh], in0=vg[hh], in1=mean,
                       op=ALU.subtract)
                self.V('tensor_tensor', (vgt[hh], rstd_t), (vgt[hh],), out=vg[hh], in0=vg[hh], in1=rstd,
                       op=ALU.mult)
                self.V('tensor_scalar', (vgt[hh], self.const_tok), (vgt[hh],), out=vg[hh], in0=vg[hh],
                       scalar1=self.sm_col(O_SGG + o * 4 + hh), scalar2=self.sm_col(O_SGB + o * 4 + hh),
                       op0=ALU.mult, op1=ALU.add)
            self.dbg("rstd", rstd, (rstd_t,))
            self.dbg("vg0_ln", vg[0], (vgt[0],))
            self.dbg("wsT", wsT, (wsT_tok,))
            for blk in range(4):
                bt = self.nb()
                for hh in range(4):
                    sc.op('pe', ('transpose', dict(out=self.PS[bt][:, hh * P:(hh + 1) * P],
                                                   in_=vg[hh][:, blk * P:(blk + 1) * P],
                                                   identity=self.sm[:, O_IDENT:O_IDENT + P])),
                          (vgt[hh], self.const_tok), (self.PT[bt],))
                self.A((self.PT[bt],), (vtok_t[blk],), out=vtok[:, blk, :], in_=self.PS[bt], func=AF.Copy)
            for hh in range(4):
                b = pj(hh)
                x = gi % 2
                gi += 1
                self.gelu2(b, TT, ug[x], ugt[x], tA[x], tAt[x], tB[x], tBt[x])
                bm = self.nb()
                for blk in range(4):
                    sc.op('pe', ('matmul', dict(out=self.PS[bm][:, blk * P:(blk + 1) * P],
                                                lhsT=vtok[:, blk, hh * P:(hh + 1) * P], rhs=wsT[:, hh, :],
                                                start=True, stop=True)),
                          (vtok_t[blk], wsT_tok), (self.PT[bm],))
                self.V('tensor_tensor', (self.PT[bm], prm_tok), (tmpt[x],),
                       out=tmp[x].rearrange("p (a b) -> p a b", a=4),
                       in0=self.PS[bm].rearrange("p (a b) -> p a b", a=4),
                       in1=biasB[:, hh:hh + 1, :].to_broadcast([P, 4, P]), op=ALU.add)
                self.dbg("vtok", vtok, vtok_t)
                self.dbg("tmp0", tmp[x], (tmpt[x],))
                self.dbg("ug0", ug[x], (ugt[x],))
                self.V('scalar_tensor_tensor', (tmpt[x], ugt[x]), (yt[hh],), out=y[:, hh, :], in0=tmp[x],
                       scalar=0.5, in1=ug[x], op0=ALU.mult, op1=ALU.mult)
            for ch in range(4):
                bx = pj(8 + ch)
                bg_ = pj(12 + ch)
                bc = pj(16 + ch)
                x = gi % 2
                gi += 1
                self.A((self.PT[bx],), (tmpt[x],), out=tmp[x], in_=self.PS[bx], func=AF.Copy)
                self.V('tensor_tensor', (self.PT[bc], tmpt[x]), (xct[ch],), out=xc[:, ch, 2:TT + 2],
                       in0=self.PS[bc], in1=tmp[x], op=ALU.mult)
                wcol = lambda i: self.sm_col(O_SCW + (o * 4 + ch) * 3 + i)
                self.V('tensor_scalar', (xct[ch], self.const_tok), (acct[x],), out=acc[x], in0=xc[:, ch, 0:TT],
                       scalar1=wcol(0), scalar2=None, op0=ALU.mult)
                for i in (1, 2):
                    self.V('scalar_tensor_tensor', (xct[ch], acct[x], self.const_tok), (acct[x],), out=acc[x],
                           in0=xc[:, ch, i:TT + i], scalar=wcol(i), in1=acc[x], op0=ALU.mult, op1=ALU.add)
                self.V('tensor_tensor', (acct[x], self.PT[bg_]), (yt[4 + ch],), out=y[:, 4 + ch, :], in0=acc[x],
                       in1=self.PS[bg_], op=ALU.mult)
                self.A((xct[ch],), (xct[ch],), out=xc[:, ch, 0:2], in_=xc[:, ch, TT:TT + 2], func=AF.Copy)
            self.dbg("y", y, yt)
            wo = [self.wload(wout[:, :, g * 512:(g + 1) * 512], self.v_k512(512)) for g in range(2)]
            self.out_proj(wo, o, T0, TT, y, yt)
        self.nb_range = (0, 7)

    def mixer_ab(self, l):
        sc, ar, S = self.sc, self.ar, self.S
        e = l // 2
        TT = 256
        NCK = 2
        IL = self.IL
        NGB = 8 - IL
        self.nb_range = (0, NGB)
        win = self.w["ab_w_in"][e].rearrange("(k p) f -> p k f", p=P)
        wout = self.w["ab_w_out"][e].rearrange("(k p) f -> p k f", p=P)
        self.phase(pool=True)
        tk = sc.tok
        ident = self.sm[:, O_IDENT:O_IDENT + P]
        f32 = lambda *s: ar.alloc(list(s), F32)
        b16 = lambda *s: ar.alloc(list(s), BF16)
        bg_full = b16(KC, 16)
        bg_ar = bg_full[:, :, 0:8]
        w_bg = (bg_ar, self.wres([win[:, :, 2560:2568]], [bg_ar]))

        class Pl:
            def __init__(s_, shape, dt, n):
                s_.free = [(ar.alloc(shape, dt), sc.tok()) for _ in range(n)]
                s_.n = n
                s_.low = n

            def get(s_):
                assert s_.free, "pool empty"
                it = s_.free.pop(0)
                s_.low = min(s_.low, len(s_.free))
                return it

            def put(s_, *items):
                for it in items:
                    s_.free.append(it)
        S32 = f32(4, P)
        S16 = b16(4, P)
        S32t = sc.toks2(4)
        S16t = sc.toks2(4)
        qcar = b16(12, 4)
        qcart = sc.toks2(12)
        DG = b16(48, P)
        DGt = tk()
        for ci in range(12):
            for i in range(4):
                self.V('tensor_scalar', (self.const_tok,), (DGt,), out=DG[:, ci * 4 + i, :], in0=self.ident_bf,
                       scalar1=self.sm_col(O_DNW + (e * 12 + ci) * 4 + i), scalar2=None, op0=ALU.mult)
        pcar = f32(4, 16)
        pcart = sc.toks2(4)
        self.V('memset', (), tuple(S32t), ap=S32, constant=0.0)
        self.V('memset', (), tuple(S16t), ap=S16, constant=0.0)
        self.V('memset', (), tuple(qcart), ap=qcar, constant=0.0)
        self.V('memset', (), tuple(pcart), ap=pcar, constant=0.0)
        nA = f32(1)
        nAt = tk()
        self.A((self.const_tok,), (nAt,), out=nA, in_=self.sm_col(O_ALOG + e), func=AF.Exp)
        self.V('tensor_scalar', (nAt,), (nAt,), out=nA, in0=nA, scalar1=-1.0, scalar2=None, op0=ALU.mult)
        hn_t = b16(KC, TT)
        hn_tok = tk()
        rs = f32(TT)
        rst = tk()
        X8, X8t = f32(TT), tk()
        W8, W8t = f32(TT), tk()
        G8, G8t = f32(TT), tk()
        GC8, GC8t = f32(TT), tk()
        abuf, abuft = f32(TT + 16), tk()
        sA, sAt = f32(TT + 16), tk()
        sB, sBt = f32(TT + 16), tk()
        pooled, pooledt = b16(TT), tk()
        t16, t16t = f32(16), tk()
        LASTC = f32(4, NCK)
        LASTCt = sc.toks2(4)
        gcol = f32(4, NCK)
        gcolt = sc.toks2(4)
        junk = [b16(P)] * 4
        junkt = [tk()] * 4
        y = b16(KC, TT)
        yt = sc.toks2(KC)
        FP = Pl([264], F32, 7 * IL + 2)
        BP = Pl([TT], BF16, 6 * IL + 3)
        HP = Pl([P], F32, 3)
        QP = Pl([P], BF16, 2 * IL + 2)
        RP = Pl([264], BF16, IL + 2)

        def rstd(src, stok, n, inv_count, out_ap, out_tok):
            q = BP.get()
            Lb = FP.get()
            self.emit_rstd([src], [stok], n, inv_count, EPS, out_ap, out_tok, [q[0]], [q[1]], Lb[0], Lb[1])
            BP.put(q)
            FP.put(Lb)
        v2 = lambda a: a.rearrange("p (a b) -> p a b", a=NCK)
        r8 = lambda a: a[0:8, 0:TT]

        for tau in range(S // TT):
            T0 = tau * TT
            q0 = BP.get()
            q1 = BP.get()
            q2 = BP.get()
            Lb = FP.get()
            self.tile_norm(l, T0, TT, hn_t, hn_tok, [q0[0], q1[0], q2[0]], [q0[1], q1[1], q2[1]], Lb[0], Lb[1],
                           rs, rst)
            BP.put(q0, q1, q2)
            FP.put(Lb)
            w_a = self.wload(win[:, :, 0:512], self.v_k512(512))
            w_q = self.wload(win[:, :, 512:1024], self.v_k512(512))
            w_k = self.wload(win[:, :, 1024:1536], self.v_k512(512))
            w_v = self.wload(win[:, :, 1536:2048], self.v_k512(512))
            w_z = self.wload(win[:, :, 2048:2560], self.v_k512(512))
            b = self.proj(w_bg, 0, 8, hn_t, hn_tok, TT)
            self.V('tensor_scalar', (self.PT[b], self.const_tok), (X8t,), out=r8(X8), in0=self.PS[b][0:8, 0:TT],
                   scalar1=self.sm[0:8, O_DTB + e:O_DTB + e + 1], scalar2=None, op0=ALU.add)
            self.A((X8t,), (W8t,), out=r8(W8), in_=r8(X8), func=AF.Abs)
            self.A((W8t,), (W8t,), out=r8(W8), in_=r8(W8), func=AF.Exp, scale=-1.0)
            self.A((W8t,), (W8t,), out=r8(W8), in_=r8(W8), func=AF.Ln, scale=1.0, bias=1.0)
            self.V('scalar_tensor_tensor', (X8t, W8t), (W8t,), out=r8(W8), in0=r8(X8), scalar=0.0, in1=r8(W8),
                   op0=ALU.max, op1=ALU.add)
            self.V('tensor_tensor', (X8t, W8t), (X8t,), out=r8(X8), in0=r8(X8), in1=r8(W8), op=ALU.subtract)
            self.A((X8t,), (X8t,), out=r8(X8), in_=r8(X8), func=AF.Exp)
            self.V('tensor_scalar', (W8t, nAt), (G8t,), out=r8(G8), in0=r8(W8), scalar1=nA[0:8, 0:1], scalar2=None,
                   op0=ALU.mult)
            self.V('tensor_tensor_scan', (G8t, self.const_tok), (GC8t,), out=r8(GC8),
                   data0=self.sm[0:8, O_RESET:O_RESET + TT], data1=r8(G8), initial=0.0, op0=ALU.mult, op1=ALU.add)

            def pool_gen():
                for g in range(4):
                    wdw = POOL_WINDOWS[g]
                    bp = self.proj(w_a, g * P, P, hn_t, hn_tok, TT)
                    self.A((pcart[g],), (abuft,), out=abuf[:, 0:16], in_=pcar[:, g, :], func=AF.Copy)
                    self.A((self.PT[bp],), (abuft,), out=abuf[:, 16:16 + TT], in_=self.PS[bp][:, 0:TT],
                           func=AF.Copy)
                    cur, curt, lo = abuf, abuft, 0
                    W_ = TT + 16
                    for k in range(g + 1):
                        sh = 1 << k
                        nxt, nxtt = (sA, sAt) if k % 2 == 0 else (sB, sBt)
                        self.V('tensor_tensor', (curt,), (nxtt,), out=nxt[:, lo + sh:W_], in0=cur[:, lo + sh:W_],
                               in1=cur[:, lo:W_ - sh], op=ALU.add)
                        cur, curt, lo = nxt, nxtt, lo + sh
                    self.V('scalar_tensor_tensor', (curt, abuft), (pooledt,), out=pooled, in0=cur[:, 16:W_],
                           scalar=1.0 / wdw, in1=abuf[:, 16:W_], op0=ALU.mult, op1=ALU.subtract)
                    if tau == 0:
                        self.V('tensor_tensor', (curt, self.const_tok), (t16t,), out=t16, in0=cur[:, 16:32],
                               in1=self.sm[:, O_INVC + g * 16:O_INVC + (g + 1) * 16], op=ALU.mult)
                        self.V('tensor_tensor', (t16t, abuft, pooledt), (pooledt,), out=pooled[:, 0:16], in0=t16,
                               in1=abuf[:, 16:32], op=ALU.subtract)
                    self.A((abuft,), (pcart[g],), out=pcar[:, g, :], in_=abuf[:, TT:TT + 16], func=AF.Copy)
                    bq = self.nb()
                    sc.op('pe', ('matmul', dict(out=self.PS[bq][:, 0:TT],
                                                lhsT=self.poolw[:, (e * 4 + g) * P:(e * 4 + g + 1) * P], rhs=pooled,
                                                start=True, stop=True)), (pooledt, self.pw_tok), (self.PT[bq],))
                    self.A((self.PT[bq], self.const_tok), (yt[g],), out=y[:, g, :], in_=self.PS[bq][:, 0:TT],
                           func=AF.Identity, scale=self.sm_col(O_PSC + e * 4 + g))
                    yield

            def head(hd, CH):
                XS = []
                for idx, wsl in enumerate((w_q, w_k, w_v)):
                    ci = idx * 4 + hd
                    raw, rawt = RP.get()
                    xs = FP.get()
                    bp = self.proj(wsl, hd * P, P, hn_t, hn_tok, TT)
                    self.A((qcart[ci],), (rawt,), out=raw[:, 0:3], in_=qcar[:, ci, 0:3], func=AF.Copy)
                    self.A((self.PT[bp],), (rawt,), out=raw[:, 3:3 + TT], in_=self.PS[bp][:, 0:TT], func=AF.Copy)
                    bc = self.nb()
                    sc.op('pe', [('matmul', dict(out=self.PS[bc][:, 0:TT], lhsT=DG[:, ci * 4 + i, :],
                                                 rhs=raw[:, i:TT + i], start=(i == 0), stop=(i == 3)))
                                 for i in range(4)], (rawt, DGt), (self.PT[bc],))
                    self.A((rawt,), (qcart[ci],), out=qcar[:, ci, 0:3], in_=raw[:, TT:TT + 3], func=AF.Copy)
                    self.A((self.PT[bc],), (xs[1],), out=xs[0][:, 0:TT], in_=self.PS[bc][:, 0:TT], func=AF.Silu)
                    RP.put((raw, rawt))
                    XS.append(xs)
                    yield
                RQ = FP.get()
                RK = FP.get()
                rstd(XS[0][0][:, 0:TT], XS[0][1], TT, 1.0, RQ[0][:, 0:TT], RQ[1])
                rstd(XS[1][0][:, 0:TT], XS[1][1], TT, 1.0, RK[0][:, 0:TT], RK[1])
                yield
                KN, KNt = BP.get()
                QN, QNt = BP.get()
                self.V('tensor_tensor', (XS[1][1], RK[1]), (KNt,), out=KN, in0=XS[1][0][:, 0:TT], in1=RK[0][:, 0:TT],
                       op=ALU.mult)
                self.V('scalar_tensor_tensor', (XS[0][1], RQ[1]), (QNt,), out=QN, in0=XS[0][0][:, 0:TT],
                       scalar=float(P ** -0.5), in1=RQ[0][:, 0:TT], op0=ALU.mult, op1=ALU.mult)
                FP.put(RQ, RK, XS[0], XS[1])
                yield
                GS, GSt = FP.get()
                BS, BSt = FP.get()
                self.V('tensor_scalar', (GC8t, self.const_tok), (GSt,), out=r8(GS), in0=r8(GC8),
                       scalar1=self.sm[0:8, O_IDENT + 4 + hd:O_IDENT + 5 + hd], scalar2=None, op0=ALU.mult)
                self.V('tensor_scalar', (X8t, self.const_tok), (BSt,), out=r8(BS), in0=r8(X8),
                       scalar1=self.sm[0:8, O_IDENT + hd:O_IDENT + hd + 1], scalar2=None, op0=ALU.mult)
                b1 = self.nb()
                sc.op('pe', [('matmul', dict(out=self.PS[b1][:, 0:TT], lhsT=self.ones32[0:8, :], rhs=r8(GS),
                                             start=True, stop=True)),
                             ('matmul', dict(out=self.PS[b1][:, TT:2 * TT], lhsT=self.ones32[0:8, :], rhs=r8(BS),
                                             start=True, stop=True))],
                      (GSt, BSt, self.const_tok), (self.PT[b1],))
                FP.put((GS, GSt), (BS, BSt))
                GCBb = FP.get()
                EGCb = FP.get()
                BETABb = FP.get()
                GCB, GCBt = GCBb[0][:, 0:TT], GCBb[1]
                EGC, EGCt = EGCb[0][:, 0:TT], EGCb[1]
                BETAB, BETABt = BETABb[0][:, 0:TT], BETABb[1]
                self.A((self.PT[b1],), (GCBt,), out=GCB, in_=self.PS[b1][:, 0:TT], func=AF.Copy)
                self.A((self.PT[b1],), (EGCt,), out=EGC, in_=self.PS[b1][:, 0:TT], func=AF.Exp)
                self.A((self.PT[b1],), (BETABt,), out=BETAB, in_=self.PS[b1][:, TT:2 * TT], func=AF.Copy)
                yield
                DLb = FP.get()
                BEb = FP.get()
                BSTb = FP.get()
                DL, DLt = DLb[0][:, 0:TT], DLb[1]
                BE, BEt = BEb[0][:, 0:TT], BEb[1]
                BST, BSTt = BSTb[0][:, 0:TT], BSTb[1]
                g3 = v2(GCB)
                self.V('tensor_tensor', (GCBt,), (DLt,), out=v2(DL),
                       in0=g3[:, :, P - 1:P].to_broadcast([P, NCK, P]), in1=g3, op=ALU.subtract)
                self.A((DLt,), (DLt,), out=DL, in_=DL, func=AF.Exp)
                self.V('tensor_copy', (EGCt,), (LASTCt[hd],), out=LASTC[:, hd, :], in_=EGC[:, P - 1:TT:P])
                self.V('tensor_tensor', (BETABt, EGCt), (BEt,), out=BE, in0=BETAB, in1=EGC, op=ALU.mult)
                self.V('tensor_tensor', (BETABt, self.const_tok), (BSTt,), out=v2(BST), in0=v2(BETAB),
                       in1=self.sm[:, O_STRICT:O_STRICT + P].unsqueeze(1).to_broadcast([P, NCK, P]), op=ALU.mult)
                yield
                KBG, KBGt = BP.get()
                QDEC, QDECt = BP.get()
                KDTb = FP.get()
                VBTb = FP.get()
                KDT, KDTt = KDTb[0][:, 0:TT], KDTb[1]
                VBT, VBTt = VBTb[0][:, 0:TT], VBTb[1]
                self.V('tensor_tensor', (KNt, BEt), (KBGt,), out=KBG, in0=KN, in1=BE, op=ALU.mult)
                self.V('tensor_tensor', (QNt, EGCt), (QDECt,), out=QDEC, in0=QN, in1=EGC, op=ALU.mult)
                self.V('tensor_tensor', (KNt, DLt), (KDTt,), out=KDT, in0=KN, in1=DL, op=ALU.mult)
                self.V('tensor_tensor', (XS[2][1], BETABt), (VBTt,), out=VBT, in0=XS[2][0][:, 0:TT], in1=BETAB,
                       op=ALU.mult)
                FP.put(BEb, DLb, XS[2], EGCb, BETABb)
                yield
                btp = self.nb()
                ins = []
                for ck in range(NCK):
                    ins.append(('transpose', dict(out=self.PS[btp][:, ck * P:(ck + 1) * P],
                                                  in_=KDT[:, ck * P:(ck + 1) * P], identity=ident)))
                for ck in range(NCK):
                    ins.append(('transpose', dict(out=self.PS[btp][:, (NCK + ck) * P:(NCK + ck + 1) * P],
                                                  in_=VBT[:, ck * P:(ck + 1) * P], identity=ident)))
                sc.op('pe', ins, (KDTt, VBTt, self.const_tok), (self.PT[btp],))
                KDECb = BP.get()
                VBb = FP.get()
                KDEC, KDECt = KDECb[0], KDECb[1]
                VB, VBt = VBb[0][:, 0:TT], VBb[1]
                self.A((self.PT[btp],), (KDECt,), out=KDEC, in_=self.PS[btp][:, 0:NCK * P], func=AF.Copy)
                self.A((self.PT[btp],), (VBt,), out=VB, in_=self.PS[btp][:, NCK * P:2 * NCK * P], func=AF.Copy)
                FP.put(KDTb, VBTb)
                yield
                ATb = BP.get()
                AT, ATt = v2(ATb[0]), ATb[1]
                N0b = FP.get()
                Qb = FP.get()
                M0b = FP.get()
                Nb = [v2(N0b[0][:, 0:TT]), None]
                Nbt = [N0b[1], None]
                Mb = [v2(M0b[0][:, 0:TT]), None]
                Mbt = [M0b[1], None]
                Q, Qt = v2(Qb[0][:, 0:TT]), Qb[1]
                for ck in range(NCK):
                    cs = slice(ck * P, (ck + 1) * P)
                    E1b = HP.get()
                    GTBb = HP.get()
                    E1, E1t = E1b
                    GTB, GTBt = GTBb
                    self.V('scalar_tensor_tensor', (GCBt, self.const_tok), (junkt[hd], gcolt[hd]), out=junk[hd],
                           in0=GCB[:, cs], scalar=1.0, in1=ident, op0=ALU.mult, op1=ALU.mult,
                           accum_out=gcol[:, hd, ck:ck + 1])
                    self.V('scalar_tensor_tensor', (GCBt, gcolt[hd], self.const_tok), (E1t,), out=E1,
                           in0=GCB[:, cs], scalar=gcol[:, hd, ck:ck + 1], in1=self.sm[:, O_MASKA:O_MASKA + P],
                           op0=ALU.subtract, op1=ALU.add)
                    self.A((E1t,), (E1t,), out=E1, in_=E1, func=AF.Exp)
                    b2 = self.nb()
                    sc.op('pe', [('matmul', dict(out=self.PS[b2][:, 0:P], lhsT=KN[:, cs], rhs=KN[:, cs],
                                                 start=True, stop=True)),
                                 ('matmul', dict(out=self.PS[b2][:, P:2 * P], lhsT=KN[:, cs], rhs=QN[:, cs],
                                                 start=True, stop=True))],
                          (KNt, QNt), (self.PT[b2],))
                    self.V('tensor_tensor', (self.PT[b2], E1t), (ATt,), out=AT[:, ck, :],
                           in0=self.PS[b2][:, P:2 * P], in1=E1, op=ALU.mult)
                    self.V('tensor_tensor', (E1t, BSTt), (GTBt,), out=GTB, in0=E1, in1=BST[:, cs], op=ALU.mult)
                    self.V('tensor_tensor', (self.PT[b2], GTBt), (Nbt[0],), out=Nb[0][:, ck, :],
                           in0=self.PS[b2][:, 0:P], in1=GTB, op=ALU.mult)
                    self.V('scalar_tensor_tensor', (Nbt[0], self.const_tok), (Qt,), out=Q[:, ck, :],
                           in0=Nb[0][:, ck, :], scalar=-1.0, in1=ident, op0=ALU.mult, op1=ALU.add)
                    HP.put(E1b, GTBb)
                    yield
                bl = self.nb()
                sc.op('pe', [('transpose', dict(out=self.PS[bl][:, ck * P:(ck + 1) * P], in_=Nb[0][:, ck, :],
                                                identity=ident)) for ck in range(NCK)],
                      (Nbt[0], self.const_tok), (self.PT[bl],))
                self.A((self.PT[bl],), (Mbt[0],), out=Mb[0].rearrange("p a b -> p (a b)"),
                       in_=self.PS[bl][:, 0:NCK * P], func=AF.Copy)
                FP.put(GCBb, BSTb)
                BP.put((KN, KNt), (QN, QNt))
                M1b = FP.get()
                N1b = FP.get()
                Mb[1], Mbt[1] = v2(M1b[0][:, 0:TT]), M1b[1]
                Nb[1], Nbt[1] = v2(N1b[0][:, 0:TT]), N1b[1]
                TTb = BP.get()
                TTm, TTt = v2(TTb[0]), TTb[1]
                yield
                fl = lambda a: a.rearrange("p a b -> p (a b)")
                for k in range(6):
                    c0, c1 = k % 2, (k + 1) % 2
                    pm = self.nb()
                    sc.op('pe', [('matmul', dict(out=self.PS[pm][:, ck * P:(ck + 1) * P], lhsT=Nb[c0][:, ck, :],
                                                 rhs=Mb[c0][:, ck, :], start=True, stop=True))
                                 for ck in range(NCK)], (Nbt[c0], Mbt[c0]), (self.PT[pm],))
                    if k < 5:
                        pn = self.nb()
                        sc.op('pe', [('matmul', dict(out=self.PS[pn][:, ck * P:(ck + 1) * P], lhsT=Mb[c0][:, ck, :],
                                                     rhs=Nb[c0][:, ck, :], start=True, stop=True))
                                     for ck in range(NCK)], (Nbt[c0], Mbt[c0]), (self.PT[pn],))
                    self.A((self.PT[pm],), (Mbt[c1],), out=fl(Mb[c1]), in_=self.PS[pm][:, 0:NCK * P], func=AF.Copy)
                    if k < 5:
                        self.V('tensor_copy', (self.PT[pn],), (Nbt[c1],), out=fl(Nb[c1]),
                               in_=self.PS[pn][:, 0:NCK * P])
                    yield
                    pq = self.nb()
                    sc.op('pe', [('matmul', dict(out=self.PS[pq][:, ck * P:(ck + 1) * P], lhsT=Mb[c1][:, ck, :],
                                                 rhs=Q[:, ck, :], start=True, stop=True))
                                 for ck in range(NCK)], (Mbt[c1], Qt), (self.PT[pq],))
                    if k < 5:
                        self.V('tensor_tensor', (self.PT[pq], Qt), (Qt,), out=fl(Q), in0=fl(Q),
                               in1=self.PS[pq][:, 0:NCK * P], op=ALU.add)
                    else:
                        self.V('tensor_tensor', (self.PT[pq], Qt), (TTt,), out=fl(TTm), in0=fl(Q),
                               in1=self.PS[pq][:, 0:NCK * P], op=ALU.add)
                    yield
                FP.put(N0b, M0b, M1b, N1b, Qb)
                OSb = FP.get()
                OS, OSt = OSb[0][:, 0:TT], OSb[1]
                for ck in range(NCK):
                    cs = slice(ck * P, (ck + 1) * P)
                    R, Rt = QP.get()
                    VN, VNt = QP.get()
                    sc.op('pe', ('matmul', dict(out=self.PS[CH][:, 0:P], lhsT=KBG[:, cs], rhs=S16[:, hd, :],
                                                start=True, stop=True)), (KBGt, S16t[hd]), (self.PT[CH],))
                    self.V('tensor_tensor', (VBt, self.PT[CH]), (Rt,), out=R, in0=v2(VB)[:, ck, :],
                           in1=self.PS[CH][:, 0:P], op=ALU.subtract)
                    yield
                    sc.op('pe', ('matmul', dict(out=self.PS[CH][:, P:2 * P], lhsT=TTm[:, ck, :], rhs=R,
                                                start=True, stop=True)), (TTt, Rt), (self.PT[CH],))
                    self.A((self.PT[CH],), (VNt,), out=VN, in_=self.PS[CH][:, P:2 * P], func=AF.Copy)
                    yield
                    sc.op('pe', [('matmul', dict(out=self.PS[CH][:, 2 * P:3 * P], lhsT=S16[:, hd, :],
                                                 rhs=QDEC[:, cs], start=True, stop=False)),
                                 ('matmul', dict(out=self.PS[CH][:, 2 * P:3 * P], lhsT=VN, rhs=AT[:, ck, :],
                                                 start=False, stop=True)),
                                 ('matmul', dict(out=self.PS[CH][:, 3 * P:4 * P], lhsT=v2(KDEC)[:, ck, :], rhs=VN,
                                                 start=True, stop=True))],
                          (S16t[hd], QDECt, VNt, ATt, KDECt), (self.PT[CH],))
                    self.V('scalar_tensor_tensor', (S32t[hd], LASTCt[hd], self.PT[CH]), (S32t[hd],),
                           out=S32[:, hd, :], in0=S32[:, hd, :], scalar=LASTC[:, hd, ck:ck + 1],
                           in1=self.PS[CH][:, 3 * P:4 * P], op0=ALU.mult, op1=ALU.add)
                    self.V('tensor_copy', (self.PT[CH],), (OSt,), out=OS[:, cs], in_=self.PS[CH][:, 2 * P:3 * P])
                    self.A((S32t[hd],), (S16t[hd],), out=S16[:, hd, :], in_=S32[:, hd, :], func=AF.Copy)
                    QP.put((R, Rt), (VN, VNt))
                    yield
                BP.put((KBG, KBGt), (QDEC, QDECt), KDECb, ATb, TTb)
                FP.put(VBb)
                ROb = FP.get()
                RO, ROt = ROb[0][:, 0:TT], ROb[1]
                rstd(OS, OSt, TT, 1.0 / P, RO, ROt)
                ZSb = FP.get()
                ZS, ZSt = ZSb[0][:, 0:TT], ZSb[1]
                bz = self.proj(w_z, hd * P, P, hn_t, hn_tok, TT)
                self.A((self.PT[bz],), (ZSt,), out=ZS, in_=self.PS[bz][:, 0:TT], func=AF.Silu)
                yield
                Tmb = FP.get()
                Tm, Tmt = Tmb[0][:, 0:TT], Tmb[1]
                self.V('scalar_tensor_tensor', (OSt, ROt, self.const_tok), (Tmt,), out=Tm, in0=OS,
                       scalar=self.sm_col(O_ONORM + e), in1=RO, op0=ALU.mult, op1=ALU.mult)
                self.V('tensor_tensor', (Tmt, ZSt), (yt[4 + hd],), out=y[:, 4 + hd, :], in0=Tm, in1=ZS, op=ALU.mult)
                FP.put(OSb, ROb, ZSb, Tmb)

            for h0 in range(0, 4, IL):
                gens = [head(h0 + i, NGB + i) for i in range(IL)]
                if h0 == 0:
                    gens.append(pool_gen())
                while gens:
                    for g_ in list(gens):
                        try:
                            next(g_)
                        except StopIteration:
                            gens.remove(g_)
            wo = [self.wload(wout[:, :, g * 512:(g + 1) * 512], self.v_k512(512)) for g in range(2)]
            self.out_proj(wo, e, T0, TT, y, yt)
        self.pool_low = (FP.low, BP.low, HP.low, QP.low)
        self.nb_range = (0, 7)


def make_smalls(inp):
    sm = np.zeros((P, NSM), np.float32)
    norms = np.concatenate([np.stack([inp['ffn1_norm'][l], inp['mix_norm'][l], inp['ffn2_norm'][l]]) for l in range(4)]
                           + [inp['final_norm'][None]], 0)
    sm[:, O_NORM:O_NORM + 104] = norms.reshape(13, 8, P).transpose(2, 0, 1).reshape(P, 104)
    sm[:, O_PSC:O_PSC + 8] = inp['pool_scale'].reshape(2, 4, P).transpose(2, 0, 1).reshape(P, 8)
    sm[:, O_DNW:O_DNW + 96] = inp['dn_conv_w'].reshape(2, 4, 12, P).transpose(3, 0, 2, 1).reshape(P, 96)
    sm[:, O_ONORM:O_ONORM + 2] = inp['dn_out_norm'].T
    sm[:, O_SGG:O_SGG + 8] = inp['sgu_norm_g'].reshape(2, 4, P).transpose(2, 0, 1).reshape(P, 8)
    sm[:, O_SGB:O_SGB + 8] = inp['sgu_norm_b'].reshape(2, 4, P).transpose(2, 0, 1).reshape(P, 8)
    sm[:, O_SCW:O_SCW + 24] = inp['sc_conv_w'].reshape(2, 3, 4, P).transpose(3, 0, 2, 1).reshape(P, 24)
    sm[4:8, O_ALOG:O_ALOG + 2] = inp['dn_a_log'].T
    sm[4:8, O_DTB:O_DTB + 2] = inp['dn_dt_bias'].T
    pp = np.arange(P)[:, None]
    ff = np.arange(P)[None, :]
    sm[:, O_IDENT:O_IDENT + P] = (pp == ff)
    sm[:, O_MASKA:O_MASKA + P] = np.where(ff >= pp, 0.0, -30000.0)
    sm[:, O_STRICT:O_STRICT + P] = (ff > pp)
    sm[:, O_TRIL:O_TRIL + P] = (ff <= pp)
    sm[:, O_RESET:O_RESET + 256] = (np.arange(256) % P != 0)[None, :]
    for g, w in enumerate(POOL_WINDOWS):
        sm[:, O_INVC + g * 16:O_INVC + (g + 1) * 16] = (1.0 / np.minimum(np.arange(16) + 1, w))[None, :]
    return sm


def make_in_maps(inp, n_cores, NB):
    inp = {k: np.asarray(v) for k, v in inp.items()}
    x = inp['x']
    shared = {}
    for k in ("ffn1_w_gate", "ffn1_w_up", "ffn1_w_down", "ffn2_w_gate", "ffn2_w_up", "ffn2_w_down",
              "ab_w_in", "ab_w_out", "cd_w_in", "cd_w_out"):
        shared[k] = np.ascontiguousarray(inp[k], dtype=np.float32)
    shared["smalls"] = make_smalls(inp)
    shared["poolw"] = np.ascontiguousarray(inp['pool_w'].transpose(2, 0, 1, 3).reshape(P, 2 * 4 * P), dtype=np.float32)
    shared["sguw"] = np.ascontiguousarray(inp['sgu_w'].transpose(0, 2, 1, 3).reshape(2, P, 4 * P), dtype=np.float32)
    shared["sgub"] = np.ascontiguousarray(
        np.broadcast_to(inp['sgu_bias'][:, None, :, :], (2, P, 4, P)).reshape(2, P, 4 * P), dtype=np.float32)
    maps = []
    for c in range(n_cores):
        m = dict(shared)
        m["xT"] = np.ascontiguousarray(x[c * NB:(c + 1) * NB].transpose(0, 2, 1))
        maps.append(m)
    return maps


def kernel(**inputs):
    n_cores = 8
    x = np.asarray(inputs['x'])
    B, S, _ = x.shape
    NB = B // n_cores
    bld = Builder(NB, S, range(4))
    nc = bld.build()
    maps = make_in_maps(inputs, n_cores, NB)
    res = run_bass_kernel_spmd(nc, maps, core_ids=list(range(n_cores)))
    out = np.empty((B, S, D), np.float32)
    for c in range(n_cores):
        out[c * NB:(c + 1) * NB] = res.results[c]["yT"].transpose(0, 2, 1)
    return out
```

```python
import numpy as np
from contextlib import ExitStack
import concourse.bass as bass
import concourse.mybir as mybir
from concourse.bass_utils import run_bass_kernel_spmd

F32 = mybir.dt.float32
BF16 = mybir.dt.bfloat16
AF = mybir.ActivationFunctionType
ALU = mybir.AluOpType

P = 128
D = 1024
KC = 8
FF = 2816
EPS = 1e-6
NSLOT = 6
AB_IN = 2568
CD_IN = 2560
POOL_WINDOWS = (2, 4, 8, 16)
GELU_C = 0.7978845608028654

O_NORM = 0
O_PSC = 104
O_DNW = 112
O_ONORM = 208
O_SGG = 210
O_SGB = 218
O_SCW = 226
O_ALOG = 250
O_DTB = 252
O_IDENT = 254
O_MASKA = 382
O_STRICT = 510
O_TRIL = 638
O_RESET = 766
O_INVC = 1022
NSM = 1086

ENGS = ('pe', 'act', 'dve', 'pool', 'sp')
BLK = {'pe': 'tensor', 'act': 'scalar', 'dve': 'vector', 'pool': 'gpsimd', 'sp': 'sync'}


class Tok:
    __slots__ = ('w', 'r')

    def __init__(self):
        self.w = None
        self.r = {}


class Sched:
    def __init__(self, nc, es, same=True):
        self.nc = nc
        self.es = es
        self.same = same
        self.prog = {e: [] for e in ENGS}
        self.seen = {e: {} for e in ENGS}
        self.sems = []
        self.toks = []
        self.esem = {}
        self.cnt = {}
        self.epoch = -1
        self.new_epoch()

    def new_sem(self, name):
        h = self.es.enter_context(self.nc.semaphore(name))
        self.sems.append(h)
        return len(self.sems) - 1

    def tok(self):
        t = Tok()
        self.toks.append(t)
        return t

    def toks2(self, *dims):
        if len(dims) == 1:
            return [self.tok() for _ in range(dims[0])]
        return [self.toks2(*dims[1:]) for _ in range(dims[0])]

    def barrier(self, engs):
        fin = {e: (self.esem[e], self.cnt[e]) for e in engs if self.cnt[e] > 0}
        for e in engs:
            for p, (s, v) in fin.items():
                if self.seen[e].get(s, 0) >= v:
                    continue
                self.seen[e][s] = v
                self.prog[e].append(('w', s, v))

    def new_epoch(self):
        if self.epoch >= 0:
            self.barrier(ENGS)
            old = set(self.esem.values())
            for t in self.toks:
                if t.w is not None and t.w[0] in old:
                    t.w = None
                t.r = {s: v for s, v in t.r.items() if s not in old}
        self.epoch += 1
        self.esem = {e: self.new_sem(f"e{self.epoch}_{e}") for e in ENGS}
        self.cnt = {e: 0 for e in ENGS}

    def op(self, eng, insts, reads=(), writes=(), dma=None):
        deps = {}

        def need(d):
            if d is None:
                return
            s, v = d
            if deps.get(s, 0) < v:
                deps[s] = v
        for t in reads:
            need(t.w)
        for t in writes:
            need(t.w)
            for s, v in t.r.items():
                need((s, v))
        my = self.esem[eng]
        for s, v in deps.items():
            if s == my and (eng == 'pe' or not self.same):
                continue
            if self.seen[eng].get(s, 0) >= v:
                continue
            self.seen[eng][s] = v
            self.prog[eng].append(('w', s, v))
        if dma is None:
            self.cnt[eng] += 1
            tk = (my, self.cnt[eng])
            inc = 1
        else:
            tk = dma
            inc = 16
        if isinstance(insts, tuple):
            insts = [insts]
        self.prog[eng].append(('o', insts, tk[0], inc))
        for t in writes:
            t.w = tk
            t.r = {}
        for t in reads:
            if t.r.get(tk[0], 0) < tk[1]:
                t.r[tk[0]] = tk[1]
        return tk

    def emit(self):
        nc = self.nc
        sems = self.sems
        with nc.Block() as block:
            for e in ENGS:
                def body(eng, prog=self.prog[e]):
                    for it in prog:
                        if it[0] == 'w':
                            eng.wait_ge(sems[it[1]], it[2])
                        else:
                            ins = None
                            for name, kw in it[1]:
                                ins = getattr(eng, name)(**kw)
                            ins.then_inc(sems[it[2]], it[3])
                getattr(block, BLK[e])(body)


class Arena:
    def __init__(self, ap, width):
        self.ap = ap
        self.w = width
        self.off = 0

    def reset(self):
        self.off = 0

    def alloc(self, shape, dt):
        n = int(np.prod(shape))
        nw = (n * (4 if dt == F32 else 2) + 3) // 4
        nw = (nw + 7) // 8 * 8
        a = self.ap[:, self.off:self.off + nw]
        self.off += nw
        assert self.off <= self.w, f"arena overflow {self.off} > {self.w}"
        if dt != F32:
            a = a.bitcast(dt)
        a = a[:, 0:n]
        if len(shape) == 2:
            a = a.rearrange("p (a b) -> p a b", a=shape[0])
        elif len(shape) == 3:
            a = a.rearrange("p (a b c) -> p a b c", a=shape[0], b=shape[1])
        return a


class Builder:
    def __init__(self, NB, S, layers, do_ffn=True, do_mix=True, same=True, IL=4):
        self.IL = IL
        self.NB, self.S, self.layers = NB, S, list(layers)
        self.do_ffn, self.do_mix = do_ffn, do_mix
        self.NT = S // 512
        self.es = ExitStack()
        nc = self.nc = bass.Bass("TRN2", target_bir_lowering=False)
        es = self.es
        dt = lambda name, shape, kind="ExternalInput": nc.dram_tensor(name, shape, F32, kind=kind).ap()
        self.xT = dt("xT", [NB, D, S])
        self.yT = dt("yT", [NB, D, S], "ExternalOutput")
        self.w = {}
        for pre in ("ffn1", "ffn2"):
            self.w[pre + "_w_gate"] = dt(pre + "_w_gate", [4, D, FF])
            self.w[pre + "_w_up"] = dt(pre + "_w_up", [4, D, FF])
            self.w[pre + "_w_down"] = dt(pre + "_w_down", [4, FF, D])
        self.w["ab_w_in"] = dt("ab_w_in", [2, D, AB_IN])
        self.w["ab_w_out"] = dt("ab_w_out", [2, D, D])
        self.w["cd_w_in"] = dt("cd_w_in", [2, D, CD_IN])
        self.w["cd_w_out"] = dt("cd_w_out", [2, D, D])
        self.d_smalls = dt("smalls", [P, NSM])
        self.d_poolw = dt("poolw", [P, 2 * 4 * 128])
        self.d_sguw = dt("sguw", [2, P, 4 * 128])
        self.d_sgub = dt("sgub", [2, P, 4 * 128])

        sb = lambda name, shape, d: es.enter_context(nc.sbuf_tensor(name, shape, d))
        self.h_t = sb("h", [P, KC, S], F32)
        self.h = self.h_t[:]
        self.slots = [sb(f"ws{i}", [P, 4096], BF16) for i in range(NSLOT)]
        self.sm = sb("smalls_sb", [P, NSM], F32)[:]
        self.ones_bf = sb("ones_bf", [P, P], BF16)[:]
        self.ones32 = sb("ones32", [P, P], F32)[:]
        self.poolw = sb("poolw_sb", [P, 2 * 4 * 128], BF16)[:]
        self.ident_bf = sb("ident_bf", [P, P], BF16)[:]
        rem = int(nc.sbuf_bytes_remaining)
        aw = (rem - 1024) // 4 // 8 * 8
        self.arena_w = aw
        self.ar = Arena(sb("arena", [P, aw], F32)[:], aw)
        self.PS = [es.enter_context(nc.psum_tensor(f"ps{i}", [P, 512], F32))[:] for i in range(8)]

        self.sc = Sched(nc, es, same=same)
        sc = self.sc
        self.PT = sc.toks2(8)
        self.HT = sc.toks2(KC, self.NT)
        self.slot_tok = sc.toks2(NSLOT)
        self.slot_sem = [sc.new_sem(f"wl{i}") for i in range(NSLOT)]
        self.slot_cnt = [0] * NSLOT
        self.slot_next = 0
        self.slot_epoch = [-1] * NSLOT
        self.h_sem = [sc.new_sem(f"hs{c}") for c in range(KC)]
        self.h_cnt = [0] * KC
        self.misc_sem = sc.new_sem("misc")
        self.misc_cnt = 0
        self.misc2_sem = sc.new_sem("misc2")
        self.misc2_cnt = 0
        self.wres_sem = sc.new_sem("wres")
        self.wres_cnt = 0
        self.const_tok = sc.tok()
        self.bank_rr = 0
        self.dbg_done = set()
        self.dbg_sems = []
        self.nb_range = (0, 7)

    def dbg(self, name, ap, toks):
        if not getattr(self, 'debug', False) or name in self.dbg_done:
            return
        self.dbg_done.add(name)
        shape = [int(v) for v in ap.shape]
        d = self.nc.dram_tensor("dbg_" + name, shape, ap.dtype, kind="ExternalOutput").ap()
        sem = self.sc.new_sem("dbg_" + name)
        self.sc.op('sp', ('dma_start', dict(out=d, in_=ap)), tuple(toks), (), dma=(sem, 16))
        self.dbg_sems.append(sem)

    def A(self, reads, writes, **kw):
        self.sc.op('act', ('activation', kw), reads, writes)

    def V(self, name, reads, writes, **kw):
        self.sc.op('dve', (name, kw), reads, writes)

    def nb(self, lo=None, hi=None):
        if lo is None:
            lo, hi = self.nb_range
        n = hi - lo
        b = lo + (self.bank_rr % n)
        self.bank_rr += 1
        return b

    def sm_col(self, off, n=1):
        return self.sm[:, off:off + n]

    def wload(self, src, view):
        i = self.slot_next % NSLOT
        self.slot_next += 1
        assert (self.slot_cnt[i] == 0 or self.slot_tok[i].r or self.slot_epoch[i] != self.sc.epoch), \
            "weight slot reloaded before any reader was emitted"
        self.slot_epoch[i] = self.sc.epoch
        self.slot_cnt[i] += 16
        dst = view(self.slots[i][:])
        self.sc.op('pool', ('dma_start', dict(out=dst, in_=src)), (), (self.slot_tok[i],),
                   dma=(self.slot_sem[i], self.slot_cnt[i]))
        return i

    def v_k512(self, ncol):
        return lambda a: a.rearrange("p (k f) -> p k f", k=8)[:, :, 0:ncol]

    def slot_k512(self, i):
        return self.slots[i][:].rearrange("p (k f) -> p k f", k=8)

    def phase(self, pool=False):
        self.sc.barrier(('pe', 'act', 'dve', 'sp', 'pool') if pool else ('pe', 'act', 'dve', 'sp'))
        self.ar.reset()

    def wres(self, srcs, views):
        tokn = self.sc.tok()
        for src, dst in zip(srcs, views):
            self.wres_cnt += 16
            self.sc.op('pool', ('dma_start', dict(out=dst, in_=src)), (), (tokn,),
                       dma=(self.wres_sem, self.wres_cnt))
        return tokn

    def _w(self, slot):
        if isinstance(slot, int):
            return self.slot_k512(slot), self.slot_tok[slot]
        return slot

    def emit_rstd(self, srcs, stoks, n, inv_count, eps, out_ap, out_tok, sq, sqt, L, Lt):
        b = self.nb()
        ns = len(srcs)
        for i in range(ns):
            q, qt = sq[i % len(sq)], sqt[i % len(sq)]
            self.A((stoks[i],), (qt,), out=q[:, 0:n], in_=srcs[i], func=AF.Square)
            self.sc.op('pe', ('matmul', dict(out=self.PS[b][:, 0:n], lhsT=self.ones_bf, rhs=q[:, 0:n],
                                             start=(i == 0), stop=(i == ns - 1))),
                       (qt, self.const_tok), (self.PT[b],))
        self.A((self.PT[b],), (Lt,), out=L[:, 0:n], in_=self.PS[b][:, 0:n], func=AF.Ln,
               scale=float(inv_count), bias=float(eps))
        self.A((Lt,), (out_tok,), out=out_ap, in_=L[:, 0:n], func=AF.Exp, scale=-0.5)

    def build(self):
        sc = self.sc
        self.misc_cnt += 16
        sc.op('sp', ('dma_start', dict(out=self.sm, in_=self.d_smalls)), (), (self.const_tok,),
              dma=(self.misc_sem, self.misc_cnt))
        pw_tok = sc.tok()
        self.pw_tok = pw_tok
        pw_sem = sc.new_sem("pwl")
        sc.op('pool', ('dma_start', dict(out=self.poolw, in_=self.d_poolw)), (), (pw_tok,),
              dma=(pw_sem, 16))
        self.V('memset', (), (self.const_tok,), ap=self.ones_bf, constant=1.0)
        self.V('memset', (), (self.const_tok,), ap=self.ones32, constant=1.0)
        self.V('tensor_copy', (self.const_tok,), (self.const_tok,), out=self.ident_bf,
               in_=self.sm[:, O_IDENT:O_IDENT + P])
        for b in range(self.NB):
            if b > 0:
                sc.new_epoch()
            self.one_batch(b)
        for c in range(KC):
            sc.prog['sp'].append(('w', self.h_sem[c], self.h_cnt[c]))
        for sm_ in self.dbg_sems:
            sc.prog['sp'].append(('w', sm_, 16))
        sc.emit()
        return self.nc

    def one_batch(self, b):
        sc = self.sc
        S = self.S
        for c in range(KC):
            self.h_cnt[c] += 16
            sc.op('sp', ('dma_start', dict(out=self.h[:, c, :], in_=self.xT[b, c * P:(c + 1) * P, :])),
                  (), tuple(self.HT[c]), dma=(self.h_sem[c], self.h_cnt[c]))
        for l in self.layers:
            if self.do_ffn:
                self.ffn(l, 0)
            if self.do_mix:
                if l % 2 == 0:
                    self.mixer_ab(l)
                else:
                    self.mixer_cd(l)
            if self.do_ffn:
                self.ffn(l, 1)
        self.final(b)

    def final(self, b):
        sc = self.sc
        self.phase()
        ar = self.ar
        sq = [ar.alloc([512], BF16) for _ in range(3)]
        sqt = sc.toks2(3)
        L = ar.alloc([512], F32)
        Lt = sc.tok()
        rs = [ar.alloc([512], F32) for _ in range(2)]
        rst = sc.toks2(2)
        for t in range(self.NT):
            tl = slice(t * 512, (t + 1) * 512)
            r, rt = rs[t % 2], rst[t % 2]
            self.emit_rstd([self.h[:, c, tl] for c in range(KC)], [self.HT[c][t] for c in range(KC)], 512,
                           1.0 / D, EPS, r, rt, sq, sqt, L, Lt)
            for c in range(KC):
                self.V('scalar_tensor_tensor', (self.HT[c][t], rt, self.const_tok), (self.HT[c][t],),
                       out=self.h[:, c, tl], in0=self.h[:, c, tl],
                       scalar=self.sm_col(O_NORM + 12 * 8 + c), in1=r, op0=ALU.mult, op1=ALU.mult)
        for c in range(KC):
            self.h_cnt[c] += 16
            sc.op('sp', ('dma_start', dict(out=self.yT[b, c * P:(c + 1) * P, :], in_=self.h[:, c, :])),
                  tuple(self.HT[c]), (), dma=(self.h_sem[c], self.h_cnt[c]))

    def ffn(self, l, which):
        sc, ar, S, NT = self.sc, self.ar, self.S, self.NT
        pre = "ffn1" if which == 0 else "ffn2"
        wg = self.w[pre + "_w_gate"][l].rearrange("(k p) f -> p k f", p=P)
        wu = self.w[pre + "_w_up"][l].rearrange("(k p) f -> p k f", p=P)
        wd = self.w[pre + "_w_down"][l].rearrange("(j p) d -> p j d", p=P)
        nidx = l * 3 + (0 if which == 0 else 2)
        self.phase()
        hn = ar.alloc([KC, S], BF16)
        act = ar.alloc([4, S], BF16)
        sil = [ar.alloc([512], F32) for _ in range(2)]
        silt = sc.toks2(2)
        sq = [ar.alloc([512], BF16) for _ in range(3)]
        sqt = sc.toks2(3)
        L = ar.alloc([512], F32)
        Lt = sc.tok()
        rs = [ar.alloc([512], F32) for _ in range(2)]
        rst = sc.toks2(2)
        HNT = sc.toks2(KC, NT)
        ACTT = sc.toks2(4, NT)
        for t in range(NT):
            tl = slice(t * 512, (t + 1) * 512)
            r, rt = rs[t % 2], rst[t % 2]
            self.emit_rstd([self.h[:, c, tl] for c in range(KC)], [self.HT[c][t] for c in range(KC)], 512,
                           1.0 / D, EPS, r, rt, sq, sqt, L, Lt)
            for c in range(KC):
                self.V('scalar_tensor_tensor', (self.HT[c][t], rt, self.const_tok), (HNT[c][t],),
                       out=hn[:, c, tl], in0=self.h[:, c, tl],
                       scalar=self.sm_col(O_NORM + nidx * 8 + c), in1=r, op0=ALU.mult, op1=ALU.mult)
        groups = [(0, 4), (4, 4), (8, 4), (12, 4), (16, 4), (20, 2)]
        ca = 0
        cc = 0
        for (j0, G) in groups:
            sa = self.wload(wg[:, :, j0 * P:(j0 + G) * P], self.v_k512(G * P))
            sb_ = self.wload(wu[:, :, j0 * P:(j0 + G) * P], self.v_k512(G * P))
            sd = self.wload(wd[:, j0:j0 + G, :],
                            lambda a, G=G: a.rearrange("p (j d) -> p j d", j=4)[:, 0:G, :])
            wa = self.slot_k512(sa)
            wb = self.slot_k512(sb_)
            wdn = self.slots[sd][:].rearrange("p (j d) -> p j d", j=4)
            for j in range(G):
                for t in range(NT):
                    tl = slice(t * 512, (t + 1) * 512)
                    x = ca % 2
                    ca += 1
                    ba, bb = x, 2 + x
                    ins = []
                    for k in range(KC):
                        ins.append(('matmul', dict(out=self.PS[ba], lhsT=wa[:, k, j * P:(j + 1) * P],
                                                   rhs=hn[:, k, tl], start=(k == 0), stop=(k == KC - 1))))
                    for k in range(KC):
                        ins.append(('matmul', dict(out=self.PS[bb], lhsT=wb[:, k, j * P:(j + 1) * P],
                                                   rhs=hn[:, k, tl], start=(k == 0), stop=(k == KC - 1))))
                    sc.op('pe', ins, [self.slot_tok[sa], self.slot_tok[sb_]] + [HNT[k][t] for k in range(KC)],
                          (self.PT[ba], self.PT[bb]))
                    self.A((self.PT[ba],), (silt[x],), out=sil[x], in_=self.PS[ba], func=AF.Silu)
                    self.V('tensor_tensor', (silt[x], self.PT[bb]), (ACTT[j][t],),
                           out=act[:, j, tl], in0=sil[x], in1=self.PS[bb], op=ALU.mult)
            for t in range(NT):
                tl = slice(t * 512, (t + 1) * 512)
                for c in range(KC):
                    y = 4 + (cc % 2)
                    cc += 1
                    ins = []
                    for j in range(G):
                        ins.append(('matmul', dict(out=self.PS[y], lhsT=wdn[:, j, c * P:(c + 1) * P],
                                                   rhs=act[:, j, tl], start=(j == 0), stop=(j == G - 1))))
                    sc.op('pe', ins, [self.slot_tok[sd]] + [ACTT[j][t] for j in range(G)], (self.PT[y],))
                    self.V('scalar_tensor_tensor', (self.PT[y], self.HT[c][t]), (self.HT[c][t],),
                           out=self.h[:, c, tl], in0=self.PS[y], scalar=0.5, in1=self.h[:, c, tl],
                           op0=ALU.mult, op1=ALU.add)

    def tile_norm(self, l, T0, TT, hn_t, hn_tok, sq, sqt, L, Lt, rs, rst):
        t5 = T0 // 512
        tl = slice(T0, T0 + TT)
        self.emit_rstd([self.h[:, c, tl] for c in range(KC)], [self.HT[c][t5] for c in range(KC)], TT,
                       1.0 / D, EPS, rs[:, 0:TT], rst, sq, sqt, L, Lt)
        for c in range(KC):
            self.V('scalar_tensor_tensor', (self.HT[c][t5], rst, self.const_tok), (hn_tok,),
                   out=hn_t[:, c, :], in0=self.h[:, c, tl],
                   scalar=self.sm_col(O_NORM + (l * 3 + 1) * 8 + c), in1=rs[:, 0:TT], op0=ALU.mult, op1=ALU.mult)

    def proj(self, slot, coff, ncol, hn_t, hn_tok, TT, bank=None):
        b = self.nb() if bank is None else bank
        w, wt = self._w(slot)
        ins = []
        for k in range(KC):
            ins.append(('matmul', dict(out=self.PS[b][0:ncol, 0:TT], lhsT=w[:, k, coff:coff + ncol],
                                       rhs=hn_t[:, k, :], start=(k == 0), stop=(k == KC - 1))))
        self.sc.op('pe', ins, (wt, hn_tok), (self.PT[b],))
        return b

    def out_proj(self, wout, o_or_e, T0, TT, y, ytoks):
        t5 = T0 // 512
        tl = slice(T0, T0 + TT)
        for c in range(KC):
            b = self.nb()
            w = self.slot_k512(wout[c // 4])
            ins = []
            for k in range(KC):
                ins.append(('matmul', dict(out=self.PS[b][:, 0:TT], lhsT=w[:, k, (c % 4) * P:(c % 4 + 1) * P],
                                           rhs=y[:, k, :], start=(k == 0), stop=(k == KC - 1))))
            self.sc.op('pe', ins, [self.slot_tok[wout[c // 4]]] + list(ytoks), (self.PT[b],))
            self.V('tensor_tensor', (self.PT[b], self.HT[c][t5]), (self.HT[c][t5],),
                   out=self.h[:, c, tl], in0=self.PS[b][:, 0:TT], in1=self.h[:, c, tl], op=ALU.add)

    def gelu2(self, b, n, out, out_tok, tA, tAt, tB, tBt):
        ps = self.PS[b][:, 0:n]
        pt = self.PT[b]
        self.A((pt,), (tAt,), out=tA, in_=ps, func=AF.Square)
        self.V('tensor_scalar', (tAt,), (tAt,), out=tA, in0=tA, scalar1=0.044715, scalar2=1.0,
               op0=ALU.mult, op1=ALU.add)
        self.V('tensor_tensor', (tAt, pt), (tBt,), out=tB, in0=tA, in1=ps, op=ALU.mult)
        self.A((tBt,), (tBt,), out=tB, in_=tB, func=AF.Tanh, scale=GELU_C)
        self.V('scalar_tensor_tensor', (tBt, pt), (out_tok,), out=out, in0=tB, scalar=1.0, in1=ps,
               op0=ALU.add, op1=ALU.mult)

    def mixer_cd(self, l):
        sc, ar, S = self.sc, self.ar, self.S
        o = l // 2
        TT = 512
        win = self.w["cd_w_in"][o].rearrange("(k p) f -> p k f", p=P)
        wout = self.w["cd_w_out"][o].rearrange("(k p) f -> p k f", p=P)
        self.phase()
        self.nb_range = (0, 5)
        tk = sc.tok
        biasB = ar.alloc([4, P], F32)
        wnat = ar.alloc([4, P], F32)
        wsT = ar.alloc([4, P], BF16)
        prm_tok = tk()
        self.misc_cnt += 16
        sc.op('sp', ('dma_start', dict(out=biasB.rearrange("p a b -> p (a b)"), in_=self.d_sgub[o])),
              (), (prm_tok,), dma=(self.misc_sem, self.misc_cnt))
        wn_tok = tk()
        self.misc2_cnt += 16
        sc.op('sp', ('dma_start', dict(out=wnat.rearrange("p a b -> p (a b)"), in_=self.d_sguw[o])),
              (), (wn_tok,), dma=(self.misc2_sem, self.misc2_cnt))
        wsT_tok = tk()
        bt = self.nb()
        for hh in range(4):
            self.V('tensor_tensor', (wn_tok, self.const_tok), (wn_tok,), out=wnat[:, hh, :], in0=wnat[:, hh, :],
                   in1=self.sm[:, O_TRIL:O_TRIL + P], op=ALU.mult)
            sc.op('pe', ('transpose', dict(out=self.PS[bt][:, hh * P:(hh + 1) * P], in_=wnat[:, hh, :],
                                           identity=self.sm[:, O_IDENT:O_IDENT + P])),
                  (wn_tok, self.const_tok), (self.PT[bt],))
        self.A((self.PT[bt],), (wsT_tok,), out=wsT.rearrange("p a b -> p (a b)"), in_=self.PS[bt], func=AF.Copy)
        xc = ar.alloc([4, TT + 2], F32)
        xct = sc.toks2(4)
        for ch in range(4):
            self.V('memset', (), (xct[ch],), ap=xc[:, ch, 0:2], constant=0.0)
        hn_t = ar.alloc([KC, TT], BF16)
        hn_tok = tk()
        sq = [ar.alloc([TT], BF16) for _ in range(3)]
        sqt = sc.toks2(3)
        L = ar.alloc([TT], F32)
        Lt = tk()
        rs = ar.alloc([TT], F32)
        rst = tk()
        vg = [ar.alloc([TT], F32) for _ in range(4)]
        vgt = sc.toks2(4)
        tA = [ar.alloc([TT], F32) for _ in range(2)]
        tAt = sc.toks2(2)
        tB = [ar.alloc([TT], F32) for _ in range(2)]
        tBt = sc.toks2(2)
        mean = ar.alloc([TT], F32)
        mean_t = tk()
        m2 = ar.alloc([TT], F32)
        m2_t = tk()
        rstd = ar.alloc([TT], F32)
        rstd_t = tk()
        vtok = ar.alloc([4, 512], BF16)
        vtok_t = sc.toks2(4)
        ug = [ar.alloc([TT], F32) for _ in range(2)]
        ugt = sc.toks2(2)
        tmp = [ar.alloc([TT], F32) for _ in range(2)]
        tmpt = sc.toks2(2)
        acc = [ar.alloc([TT], F32) for _ in range(2)]
        acct = sc.toks2(2)
        y = ar.alloc([KC, TT], BF16)
        yt = sc.toks2(KC)
        gi = 0
        for tau in range(S // TT):
            T0 = tau * TT
            self.tile_norm(l, T0, TT, hn_t, hn_tok, sq, sqt, L, Lt, rs, rst)
            wi = [self.wload(win[:, :, g * 512:(g + 1) * 512], self.v_k512(512)) for g in range(5)]

            def pj(m):
                return self.proj(wi[m // 4], (m % 4) * P, P, hn_t, hn_tok, TT)
            s1, s2 = 5, 6
            for hh in range(4):
                b = pj(4 + hh)
                x = gi % 2
                gi += 1
                self.gelu2(b, TT, vg[hh], vgt[hh], tA[x], tAt[x], tB[x], tBt[x])
                self.A((vgt[hh],), (tAt[x],), out=tA[x], in_=vg[hh], func=AF.Square)
                sc.op('pe', ('matmul', dict(out=self.PS[s1], lhsT=self.ones32, rhs=vg[hh], start=(hh == 0),
                                            stop=(hh == 3))), (vgt[hh], self.const_tok), (self.PT[s1],))
                sc.op('pe', ('matmul', dict(out=self.PS[s2], lhsT=self.ones32, rhs=tA[x], start=(hh == 0),
                                            stop=(hh == 3))), (tAt[x], self.const_tok), (self.PT[s2],))
            self.dbg("hn", hn_t, (hn_tok,))
            self.dbg("vg0_pre", vg[0], (vgt[0],))
            self.A((self.PT[s1],), (mean_t,), out=mean, in_=self.PS[s1], func=AF.Identity, scale=1.0 / 512)
            self.dbg("mean", mean, (mean_t,))
            self.V('tensor_tensor', (mean_t,), (m2_t,), out=m2, in0=mean, in1=mean, op=ALU.mult)
            self.V('scalar_tensor_tensor', (self.PT[s2], m2_t), (m2_t,), out=m2, in0=self.PS[s2],
                   scalar=1.0 / 512, in1=m2, op0=ALU.mult, op1=ALU.subtract)
            self.A((m2_t,), (m2_t,), out=m2, in_=m2, func=AF.Ln, scale=1.0, bias=4.0 * EPS)
            self.A((m2_t,), (rstd_t,), out=rstd, in_=m2, func=AF.Exp, scale=-0.5)
            for hh in range(4):
                self.V('tensor_tensor', (vgt[hh], mean_t), (vgt[hh],), out=vg[hh], in0=vg[hh], in1=mean,
                       op=ALU.subtract)
                self.V('tensor_tensor', (vgt[hh], rstd_t), (vgt[hh],), out=vg[hh], in0=vg[hh], in1=rstd,
                       op=ALU.mult)
                self.V('tensor_scalar', (vgt[hh], self.const_tok), (vgt[hh],), out=vg[hh], in0=vg[hh],
                       scalar1=self.sm_col(O_SGG + o * 4 + hh), scalar2=self.sm_col(O_SGB + o * 4 + hh),
                       op0=ALU.mult, op1=ALU.add)
            self.dbg("rstd", rstd, (rstd_t,))
            self.dbg("vg0_ln", vg[0], (vgt[0],))
            self.dbg("wsT", wsT, (wsT_tok,))
            for blk in range(4):
                bt = self.nb()
                for hh in range(4):
                    sc.op('pe', ('transpose', dict(out=self.PS[bt][:, hh * P:(hh + 1) * P],
                                                   in_=vg[hh][:, blk * P:(blk + 1) * P],
                                                   identity=self.sm[:, O_IDENT:O_IDENT + P])),
                          (vgt[hh], self.const_tok), (self.PT[bt],))
                self.A((self.PT[bt],), (vtok_t[blk],), out=vtok[:, blk, :], in_=self.PS[bt], func=AF.Copy)
            for hh in range(4):
                b = pj(hh)
                x = gi % 2
                gi += 1
                self.gelu2(b, TT, ug[x], ugt[x], tA[x], tAt[x], tB[x], tBt[x])
                bm = self.nb()
                for blk in range(4):
                    sc.op('pe', ('matmul', dict(out=self.PS[bm][:, blk * P:(blk + 1) * P],
                                                lhsT=vtok[:, blk, hh * P:(hh + 1) * P], rhs=wsT[:, hh, :],
                                                start=True, stop=True)),
                          (vtok_t[blk], wsT_tok), (self.PT[bm],))
                self.V('tensor_tensor', (self.PT[bm], prm_tok), (tmpt[x],),
                       out=tmp[x].rearrange("p (a b) -> p a b", a=4),
                       in0=self.PS[bm].rearrange("p (a b) -> p a b", a=4),
                       in1=biasB[:, hh:hh + 1, :].to_broadcast([P, 4, P]), op=ALU.add)
                self.dbg("vtok", vtok, vtok_t)
                self.dbg("tmp0", tmp[x], (tmpt[x],))
                self.dbg("ug0", ug[x], (ugt[x],))
                self.V('scalar_tensor_tensor', (tmpt[x], ugt[x]), (yt[hh],), out=y[:, hh, :], in0=tmp[x],
                       scalar=0.5, in1=ug[x], op0=ALU.mult, op1=ALU.mult)
            for ch in range(4):
                bx = pj(8 + ch)
                bg_ = pj(12 + ch)
                bc = pj(16 + ch)
                x = gi % 2
                gi += 1
                self.A((self.PT[bx],), (tmpt[x],), out=tmp[x], in_=self.PS[bx], func=AF.Copy)
                self.V('tensor_tensor', (self.PT[bc], tmpt[x]), (xct[ch],), out=xc[:, ch, 2:TT + 2],
                       in0=self.PS[bc], in1=tmp[x], op=ALU.mult)
                wcol = lambda i: self.sm_col(O_SCW + (o * 4 + ch) * 3 + i)
                self.V('tensor_scalar', (xct[ch], self.const_tok), (acct[x],), out=acc[x], in0=xc[:, ch, 0:TT],
                       scalar1=wcol(0), scalar2=None, op0=ALU.mult)
                for i in (1, 2):
                    self.V('scalar_tensor_tensor', (xct[ch], acct[x], self.const_tok), (acct[x],), out=acc[x],
                           in0=xc[:, ch, i:TT + i], scalar=wcol(i), in1=acc[x], op0=ALU.mult, op1=ALU.add)
                self.V('tensor_tensor', (acct[x], self.PT[bg_]), (yt[4 + ch],), out=y[:, 4 + ch, :], in0=acc[x],
                       in1=self.PS[bg_], op=ALU.mult)
                self.A((xct[ch],), (xct[ch],), out=xc[:, ch, 0:2], in_=xc[:, ch, TT:TT + 2], func=AF.Copy)
            self.dbg("y", y, yt)
            wo = [self.wload(wout[:, :, g * 512:(g + 1) * 512], self.v_k512(512)) for g in range(2)]
            self.out_proj(wo, o, T0, TT, y, yt)
        self.nb_range = (0, 7)

    def mixer_ab(self, l):
        sc, ar, S = self.sc, self.ar, self.S
        e = l // 2
        TT = 256
        NCK = 2
        IL = self.IL
        NGB = 8 - IL
        self.nb_range = (0, NGB)
        win = self.w["ab_w_in"][e].rearrange("(k p) f -> p k f", p=P)
        wout = self.w["ab_w_out"][e].rearrange("(k p) f -> p k f", p=P)
        self.phase(pool=True)
        tk = sc.tok
        ident = self.sm[:, O_IDENT:O_IDENT + P]
        f32 = lambda *s: ar.alloc(list(s), F32)
        b16 = lambda *s: ar.alloc(list(s), BF16)
        bg_full = b16(KC, 16)
        bg_ar = bg_full[:, :, 0:8]
        w_bg = (bg_ar, self.wres([win[:, :, 2560:2568]], [bg_ar]))

        class Pl:
            def __init__(s_, shape, dt, n):
                s_.free = [(ar.alloc(shape, dt), sc.tok()) for _ in range(n)]
                s_.n = n
                s_.low = n

            def get(s_):
                assert s_.free, "pool empty"
                it = s_.free.pop(0)
                s_.low = min(s_.low, len(s_.free))
                return it

            def put(s_, *items):
                for it in items:
                    s_.free.append(it)
        S32 = f32(4, P)
        S16 = b16(4, P)
        S32t = sc.toks2(4)
        S16t = sc.toks2(4)
        qcar = b16(12, 4)
        qcart = sc.toks2(12)
        DG = b16(48, P)
        DGt = tk()
        for ci in range(12):
            for i in range(4):
                self.V('tensor_scalar', (self.const_tok,), (DGt,), out=DG[:, ci * 4 + i, :], in0=self.ident_bf,
                       scalar1=self.sm_col(O_DNW + (e * 12 + ci) * 4 + i), scalar2=None, op0=ALU.mult)
        pcar = f32(4, 16)
        pcart = sc.toks2(4)
        self.V('memset', (), tuple(S32t), ap=S32, constant=0.0)
        self.V('memset', (), tuple(S16t), ap=S16, constant=0.0)
        self.V('memset', (), tuple(qcart), ap=qcar, constant=0.0)
        self.V('memset', (), tuple(pcart), ap=pcar, constant=0.0)
        nA = f32(1)
        nAt = tk()
        self.A((self.const_tok,), (nAt,), out=nA, in_=self.sm_col(O_ALOG + e), func=AF.Exp)
        self.V('tensor_scalar', (nAt,), (nAt,), out=nA, in0=nA, scalar1=-1.0, scalar2=None, op0=ALU.mult)
        hn_t = b16(KC, TT)
        hn_tok = tk()
        rs = f32(TT)
        rst = tk()
        X8, X8t = f32(TT), tk()
        W8, W8t = f32(TT), tk()
        G8, G8t = f32(TT), tk()
        GC8, GC8t = f32(TT), tk()
        abuf, abuft = f32(TT + 16), tk()
        sA, sAt = f32(TT + 16), tk()
        sB, sBt = f32(TT + 16), tk()
        pooled, pooledt = b16(TT), tk()
        t16, t16t = f32(16), tk()
        LASTC = f32(4, NCK)
        LASTCt = sc.toks2(4)
        gcol = f32(4, NCK)
        gcolt = sc.toks2(4)
        junk = [b16(P)] * 4
        junkt = [tk()] * 4
        y = b16(KC, TT)
        yt = sc.toks2(KC)
        FP = Pl([264], F32, 7 * IL + 2)
        BP = Pl([TT], BF16, 6 * IL + 3)
        HP = Pl([P], F32, 3)
        QP = Pl([P], BF16, 2 * IL + 2)
        RP = Pl([264], BF16, IL + 2)

        def rstd(src, stok, n, inv_count, out_ap, out_tok):
            q = BP.get()
            Lb = FP.get()
            self.emit_rstd([src], [stok], n, inv_count, EPS, out_ap, out_tok, [q[0]], [q[1]], Lb[0], Lb[1])
            BP.put(q)
            FP.put(Lb)
        v2 = lambda a: a.rearrange("p (a b) -> p a b", a=NCK)
        r8 = lambda a: a[0:8, 0:TT]

        for tau in range(S // TT):
            T0 = tau * TT
            q0 = BP.get()
            q1 = BP.get()
            q2 = BP.get()
            Lb = FP.get()
            self.tile_norm(l, T0, TT, hn_t, hn_tok, [q0[0], q1[0], q2[0]], [q0[1], q1[1], q2[1]], Lb[0], Lb[1],
                           rs, rst)
            BP.put(q0, q1, q2)
            FP.put(Lb)
            w_a = self.wload(win[:, :, 0:512], self.v_k512(512))
            w_q = self.wload(win[:, :, 512:1024], self.v_k512(512))
            w_k = self.wload(win[:, :, 1024:1536], self.v_k512(512))
            w_v = self.wload(win[:, :, 1536:2048], self.v_k512(512))
            w_z = self.wload(win[:, :, 2048:2560], self.v_k512(512))
            b = self.proj(w_bg, 0, 8, hn_t, hn_tok, TT)
            self.V('tensor_scalar', (self.PT[b], self.const_tok), (X8t,), out=r8(X8), in0=self.PS[b][0:8, 0:TT],
                   scalar1=self.sm[0:8, O_DTB + e:O_DTB + e + 1], scalar2=None, op0=ALU.add)
            self.A((X8t,), (W8t,), out=r8(W8), in_=r8(X8), func=AF.Abs)
            self.A((W8t,), (W8t,), out=r8(W8), in_=r8(W8), func=AF.Exp, scale=-1.0)
            self.A((W8t,), (W8t,), out=r8(W8), in_=r8(W8), func=AF.Ln, scale=1.0, bias=1.0)
            self.V('scalar_tensor_tensor', (X8t, W8t), (W8t,), out=r8(W8), in0=r8(X8), scalar=0.0, in1=r8(W8),
                   op0=ALU.max, op1=ALU.add)
            self.V('tensor_tensor', (X8t, W8t), (X8t,), out=r8(X8), in0=r8(X8), in1=r8(W8), op=ALU.subtract)
            self.A((X8t,), (X8t,), out=r8(X8), in_=r8(X8), func=AF.Exp)
            self.V('tensor_scalar', (W8t, nAt), (G8t,), out=r8(G8), in0=r8(W8), scalar1=nA[0:8, 0:1], scalar2=None,
                   op0=ALU.mult)
            self.V('tensor_tensor_scan', (G8t, self.const_tok), (GC8t,), out=r8(GC8),
                   data0=self.sm[0:8, O_RESET:O_RESET + TT], data1=r8(G8), initial=0.0, op0=ALU.mult, op1=ALU.add)

            def pool_gen():
                for g in range(4):
                    wdw = POOL_WINDOWS[g]
                    bp = self.proj(w_a, g * P, P, hn_t, hn_tok, TT)
                    self.A((pcart[g],), (abuft,), out=abuf[:, 0:16], in_=pcar[:, g, :], func=AF.Copy)
                    self.A((self.PT[bp],), (abuft,), out=abuf[:, 16:16 + TT], in_=self.PS[bp][:, 0:TT],
                           func=AF.Copy)
                    cur, curt, lo = abuf, abuft, 0
                    W_ = TT + 16
                    for k in range(g + 1):
                        sh = 1 << k
                        nxt, nxtt = (sA, sAt) if k % 2 == 0 else (sB, sBt)
                        self.V('tensor_tensor', (curt,), (nxtt,), out=nxt[:, lo + sh:W_], in0=cur[:, lo + sh:W_],
                               in1=cur[:, lo:W_ - sh], op=ALU.add)
                        cur, curt, lo = nxt, nxtt, lo + sh
                    self.V('scalar_tensor_tensor', (curt, abuft), (pooledt,), out=pooled, in0=cur[:, 16:W_],
                           scalar=1.0 / wdw, in1=abuf[:, 16:W_], op0=ALU.mult, op1=ALU.subtract)
                    if tau == 0:
                        self.V('tensor_tensor', (curt, self.const_tok), (t16t,), out=t16, in0=cur[:, 16:32],
                               in1=self.sm[:, O_INVC + g * 16:O_INVC + (g + 1) * 16], op=ALU.mult)
                        self.V('tensor_tensor', (t16t, abuft, pooledt), (pooledt,), out=pooled[:, 0:16], in0=t16,
                               in1=abuf[:, 16:32], op=ALU.subtract)
                    self.A((abuft,), (pcart[g],), out=pcar[:, g, :], in_=abuf[:, TT:TT + 16], func=AF.Copy)
                    bq = self.nb()
                    sc.op('pe', ('matmul', dict(out=self.PS[bq][:, 0:TT],
                                                lhsT=self.poolw[:, (e * 4 + g) * P:(e * 4 + g + 1) * P], rhs=pooled,
                                                start=True, stop=True)), (pooledt, self.pw_tok), (self.PT[bq],))
                    self.A((self.PT[bq], self.const_tok), (yt[g],), out=y[:, g, :], in_=self.PS[bq][:, 0:TT],
                           func=AF.Identity, scale=self.sm_col(O_PSC + e * 4 + g))
                    yield

            def head(hd, CH):
                XS = []
                for idx, wsl in enumerate((w_q, w_k, w_v)):
                    ci = idx * 4 + hd
                    raw, rawt = RP.get()
                    xs = FP.get()
                    bp = self.proj(wsl, hd * P, P, hn_t, hn_tok, TT)
                    self.A((qcart[ci],), (rawt,), out=raw[:, 0:3], in_=qcar[:, ci, 0:3], func=AF.Copy)
                    self.A((self.PT[bp],), (rawt,), out=raw[:, 3:3 + TT], in_=self.PS[bp][:, 0:TT], func=AF.Copy)
                    bc = self.nb()
                    sc.op('pe', [('matmul', dict(out=self.PS[bc][:, 0:TT], lhsT=DG[:, ci * 4 + i, :],
                                                 rhs=raw[:, i:TT + i], start=(i == 0), stop=(i == 3)))
                                 for i in range(4)], (rawt, DGt), (self.PT[bc],))
                    self.A((rawt,), (qcart[ci],), out=qcar[:, ci, 0:3], in_=raw[:, TT:TT + 3], func=AF.Copy)
                    self.A((self.PT[bc],), (xs[1],), out=xs[0][:, 0:TT], in_=self.PS[bc][:, 0:TT], func=AF.Silu)
                    RP.put((raw, rawt))
                    XS.append(xs)
                    yield
                RQ = FP.get()
                RK = FP.get()
                rstd(XS[0][0][:, 0:TT], XS[0][1], TT, 1.0, RQ[0][:, 0:TT], RQ[1])
                rstd(XS[1][0][:, 0:TT], XS[1][1], TT, 1.0, RK[0][:, 0:TT], RK[1])
                yield
                KN, KNt = BP.get()
                QN, QNt = BP.get()
                self.V('tensor_tensor', (XS[1][1], RK[1]), (KNt,), out=KN, in0=XS[1][0][:, 0:TT], in1=RK[0][:, 0:TT],
                       op=ALU.mult)
                self.V('scalar_tensor_tensor', (XS[0][1], RQ[1]), (QNt,), out=QN, in0=XS[0][0][:, 0:TT],
                       scalar=float(P ** -0.5), in1=RQ[0][:, 0:TT], op0=ALU.mult, op1=ALU.mult)
                FP.put(RQ, RK, XS[0], XS[1])
                yield
                GS, GSt = FP.get()
                BS, BSt = FP.get()
                self.V('tensor_scalar', (GC8t, self.const_tok), (GSt,), out=r8(GS), in0=r8(GC8),
                       scalar1=self.sm[0:8, O_IDENT + 4 + hd:O_IDENT + 5 + hd], scalar2=None, op0=ALU.mult)
                self.V('tensor_scalar', (X8t, self.const_tok), (BSt,), out=r8(BS), in0=r8(X8),
                       scalar1=self.sm[0:8, O_IDENT + hd:O_IDENT + hd + 1], scalar2=None, op0=ALU.mult)
                b1 = self.nb()
                sc.op('pe', [('matmul', dict(out=self.PS[b1][:, 0:TT], lhsT=self.ones32[0:8, :], rhs=r8(GS),
                                             start=True, stop=True)),
                             ('matmul', dict(out=self.PS[b1][:, TT:2 * TT], lhsT=self.ones32[0:8, :], rhs=r8(BS),
                                             start=True, stop=True))],
                      (GSt, BSt, self.const_tok), (self.PT[b1],))
                FP.put((GS, GSt), (BS, BSt))
                GCBb = FP.get()
                EGCb = FP.get()
                BETABb = FP.get()
                GCB, GCBt = GCBb[0][:, 0:TT], GCBb[1]
                EGC, EGCt = EGCb[0][:, 0:TT], EGCb[1]
                BETAB, BETABt = BETABb[0][:, 0:TT], BETABb[1]
                self.A((self.PT[b1],), (GCBt,), out=GCB, in_=self.PS[b1][:, 0:TT], func=AF.Copy)
                self.A((self.PT[b1],), (EGCt,), out=EGC, in_=self.PS[b1][:, 0:TT], func=AF.Exp)
                self.A((self.PT[b1],), (BETABt,), out=BETAB, in_=self.PS[b1][:, TT:2 * TT], func=AF.Copy)
                yield
                DLb = FP.get()
                BEb = FP.get()
                BSTb = FP.get()
                DL, DLt = DLb[0][:, 0:TT], DLb[1]
                BE, BEt = BEb[0][:, 0:TT], BEb[1]
                BST, BSTt = BSTb[0][:, 0:TT], BSTb[1]
                g3 = v2(GCB)
                self.V('tensor_tensor', (GCBt,), (DLt,), out=v2(DL),
                       in0=g3[:, :, P - 1:P].to_broadcast([P, NCK, P]), in1=g3, op=ALU.subtract)
                self.A((DLt,), (DLt,), out=DL, in_=DL, func=AF.Exp)
                self.V('tensor_copy', (EGCt,), (LASTCt[hd],), out=LASTC[:, hd, :], in_=EGC[:, P - 1:TT:P])
                self.V('tensor_tensor', (BETABt, EGCt), (BEt,), out=BE, in0=BETAB, in1=EGC, op=ALU.mult)
                self.V('tensor_tensor', (BETABt, self.const_tok), (BSTt,), out=v2(BST), in0=v2(BETAB),
                       in1=self.sm[:, O_STRICT:O_STRICT + P].unsqueeze(1).to_broadcast([P, NCK, P]), op=ALU.mult)
                yield
                KBG, KBGt = BP.get()
                QDEC, QDECt = BP.get()
                KDTb = FP.get()
                VBTb = FP.get()
                KDT, KDTt = KDTb[0][:, 0:TT], KDTb[1]
                VBT, VBTt = VBTb[0][:, 0:TT], VBTb[1]
                self.V('tensor_tensor', (KNt, BEt), (KBGt,), out=KBG, in0=KN, in1=BE, op=ALU.mult)
                self.V('tensor_tensor', (QNt, EGCt), (QDECt,), out=QDEC, in0=QN, in1=EGC, op=ALU.mult)
                self.V('tensor_tensor', (KNt, DLt), (KDTt,), out=KDT, in0=KN, in1=DL, op=ALU.mult)
                self.V('tensor_tensor', (XS[2][1], BETABt), (VBTt,), out=VBT, in0=XS[2][0][:, 0:TT], in1=BETAB,
                       op=ALU.mult)
                FP.put(BEb, DLb, XS[2], EGCb, BETABb)
                yield
                btp = self.nb()
                ins = []
                for ck in range(NCK):
                    ins.append(('transpose', dict(out=self.PS[btp][:, ck * P:(ck + 1) * P],
                                                  in_=KDT[:, ck * P:(ck + 1) * P], identity=ident)))
                for ck in range(NCK):
                    ins.append(('transpose', dict(out=self.PS[btp][:, (NCK + ck) * P:(NCK + ck + 1) * P],
                                                  in_=VBT[:, ck * P:(ck + 1) * P], identity=ident)))
                sc.op('pe', ins, (KDTt, VBTt, self.const_tok), (self.PT[btp],))
                KDECb = BP.get()
                VBb = FP.get()
                KDEC, KDECt = KDECb[0], KDECb[1]
                VB, VBt = VBb[0][:, 0:TT], VBb[1]
                self.A((self.PT[btp],), (KDECt,), out=KDEC, in_=self.PS[btp][:, 0:NCK * P], func=AF.Copy)
                self.A((self.PT[btp],), (VBt,), out=VB, in_=self.PS[btp][:, NCK * P:2 * NCK * P], func=AF.Copy)
                FP.put(KDTb, VBTb)
                yield
                ATb = BP.get()
                AT, ATt = v2(ATb[0]), ATb[1]
                N0b = FP.get()
                Qb = FP.get()
                M0b = FP.get()
                Nb = [v2(N0b[0][:, 0:TT]), None]
                Nbt = [N0b[1], None]
                Mb = [v2(M0b[0][:, 0:TT]), None]
                Mbt = [M0b[1], None]
                Q, Qt = v2(Qb[0][:, 0:TT]), Qb[1]
                for ck in range(NCK):
                    cs = slice(ck * P, (ck + 1) * P)
                    E1b = HP.get()
                    GTBb = HP.get()
                    E1, E1t = E1b
                    GTB, GTBt = GTBb
                    self.V('scalar_tensor_tensor', (GCBt, self.const_tok), (junkt[hd], gcolt[hd]), out=junk[hd],
                           in0=GCB[:, cs], scalar=1.0, in1=ident, op0=ALU.mult, op1=ALU.mult,
                           accum_out=gcol[:, hd, ck:ck + 1])
                    self.V('scalar_tensor_tensor', (GCBt, gcolt[hd], self.const_tok), (E1t,), out=E1,
                           in0=GCB[:, cs], scalar=gcol[:, hd, ck:ck + 1], in1=self.sm[:, O_MASKA:O_MASKA + P],
                           op0=ALU.subtract, op1=ALU.add)
                    self.A((E1t,), (E1t,), out=E1, in_=E1, func=AF.Exp)
                    b2 = self.nb()
                    sc.op('pe', [('matmul', dict(out=self.PS[b2][:, 0:P], lhsT=KN[:, cs], rhs=KN[:, cs],
                                                 start=True, stop=True)),
                                 ('matmul', dict(out=self.PS[b2][:, P:2 * P], lhsT=KN[:, cs], rhs=QN[:, cs],
                                                 start=True, stop=True))],
                          (KNt, QNt), (self.PT[b2],))
                    self.V('tensor_tensor', (self.PT[b2], E1t), (ATt,), out=AT[:, ck, :],
                           in0=self.PS[b2][:, P:2 * P], in1=E1, op=ALU.mult)
                    self.V('tensor_tensor', (E1t, BSTt), (GTBt,), out=GTB, in0=E1, in1=BST[:, cs], op=ALU.mult)
                    self.V('tensor_tensor', (self.PT[b2], GTBt), (Nbt[0],), out=Nb[0][:, ck, :],
                           in0=self.PS[b2][:, 0:P], in1=GTB, op=ALU.mult)
                    self.V('scalar_tensor_tensor', (Nbt[0], self.const_tok), (Qt,), out=Q[:, ck, :],
                           in0=Nb[0][:, ck, :], scalar=-1.0, in1=ident, op0=ALU.mult, op1=ALU.add)
                    HP.put(E1b, GTBb)
                    yield
                bl = self.nb()
                sc.op('pe', [('transpose', dict(out=self.PS[bl][:, ck * P:(ck + 1) * P], in_=Nb[0][:, ck, :],
                                                identity=ident)) for ck in range(NCK)],
                      (Nbt[0], self.const_tok), (self.PT[bl],))
                self.A((self.PT[bl],), (Mbt[0],), out=Mb[0].rearrange("p a b -> p (a b)"),
                       in_=self.PS[bl][:, 0:NCK * P], func=AF.Copy)
                FP.put(GCBb, BSTb)
                BP.put((KN, KNt), (QN, QNt))
                M1b = FP.get()
                N1b = FP.get()
                Mb[1], Mbt[1] = v2(M1b[0][:, 0:TT]), M1b[1]
                Nb[1], Nbt[1] = v2(N1b[0][:, 0:TT]), N1b[1]
                TTb = BP.get()
                TTm, TTt = v2(TTb[0]), TTb[1]
                yield
                fl = lambda a: a.rearrange("p a b -> p (a b)")
                for k in range(6):
                    c0, c1 = k % 2, (k + 1) % 2
                    pm = self.nb()
                    sc.op('pe', [('matmul', dict(out=self.PS[pm][:, ck * P:(ck + 1) * P], lhsT=Nb[c0][:, ck, :],
                                                 rhs=Mb[c0][:, ck, :], start=True, stop=True))
                                 for ck in range(NCK)], (Nbt[c0], Mbt[c0]), (self.PT[pm],))
                    if k < 5:
                        pn = self.nb()
                        sc.op('pe', [('matmul', dict(out=self.PS[pn][:, ck * P:(ck + 1) * P], lhsT=Mb[c0][:, ck, :],
                                                     rhs=Nb[c0][:, ck, :], start=True, stop=True))
                                     for ck in range(NCK)], (Nbt[c0], Mbt[c0]), (self.PT[pn],))
                    self.A((self.PT[pm],), (Mbt[c1],), out=fl(Mb[c1]), in_=self.PS[pm][:, 0:NCK * P], func=AF.Copy)
                    if k < 5:
                        self.V('tensor_copy', (self.PT[pn],), (Nbt[c1],), out=fl(Nb[c1]),
                               in_=self.PS[pn][:, 0:NCK * P])
                    yield
                    pq = self.nb()
                    sc.op('pe', [('matmul', dict(out=self.PS[pq][:, ck * P:(ck + 1) * P], lhsT=Mb[c1][:, ck, :],
                                                 rhs=Q[:, ck, :], start=True, stop=True))
                                 for ck in range(NCK)], (Mbt[c1], Qt), (self.PT[pq],))
                    if k < 5:
                        self.V('tensor_tensor', (self.PT[pq], Qt), (Qt,), out=fl(Q), in0=fl(Q),
                               in1=self.PS[pq][:, 0:NCK * P], op=ALU.add)
                    else:
                        self.V('tensor_tensor', (self.PT[pq], Qt), (TTt,), out=fl(TTm), in0=fl(Q),
                               in1=self.PS[pq][:, 0:NCK * P], op=ALU.add)
                    yield
                FP.put(N0b, M0b, M1b, N1b, Qb)
                OSb = FP.get()
                OS, OSt = OSb[0][:, 0:TT], OSb[1]
                for ck in range(NCK):
                    cs = slice(ck * P, (ck + 1) * P)
                    R, Rt = QP.get()
                    VN, VNt = QP.get()
                    sc.op('pe', ('matmul', dict(out=self.PS[CH][:, 0:P], lhsT=KBG[:, cs], rhs=S16[:, hd, :],
                                                start=True, stop=True)), (KBGt, S16t[hd]), (self.PT[CH],))
                    self.V('tensor_tensor', (VBt, self.PT[CH]), (Rt,), out=R, in0=v2(VB)[:, ck, :],
                           in1=self.PS[CH][:, 0:P], op=ALU.subtract)
                    yield
                    sc.op('pe', ('matmul', dict(out=self.PS[CH][:, P:2 * P], lhsT=TTm[:, ck, :], rhs=R,
                                                start=True, stop=True)), (TTt, Rt), (self.PT[CH],))
                    self.A((self.PT[CH],), (VNt,), out=VN, in_=self.PS[CH][:, P:2 * P], func=AF.Copy)
                    yield
                    sc.op('pe', [('matmul', dict(out=self.PS[CH][:, 2 * P:3 * P], lhsT=S16[:, hd, :],
                                                 rhs=QDEC[:, cs], start=True, stop=False)),
                                 ('matmul', dict(out=self.PS[CH][:, 2 * P:3 * P], lhsT=VN, rhs=AT[:, ck, :],
                                                 start=False, stop=True)),
                                 ('matmul', dict(out=self.PS[CH][:, 3 * P:4 * P], lhsT=v2(KDEC)[:, ck, :], rhs=VN,
                                                 start=True, stop=True))],
                          (S16t[hd], QDECt, VNt, ATt, KDECt), (self.PT[CH],))
                    self.V('scalar_tensor_tensor', (S32t[hd], LASTCt[hd], self.PT[CH]), (S32t[hd],),
                           out=S32[:, hd, :], in0=S32[:, hd, :], scalar=LASTC[:, hd, ck:ck + 1],
                           in1=self.PS[CH][:, 3 * P:4 * P], op0=ALU.mult, op1=ALU.add)
                    self.V('tensor_copy', (self.PT[CH],), (OSt,), out=OS[:, cs], in_=self.PS[CH][:, 2 * P:3 * P])
                    self.A((S32t[hd],), (S16t[hd],), out=S16[:, hd, :], in_=S32[:, hd, :], func=AF.Copy)
                    QP.put((R, Rt), (VN, VNt))
                    yield
                BP.put((KBG, KBGt), (QDEC, QDECt), KDECb, ATb, TTb)
                FP.put(VBb)
                ROb = FP.get()
                RO, ROt = ROb[0][:, 0:TT], ROb[1]
                rstd(OS, OSt, TT, 1.0 / P, RO, ROt)
                ZSb = FP.get()
                ZS, ZSt = ZSb[0][:, 0:TT], ZSb[1]
                bz = self.proj(w_z, hd * P, P, hn_t, hn_tok, TT)
                self.A((self.PT[bz],), (ZSt,), out=ZS, in_=self.PS[bz][:, 0:TT], func=AF.Silu)
                yield
                Tmb = FP.get()
                Tm, Tmt = Tmb[0][:, 0:TT], Tmb[1]
                self.V('scalar_tensor_tensor', (OSt, ROt, self.const_tok), (Tmt,), out=Tm, in0=OS,
                       scalar=self.sm_col(O_ONORM + e), in1=RO, op0=ALU.mult, op1=ALU.mult)
                self.V('tensor_tensor', (Tmt, ZSt), (yt[4 + hd],), out=y[:, 4 + hd, :], in0=Tm, in1=ZS, op=ALU.mult)
                FP.put(OSb, ROb, ZSb, Tmb)

            for h0 in range(0, 4, IL):
                gens = [head(h0 + i, NGB + i) for i in range(IL)]
                if h0 == 0:
                    gens.append(pool_gen())
                while gens:
                    for g_ in list(gens):
                        try:
                            next(g_)
                        except StopIteration:
                            gens.remove(g_)
            wo = [self.wload(wout[:, :, g * 512:(g + 1) * 512], self.v_k512(512)) for g in range(2)]
            self.out_proj(wo, e, T0, TT, y, yt)
        self.pool_low = (FP.low, BP.low, HP.low, QP.low)
        self.nb_range = (0, 7)


def make_smalls(inp):
    sm = np.zeros((P, NSM), np.float32)
    norms = np.concatenate([np.stack([inp['ffn1_norm'][l], inp['mix_norm'][l], inp['ffn2_norm'][l]]) for l in range(4)]
                           + [inp['final_norm'][None]], 0)
    sm[:, O_NORM:O_NORM + 104] = norms.reshape(13, 8, P).transpose(2, 0, 1).reshape(P, 104)
    sm[:, O_PSC:O_PSC + 8] = inp['pool_scale'].reshape(2, 4, P).transpose(2, 0, 1).reshape(P, 8)
    sm[:, O_DNW:O_DNW + 96] = inp['dn_conv_w'].reshape(2, 4, 12, P).transpose(3, 0, 2, 1).reshape(P, 96)
    sm[:, O_ONORM:O_ONORM + 2] = inp['dn_out_norm'].T
    sm[:, O_SGG:O_SGG + 8] = inp['sgu_norm_g'].reshape(2, 4, P).transpose(2, 0, 1).reshape(P, 8)
    sm[:, O_SGB:O_SGB + 8] = inp['sgu_norm_b'].reshape(2, 4, P).transpose(2, 0, 1).reshape(P, 8)
    sm[:, O_SCW:O_SCW + 24] = inp['sc_conv_w'].reshape(2, 3, 4, P).transpose(3, 0, 2, 1).reshape(P, 24)
    sm[4:8, O_ALOG:O_ALOG + 2] = inp['dn_a_log'].T
    sm[4:8, O_DTB:O_DTB + 2] = inp['dn_dt_bias'].T
    pp = np.arange(P)[:, None]
    ff = np.arange(P)[None, :]
    sm[:, O_IDENT:O_IDENT + P] = (pp == ff)
    sm[:, O_MASKA:O_MASKA + P] = np.where(ff >= pp, 0.0, -30000.0)
    sm[:, O_STRICT:O_STRICT + P] = (ff > pp)
    sm[:, O_TRIL:O_TRIL + P] = (ff <= pp)
    sm[:, O_RESET:O_RESET + 256] = (np.arange(256) % P != 0)[None, :]
    for g, w in enumerate(POOL_WINDOWS):
        sm[:, O_INVC + g * 16:O_INVC + (g + 1) * 16] = (1.0 / np.minimum(np.arange(16) + 1, w))[None, :]
    return sm


def make_in_maps(inp, n_cores, NB):
    inp = {k: np.asarray(v) for k, v in inp.items()}
    x = inp['x']
    shared = {}
    for k in ("ffn1_w_gate", "ffn1_w_up", "ffn1_w_down", "ffn2_w_gate", "ffn2_w_up", "ffn2_w_down",
              "ab_w_in", "ab_w_out", "cd_w_in", "cd_w_out"):
        shared[k] = np.ascontiguousarray(inp[k], dtype=np.float32)
    shared["smalls"] = make_smalls(inp)
    shared["poolw"] = np.ascontiguousarray(inp['pool_w'].transpose(2, 0, 1, 3).reshape(P, 2 * 4 * P), dtype=np.float32)
    shared["sguw"] = np.ascontiguousarray(inp['sgu_w'].transpose(0, 2, 1, 3).reshape(2, P, 4 * P), dtype=np.float32)
    shared["sgub"] = np.ascontiguousarray(
        np.broadcast_to(inp['sgu_bias'][:, None, :, :], (2, P, 4, P)).reshape(2, P, 4 * P), dtype=np.float32)
    maps = []
    for c in range(n_cores):
        m = dict(shared)
        m["xT"] = np.ascontiguousarray(x[c * NB:(c + 1) * NB].transpose(0, 2, 1))
        maps.append(m)
    return maps


def kernel(**inputs):
    n_cores = 8
    x = np.asarray(inputs['x'])
    B, S, _ = x.shape
    NB = B // n_cores
    bld = Builder(NB, S, range(4))
    nc = bld.build()
    maps = make_in_maps(inputs, n_cores, NB)
    res = run_bass_kernel_spmd(nc, maps, core_ids=list(range(n_cores)))
    out = np.empty((B, S, D), np.float32)
    for c in range(n_cores):
        out[c * NB:(c + 1) * NB] = res.results[c]["yT"].transpose(0, 2, 1)
    return out
```
